# Optimizing a Trainium2 kernel written in Bass

```python
import jax
import jax.numpy as jnp
from jax import lax
import numpy as np


D_MODEL = 1024
BATCH = 2
SEQ = 16384
DEPTH = 2
DEC_BATCH = 4
DEC_SEQ = 4096
PAST_LEN = 128

N_META = 16
LRU_WIDTH = 512
LRU_BLOCKS = 8
LRU_BLOCK = LRU_WIDTH // LRU_BLOCKS
CONV_WIDTH = 4
CONV_LEFT = 2
CONV_RIGHT = CONV_WIDTH - 1 - CONV_LEFT
LRU_C = 8.0
MLA_HEADS = 8
Q_LORA = 256
KV_LORA = 128
QK_NOPE = 64
QK_ROPE = 32
QK_HEAD = QK_NOPE + QK_ROPE
V_HEAD = 64
ATTN_WIDTH = MLA_HEADS * V_HEAD
ROPE_THETA = 10000.0
Q_BLOCK = 128
MIX_WIDTH = LRU_WIDTH + ATTN_WIDTH
OFF_U = 0
OFF_GATE = OFF_U + LRU_WIDTH
OFF_CQ = OFF_GATE + LRU_WIDTH
OFF_CKV = OFF_CQ + Q_LORA
OFF_KR = OFF_CKV + KV_LORA
IN_WIDTH = OFF_KR + QK_ROPE
D_FF = 4 * D_MODEL
EPS = 1e-6

kernel_name = 'hymba_rglru_mla_bidir_encoder'


def rmsnorm(x, g):
    xf = x.astype(jnp.float32)
    y = xf * lax.rsqrt(jnp.mean(xf * xf, axis=-1, keepdims=True) + EPS)
    return (y * g.astype(jnp.float32)).astype(x.dtype)


def rope_tables(T):
    inv_freq = 1.0 / (ROPE_THETA ** (jnp.arange(0, QK_ROPE, 2, dtype=jnp.float32) / QK_ROPE))
    ang = jnp.arange(T, dtype=jnp.float32)[:, None] * inv_freq[None, :]
    return jnp.cos(ang)[:, None, :], jnp.sin(ang)[:, None, :]


def apply_rope(x, cos, sin):
    xf = x.astype(jnp.float32)
    x_nope = xf[..., :QK_NOPE]
    x1 = xf[..., QK_NOPE:QK_NOPE + QK_ROPE // 2]
    x2 = xf[..., QK_NOPE + QK_ROPE // 2:]
    out = jnp.concatenate([x_nope, x1 * cos - x2 * sin, x2 * cos + x1 * sin], axis=-1)
    return out.astype(x.dtype)


def linear_scan_combine(left, right):
    a_l, b_l = left
    a_r, b_r = right
    return a_l * a_r, a_r * b_l + b_r


def rglru_group(u, gate, conv_w, conv_b, wa, ba, wx, bx, lam):
    B, T, _ = u.shape
    up = jnp.pad(u, ((0, 0), (CONV_LEFT, CONV_RIGHT), (0, 0)))
    xc = conv_b.astype(u.dtype)
    for tap in range(CONV_WIDTH):
        xc = xc + up[:, tap:tap + T] * conv_w[tap]
    xf = xc.astype(jnp.float32)
    xb = xf.reshape(B, T, LRU_BLOCKS, LRU_BLOCK)
    r = jax.nn.sigmoid(jnp.einsum('btgi,zgij->zbtgj', xb, wa.astype(jnp.float32)).reshape(2, B, T, LRU_WIDTH)
                       + ba.astype(jnp.float32)[:, None, None, :])
    ig = jax.nn.sigmoid(jnp.einsum('btgi,zgij->zbtgj', xb, wx.astype(jnp.float32)).reshape(2, B, T, LRU_WIDTH)
                        + bx.astype(jnp.float32)[:, None, None, :])
    log_a = -LRU_C * r * jax.nn.softplus(-lam.astype(jnp.float32))[:, None, None, :]
    a = jnp.exp(log_a)
    b = jnp.sqrt(-jnp.expm1(2.0 * log_a)) * (ig * xf[None])
    _, h_fwd = lax.associative_scan(linear_scan_combine, (a[0], b[0]), axis=1)
    _, h_bwd = lax.associative_scan(linear_scan_combine, (a[1], b[1]), axis=1, reverse=True)
    y = (h_fwd + h_bwd) * jax.nn.gelu(gate.astype(jnp.float32))
    return y.astype(u.dtype)


def block_attention(q, k, v):
    B, T, H, _ = q.shape
    nblk = -(-T // Q_BLOCK)
    pad = nblk * Q_BLOCK - T
    qp = jnp.pad(q, ((0, 0), (0, pad), (0, 0), (0, 0)))
    qb = qp.reshape(B, nblk, Q_BLOCK, H, QK_HEAD).transpose(1, 0, 2, 3, 4)
    scale = QK_HEAD ** -0.5

    def one_block(q_blk):
        s = jnp.einsum('bqhd,bkhd->bhqk', q_blk, k, preferred_element_type=jnp.float32) * scale
        p = jax.nn.softmax(s, axis=-1)
        return jnp.einsum('bhqk,bkhd->bqhd', p.astype(v.dtype), v)

    o = lax.map(one_block, qb)
    return o.transpose(1, 0, 2, 3, 4).reshape(B, nblk * Q_BLOCK, H, V_HEAD)[:, :T]


def mla_group(c_q, c_kv, k_rope, cos, sin, q_norm_g, w_uq, kv_norm_g, w_ukv, qk_q_g, qk_k_g):
    B, T, _ = c_q.shape
    q = (rmsnorm(c_q, q_norm_g) @ w_uq).reshape(B, T, MLA_HEADS, QK_HEAD)
    kv = (rmsnorm(c_kv, kv_norm_g) @ w_ukv).reshape(B, T, MLA_HEADS, QK_NOPE + V_HEAD)
    k_nope, v = kv[..., :QK_NOPE], kv[..., QK_NOPE:]
    k = jnp.concatenate([k_nope, jnp.broadcast_to(k_rope[:, :, None, :], (B, T, MLA_HEADS, QK_ROPE))], axis=-1)
    q = apply_rope(rmsnorm(q, qk_q_g), cos, sin)
    k = apply_rope(rmsnorm(k, qk_k_g), cos, sin)
    o = block_attention(q, k, v)
    return o.reshape(B, T, ATTN_WIDTH)


def run_trunk(x, meta_tokens, norm_mix_g, w_in, conv_w, conv_b, lru_wa, lru_ba, lru_wx, lru_bx, lru_lambda,
              q_norm_g, w_uq, kv_norm_g, w_ukv, qk_q_g, qk_k_g, out_norm_lru_g, out_norm_attn_g, w_out,
              norm_ff_g, w_up, w_down):
    B, S, _ = x.shape
    T = S + N_META
    meta = jnp.broadcast_to(meta_tokens.astype(x.dtype)[None], (B, N_META, D_MODEL))
    h = jnp.concatenate([meta, x], axis=1)
    cos, sin = rope_tables(T)
    for l in range(DEPTH):
        hn = rmsnorm(h, norm_mix_g[l])
        proj = hn @ w_in[l]
        u = proj[..., OFF_U:OFF_U + LRU_WIDTH]
        gate = proj[..., OFF_GATE:OFF_GATE + LRU_WIDTH]
        c_q = proj[..., OFF_CQ:OFF_CQ + Q_LORA]
        c_kv = proj[..., OFF_CKV:OFF_CKV + KV_LORA]
        k_rope = proj[..., OFF_KR:OFF_KR + QK_ROPE]
        y_lru = rglru_group(u, gate, conv_w[l], conv_b[l], lru_wa[l], lru_ba[l], lru_wx[l], lru_bx[l], lru_lambda[l])
        y_att = mla_group(c_q, c_kv, k_rope, cos, sin, q_norm_g[l], w_uq[l], kv_norm_g[l], w_ukv[l],
                          qk_q_g[l], qk_k_g[l])
        mixed = jnp.concatenate([rmsnorm(y_lru, out_norm_lru_g[l]), rmsnorm(y_att, out_norm_attn_g[l])], axis=-1)
        h = h + mixed @ w_out[l]
        hf = rmsnorm(h, norm_ff_g[l])
        h = h + jnp.square(jax.nn.relu(hf @ w_up[l])) @ w_down[l]
    return h[:, N_META:]


def setup_inputs(seed: int = 0) -> dict:
    key = jax.random.key(seed)
    ks = jax.random.split(key, 24)

    def nrm(k, shape, scale):
        return jax.random.normal(k, shape, jnp.float32) * scale

    x_prompt = nrm(ks[0], (BATCH, SEQ, D_MODEL), 1.0)
    x_sample = nrm(ks[1], (DEC_BATCH, DEC_SEQ, D_MODEL), 1.0)
    meta_tokens = nrm(ks[2], (N_META, D_MODEL), 1.0)
    norm_mix_g = 1.0 + nrm(ks[3], (DEPTH, D_MODEL), 0.02)
    w_in = nrm(ks[4], (DEPTH, D_MODEL, IN_WIDTH), D_MODEL ** -0.5)
    conv_w = nrm(ks[5], (DEPTH, CONV_WIDTH, LRU_WIDTH), CONV_WIDTH ** -0.5)
    conv_b = nrm(ks[6], (DEPTH, LRU_WIDTH), 0.02)
    lru_wa = nrm(ks[7], (DEPTH, 2, LRU_BLOCKS, LRU_BLOCK, LRU_BLOCK), LRU_BLOCK ** -0.5)
    lru_ba = nrm(ks[8], (DEPTH, 2, LRU_WIDTH), 0.1)
    lru_wx = nrm(ks[9], (DEPTH, 2, LRU_BLOCKS, LRU_BLOCK, LRU_BLOCK), LRU_BLOCK ** -0.5)
    lru_bx = nrm(ks[10], (DEPTH, 2, LRU_WIDTH), 0.1)
    a_c = jax.random.uniform(ks[11], (DEPTH, 2, LRU_WIDTH), jnp.float32, 0.9, 0.999)
    a0 = a_c ** (1.0 / LRU_C)
    lru_lambda = jnp.log(a0) - jnp.log1p(-a0)
    q_norm_g = 1.0 + nrm(ks[12], (DEPTH, Q_LORA), 0.02)
    w_uq = nrm(ks[13], (DEPTH, Q_LORA, MLA_HEADS * QK_HEAD), Q_LORA ** -0.5)
    kv_norm_g = 1.0 + nrm(ks[14], (DEPTH, KV_LORA), 0.02)
    w_ukv = nrm(ks[15], (DEPTH, KV_LORA, MLA_HEADS * (QK_NOPE + V_HEAD)), KV_LORA ** -0.5)
    qk_q_g = 1.0 + nrm(ks[16], (DEPTH, QK_HEAD), 0.02)
    qk_k_g = 1.0 + nrm(ks[17], (DEPTH, QK_HEAD), 0.02)
    out_norm_lru_g = 1.0 + nrm(ks[18], (DEPTH, LRU_WIDTH), 0.02)
    out_norm_attn_g = 1.0 + nrm(ks[19], (DEPTH, ATTN_WIDTH), 0.02)
    w_out = nrm(ks[20], (DEPTH, MIX_WIDTH, D_MODEL), MIX_WIDTH ** -0.5)
    norm_ff_g = 1.0 + nrm(ks[21], (DEPTH, D_MODEL), 0.02)
    w_up = nrm(ks[22], (DEPTH, D_MODEL, D_FF), D_MODEL ** -0.5)
    w_down = nrm(ks[23], (DEPTH, D_FF, D_MODEL), D_FF ** -0.5)
    return {'x_prompt': x_prompt, 'x_sample': x_sample, 'meta_tokens': meta_tokens,
            'norm_mix_g': norm_mix_g, 'w_in': w_in, 'conv_w': conv_w, 'conv_b': conv_b,
            'lru_wa': lru_wa, 'lru_ba': lru_ba, 'lru_wx': lru_wx, 'lru_bx': lru_bx, 'lru_lambda': lru_lambda,
            'q_norm_g': q_norm_g, 'w_uq': w_uq, 'kv_norm_g': kv_norm_g, 'w_ukv': w_ukv,
            'qk_q_g': qk_q_g, 'qk_k_g': qk_k_g, 'out_norm_lru_g': out_norm_lru_g,
            'out_norm_attn_g': out_norm_attn_g, 'w_out': w_out, 'norm_ff_g': norm_ff_g,
            'w_up': w_up, 'w_down': w_down}


def reference(x_prompt, x_sample, meta_tokens, norm_mix_g, w_in, conv_w, conv_b, lru_wa, lru_ba, lru_wx, lru_bx,
              lru_lambda, q_norm_g, w_uq, kv_norm_g, w_ukv, qk_q_g, qk_k_g, out_norm_lru_g, out_norm_attn_g,
              w_out, norm_ff_g, w_up, w_down):
    y_prompt = run_trunk(x_prompt, meta_tokens, norm_mix_g, w_in, conv_w, conv_b, lru_wa, lru_ba, lru_wx, lru_bx,
                         lru_lambda, q_norm_g, w_uq, kv_norm_g, w_ukv, qk_q_g, qk_k_g, out_norm_lru_g,
                         out_norm_attn_g, w_out, norm_ff_g, w_up, w_down)
    y_sample = run_trunk(x_sample, meta_tokens, norm_mix_g, w_in, conv_w, conv_b, lru_wa, lru_ba, lru_wx, lru_bx,
                         lru_lambda, q_norm_g, w_uq, kv_norm_g, w_ukv, qk_q_g, qk_k_g, out_norm_lru_g,
                         out_norm_attn_g, w_out, norm_ff_g, w_up, w_down)
    return (y_prompt, y_sample)
```

```python
import numpy as np
from contextlib import ExitStack
import concourse.bass as bass
import concourse.mybir as mybir
from concourse.bass_utils import run_bass_kernel_spmd

F32 = mybir.dt.float32
BF16 = mybir.dt.bfloat16
AF = mybir.ActivationFunctionType
ALU = mybir.AluOpType

D = 1024
NCORE = 8
VR = 8
NV = 75
EPS = 1e-6
IN_W = 1440
SCALE = 96 ** -0.5
PHASES = "AGCBDE"
NLAYER = 2
DBG = 99
VAR = 0


class Buf:
    __slots__ = ("name", "w", "r", "dsem")

    def __init__(self, name):
        self.name = name
        self.w = None
        self.r = []
        self.dsem = None


class Eng:
    def __init__(self, name, sem):
        self.name = name
        self.sem = sem
        self.cnt = 0
        self.known = {}
        self.insts = []


class Sched:
    def __init__(self, nc):
        self.nc = nc
        self.sems = {}
        self.E = {}
        for name in ("pe", "act", "dve", "pool", "sp"):
            self.sems[name] = nc.alloc_semaphore(name="sem_" + name)
            self.E[name] = Eng(name, name)
        self.dval = {}

    def _waits(self, eng, reads, writes):
        need = {}
        for b in reads:
            if b.w is not None:
                k, v = b.w
                if need.get(k, 0) < v:
                    need[k] = v
        for b in writes:
            if b.w is not None:
                k, v = b.w
                if need.get(k, 0) < v:
                    need[k] = v
            for (k, v) in b.r:
                if need.get(k, 0) < v:
                    need[k] = v
        out = []
        for k, v in need.items():
            if eng.name == "pe" and k == "pe":
                continue
            if eng.known.get(k, 0) >= v:
                continue
            eng.known[k] = v
            out.append((k, v))
        return out

    def _mark(self, tag, reads, writes):
        for b in reads:
            if len(b.r) > 64:
                m = {}
                for (k, v) in b.r:
                    if m.get(k, 0) < v:
                        m[k] = v
                b.r = list(m.items())
            b.r.append(tag)
        for b in writes:
            b.w = tag
            b.r = []

    def op(self, en, fn, reads=(), writes=()):
        eng = self.E[en]
        waits = self._waits(eng, reads, writes)
        eng.cnt += 1
        tag = (eng.sem, eng.cnt)
        eng.insts.append((waits, fn, (eng.sem, 1)))
        self._mark(tag, reads, writes)
        return tag

    def _own(self, owner):
        if owner.dsem is None:
            key = "d_" + owner.name
            if key not in self.sems:
                self.sems[key] = self.nc.alloc_semaphore(name=key)
                self.dval[key] = 0
            owner.dsem = key

    def dma(self, en, out, in_, reads=(), writes=(), owner=None, slow=False):
        eng = self.E[en]
        waits = self._waits(eng, reads, writes)
        if owner is None:
            owner = writes[0] if writes else reads[0]
        self._own(owner)
        self.dval[owner.dsem] += 16
        tag = (owner.dsem, self.dval[owner.dsem])
        kw = {"allow_slow_non_contiguous": True} if slow else {}
        def _f(e, o=out, i=in_):
            try:
                return e.dma_start(out=o, in_=i, **kw)
            except Exception:
                print("DMA FAIL", en, o, i)
                raise
        eng.insts.append((waits, _f, (owner.dsem, 16)))
        self._mark(tag, reads, writes)
        return tag

    def coll(self, ins, outs, reads, writes, owner):
        eng = self.E["pool"]
        waits = self._waits(eng, reads, writes)
        self._own(owner)
        self.dval[owner.dsem] += 1
        tag = (owner.dsem, self.dval[owner.dsem])
        rg = [list(range(NCORE))]
        eng.insts.append((waits, (lambda e: e.collective_compute("AllGather", ALU.bypass, replica_groups=rg,
                                                                 ins=ins, outs=outs)), (owner.dsem, None)))
        self._mark(tag, reads, writes)
        return tag

    def flush(self):
        eng = self.E["sp"]
        dr = []
        for key, val in self.dval.items():
            if val > 0 and eng.known.get(key, 0) < val:
                eng.known[key] = val
                dr.append((key, val))
        if dr:
            eng.insts.append((dr, None, None))
        sems = self.sems
        with self.nc.Block() as block:
            for name, attr in (("pe", "tensor"), ("act", "scalar"), ("dve", "vector"),
                               ("pool", "gpsimd"), ("sp", "sync")):
                e = self.E[name]
                if not e.insts:
                    continue

                def body(be, insts=e.insts):
                    for waits, fn, inc in insts:
                        for (k, v) in waits:
                            be.wait_ge(sems[k], v)
                        if fn is None:
                            continue
                        ins = fn(be)
                        if inc is not None:
                            if inc[1] is None:
                                ins.then_inc(sems[inc[0]])
                            else:
                                ins.then_inc(sems[inc[0]], inc[1])
                getattr(block, attr)(body)
                e.insts = []


_UID = [0]


def _uname(name):
    _UID[0] += 1
    return f"{name}_u{_UID[0]}"


class Ring:
    def __init__(self, nc, es, name, n, shape, dtype):
        self.items = []
        for i in range(n):
            t = es.enter_context(nc.sbuf_tensor(_uname(f"{name}{i}"), shape, dtype))
            self.items.append((t, Buf(f"{name}{i}")))
        self.i = 0

    def next(self):
        it = self.items[self.i % len(self.items)]
        self.i += 1
        return it


def build(SEQ, DSEQ):
    TP, TS = SEQ + 16, DSEQ + 16
    LP, LS = TP // VR, TS // VR
    assert LP * VR == TP and LS * VR == TS
    seg_len = [LP, LS]
    seq_T = [TP, TS]
    seg_off = [0, LP]
    LT = LP + LS
    NTOK = VR * LT

    nc = bass.Bass("TRN2", target_bir_lowering=False)

    def din(name, shape, dtype=F32):
        return nc.dram_tensor(name, shape, dtype, kind="ExternalInput").ap()

    xT = din("xT", [D, NTOK])
    cs = din("cs", [2, 32, NTOK])
    vecs = din("vecs", [2, 128, NV])
    lruw = din("lruw", [2, 4, 4, 128, 128])
    w_in = din("w_in", [2, D, IN_W])
    w_uq = din("w_uq", [2, 256, 768])
    w_ukv = din("w_ukv", [2, 128, 1024])
    w_out = din("w_out", [2, D, D])
    w_up = din("w_up", [2, D, 4 * D])
    w_down = din("w_down", [2, 4 * D, D])
    yT = nc.dram_tensor("yT", [D, NTOK], F32, kind="ExternalOutput").ap()

    def dscr(name, shape, dtype=F32):
        return nc.dram_tensor(name, shape, dtype, kind="Internal")

    hT_scr = dscr("hT_scr", [D, NTOK]).ap()
    ug_gath = dscr("ug_gath", [VR * D, LT]).ap()
    k_gath = dscr("k_gath", [VR * 768, LT], BF16).ap()
    v_gath = dscr("v_gath", [VR * LT, 512], BF16).ap()
    q_scr = dscr("q_scr", [768, NTOK], BF16).ap()
    hf_scr = dscr("hf_scr", [128, TP]).ap()
    y_gath = dscr("y_gath", [512, VR * LT]).ap()
    ya_scr = dscr("ya_scr", [512, NTOK]).ap()
    Bug, Bkg, Bvg, Byg = Buf("ug_gath"), Buf("k_gath"), Buf("v_gath"), Buf("y_gath")

    S = Sched(nc)
    G = ExitStack()

    def tile(es, name, shape, dtype):
        t = es.enter_context(nc.sbuf_tensor(_uname(name), shape, dtype))
        return t, Buf(name)

    PS = []
    for i in range(8):
        t = G.enter_context(nc.psum_tensor(f"ps{i}", [128, 512], F32))
        PS.append((t, Buf(f"ps{i}")))
    pbi = [0]

    def pb():
        it = PS[pbi[0] % 8]
        pbi[0] += 1
        return it

    ones_b, Bones = tile(G, "ones_b", [128, 128], BF16)
    onesR_b, BonesR = tile(G, "onesR_b", [96, 96], BF16)
    ones_f, Bonesf = tile(G, "ones_f", [128, 128], F32)
    S.op("pool", lambda e: e.memset(ones_b[:, :], 1.0), writes=[Bones])
    S.op("pool", lambda e: e.memset(onesR_b[:, :], 0.0), writes=[BonesR])
    S.op("pool", lambda e: e.memset(onesR_b[64:96, :], 1.0), writes=[BonesR])
    S.op("pool", lambda e: e.memset(ones_f[:, :], 0.0), writes=[Bonesf])
    S.op("pool", lambda e: e.memset(ones_f[0:1, :], 1.0), writes=[Bonesf])

    def mm(out, lhsT, rhs, start, stop, reads, writes):
        S.op("pe", lambda e: e.matmul(out, lhsT=lhsT, rhs=rhs, start=start, stop=stop), reads, writes)

    def act(out, in_, func, reads, writes, bias=None, scale=None):
        kw = {}
        if bias is not None:
            kw["bias"] = bias
        if scale is not None:
            kw["scale"] = scale
        S.op("act", lambda e: e.activation(out=out, in_=in_, func=func, **kw), reads, writes)

    def tt(en, out, in0, in1, op, reads, writes):
        S.op(en, lambda e: e.tensor_tensor(out=out, in0=in0, in1=in1, op=op), reads, writes)

    def ts(en, out, in0, s1, s2, op0, op1, reads, writes):
        if s2 is None:
            S.op(en, lambda e: e.tensor_scalar(out=out, in0=in0, scalar1=s1, scalar2=None, op0=op0), reads, writes)
        else:
            S.op(en, lambda e: e.tensor_scalar(out=out, in0=in0, scalar1=s1, scalar2=s2, op0=op0, op1=op1),
                 reads, writes)

    def stt(en, out, in0, scalar, in1, op0, op1, reads, writes):
        S.op(en, lambda e: e.scalar_tensor_tensor(out=out, in0=in0, scalar=scalar, in1=in1, op0=op0, op1=op1),
             reads, writes)

    def cp(en, out, in_, reads, writes):
        S.op(en, lambda e: e.tensor_copy(out=out, in_=in_), reads, writes)

    def acp(out, in_, reads, writes):
        S.op("act", lambda e: e.copy(out=out, in_=in_), reads, writes)

    def recip(out, in_, reads, writes):
        S.op("dve", lambda e: e.reciprocal(out=out, in_=in_), reads, writes)

    def mset(en, ap, val, writes):
        S.op(en, lambda e: e.memset(ap, val), (), writes)

    def rstd_from(ps_ap, scale, bias_c, tmp_ap, out_ap, Bps, Btmp, Bout):
        act(tmp_ap, ps_ap, AF.Sqrt, [Bps], [Btmp], bias=bias_c, scale=scale)
        recip(out_ap, tmp_ap, [Btmp], [Bout])

    def phaseA(l, src):
        with ExitStack() as es:
            win_b, Bwin = tile(es, "win_b", [128, 8, 1632], BF16)
            wq_b, Bwq = tile(es, "wq_b", [128, 2, 768], BF16)
            wqr_b, Bwqr = tile(es, "wqr_b", [128, 2, 768], BF16)
            wkv_b, Bwkv = tile(es, "wkv_b", [128, 1024], BF16)
            wv_b, Bwv = tile(es, "wv_b", [128, 512], BF16)
            vec, Bvec = tile(es, "vecA", [128, NV], F32)
            stg = Ring(nc, es, "wstgA", 2, [128, IN_W], F32)
            S.dma("sp", vec[:, :], vecs[l, :, :], writes=[Bvec])
            mset("pool", win_b[:, :, 1440:1632], 0.0, [Bwin])
            mset("pool", wqr_b[:, :, :], 0.0, [Bwqr])
            for kc in range(8):
                st, Bst = stg.next()
                S.dma("sp", st[:, :], w_in[l, kc * 128:(kc + 1) * 128, :], writes=[Bst])
                g = vec[:, kc:kc + 1]
                ts("dve", win_b[:, kc, 0:1440], st[:, :], g, None, ALU.mult, None, [Bst, Bvec], [Bwin])
                ts("dve", win_b[:, kc, 1504:1536], st[:, 1408:1440], g, None, ALU.mult, None, [Bst, Bvec], [Bwin])
                ts("dve", win_b[:, kc, 1600:1616], st[:, 1424:1440], g, -1.0, ALU.mult, ALU.mult, [Bst, Bvec], [Bwin])
                ts("dve", win_b[:, kc, 1616:1632], st[:, 1408:1424], g, None, ALU.mult, None, [Bst, Bvec], [Bwin])
            for j in range(2):
                st, Bst = stg.next()
                S.dma("sp", st[:, 0:768], w_uq[l, j * 128:(j + 1) * 128, :], writes=[Bst])
                g = vec[:, 8 + j:9 + j]
                ts("dve", wq_b[:, j, :], st[:, 0:768], g, None, ALU.mult, None, [Bst, Bvec], [Bwq])
                sv = st[:, 0:768].rearrange("p (h c) -> p h c", c=96)
                rv = wqr_b[:, j, :].rearrange("p (h c) -> p h c", c=96)
                ts("dve", rv[:, :, 64:80], sv[:, :, 80:96], g, -1.0, ALU.mult, ALU.mult, [Bst, Bvec], [Bwqr])
                ts("dve", rv[:, :, 80:96], sv[:, :, 64:80], g, None, ALU.mult, None, [Bst, Bvec], [Bwqr])
            st, Bst = stg.next()
            S.dma("sp", st[:, 0:1024], w_ukv[l, :, :], writes=[Bst])
            ts("dve", wkv_b[:, :], st[:, 0:1024], vec[:, 10:11], None, ALU.mult, None, [Bst, Bvec], [Bwkv])
            cp("pool", wv_b[:, :].rearrange("p (h c) -> p h c", c=64),
               wkv_b[:, :].rearrange("p (h c) -> p h c", c=128)[:, :, 64:128], [Bwkv], [Bwv])

            tiles = [(r, t0, min(512, LT - t0)) for r in range(VR) for t0 in range(0, LT, 512)]
            hT_r = Ring(nc, es, "hTA", 2, [128, 8, 512], F32)
            cs_r = Ring(nc, es, "csA", 2, [96, 2, 512], F32)
            sq_r = Ring(nc, es, "sqA", 2, [128, 8, 512], BF16)
            hb_r = Ring(nc, es, "hbA", 2, [128, 8, 512], BF16)
            f32_r = Ring(nc, es, "f32A", 6, [128, 512], F32)
            stgo_r = Ring(nc, es, "stgoA", 3, [128, 512], F32)
            b16_r = Ring(nc, es, "b16A", 6, [128, 512], BF16)
            qo_r = Ring(nc, es, "qoA", 3, [96, 512], BF16)
            ko_r = Ring(nc, es, "koA", 3, [96, 512], BF16)
            vo_r = Ring(nc, es, "voA", 2, [128, 512], BF16)
            cqb_r = Ring(nc, es, "cqbA", 2, [128, 2, 512], BF16)
            sqq_r = Ring(nc, es, "sqqA", 2, [128, 2, 512], BF16)
            ckvb_r = Ring(nc, es, "ckvbA", 2, [128, 512], BF16)
            sqkv_r = Ring(nc, es, "sqkvA", 2, [128, 512], BF16)
            per_r = Ring(nc, es, "perA", 10, [96, 512], F32)
            sv_r = Ring(nc, es, "svA", 4, [128, 2], F32)
            qf_r = Ring(nc, es, "qfA", 3, [96, 512], F32)
            sx_r = Ring(nc, es, "sxA", 2, [128, 512], BF16)
            for (sxt, Bsxt) in sx_r.items:
                mset("pool", sxt[:, :], 0.0, [Bsxt])
            hsrc = src.rearrange("(kc p) t -> p kc t", p=128)
            csrc = cs.rearrange("two r t -> r two t")

            def load(i):
                r, t0, N = tiles[i]
                g0 = r * LT + t0
                h, Bh = hT_r.next()
                S.dma("sp", h[:, :, 0:N], hsrc[:, :, g0:g0 + N], writes=[Bh])
                c, Bc = cs_r.next()
                S.dma("sp", c[64:96, :, 0:N], csrc[:, :, g0:g0 + N], writes=[Bc])
                return h, Bh, c, Bc

            def compute(i, h, Bh, c, Bc):
                r, t0, N = tiles[i]
                g0 = r * LT + t0
                if DBG < 1:
                    return
                sq, Bsq = sq_r.next()
                hb, Bhb = hb_r.next()
                for kc in range(8):
                    act(sq[:, kc, 0:N], h[:, kc, 0:N], AF.Square, [Bh], [Bsq])
                ps, Bps = pb()
                for kc in range(8):
                    mm(ps[:, 0:N], ones_b[:, :], sq[:, kc, 0:N], kc == 0, kc == 7, [Bones, Bsq], [Bps])
                tmp, Btmp = f32_r.next()
                R, BR = f32_r.next()
                rstd_from(ps[:, 0:N], 1.0 / D, EPS, tmp[:, 0:N], R[:, 0:N], Bps, Btmp, BR)
                for kc in range(8):
                    tt("dve", hb[:, kc, 0:N], h[:, kc, 0:N], R[:, 0:N], ALU.mult, [Bh, BR], [Bhb])
                for oc in range(8):
                    ps, Bps = pb()
                    for kc in range(8):
                        mm(ps[:, 0:N], win_b[:, kc, oc * 128:(oc + 1) * 128], hb[:, kc, 0:N], kc == 0, kc == 7,
                           [Bwin, Bhb], [Bps])
                    so, Bso = stgo_r.next()
                    if oc % 2 == 0:
                        S.op("act", lambda e, o=so[:, 0:N], p=ps[:, 0:N]: e.copy(out=o, in_=p), [Bps], [Bso])
                    else:
                        cp("dve", so[:, 0:N], ps[:, 0:N], [Bps], [Bso])
                    S.dma("sp", ug_gath[r * D + oc * 128:r * D + (oc + 1) * 128, t0:t0 + N], so[:, 0:N], reads=[Bso])
                if DBG < 2:
                    return
                cqb, Bcqb = cqb_r.next()
                sqq, Bsqq = sqq_r.next()
                for j in range(2):
                    ps, Bps = pb()
                    for kc in range(8):
                        mm(ps[:, 0:N], win_b[:, kc, 1024 + j * 128:1152 + j * 128], hb[:, kc, 0:N], kc == 0, kc == 7,
                           [Bwin, Bhb], [Bps])
                    act(sqq[:, j, 0:N], ps[:, 0:N], AF.Square, [Bps], [Bsqq])
                    acp(cqb[:, j, 0:N], ps[:, 0:N], [Bps], [Bcqb])
                ckvb, Bckvb = ckvb_r.next()
                sqkv, Bsqkv = sqkv_r.next()
                ps, Bps = pb()
                for kc in range(8):
                    mm(ps[:, 0:N], win_b[:, kc, 1280:1408], hb[:, kc, 0:N], kc == 0, kc == 7, [Bwin, Bhb], [Bps])
                act(sqkv[:, 0:N], ps[:, 0:N], AF.Square, [Bps], [Bsqkv])
                acp(ckvb[:, 0:N], ps[:, 0:N], [Bps], [Bckvb])
                if DBG == 2:
                    return
                kr, Bkr = per_r.next()
                krot, Bkrot = per_r.next()
                sqr, Bsqr = b16_r.next()
                ps, Bps = pb()
                for kc in range(8):
                    mm(ps[0:96, 0:N], win_b[:, kc, 1440:1536], hb[:, kc, 0:N], kc == 0, kc == 7, [Bwin, Bhb], [Bps])
                act(sqr[0:96, 0:N], ps[0:96, 0:N], AF.Square, [Bps], [Bsqr])
                acp(kr[64:96, 0:N], ps[64:96, 0:N], [Bps], [Bkr])
                ps, Bps = pb()
                for kc in range(8):
                    mm(ps[0:96, 0:N], win_b[:, kc, 1536:1632], hb[:, kc, 0:N], kc == 0, kc == 7, [Bwin, Bhb], [Bps])
                cp("dve", krot[64:96, 0:N], ps[64:96, 0:N], [Bps], [Bkrot])
                if DBG < 3:
                    return
                Cq, BCq = per_r.next()
                ps, Bps = pb()
                for j in range(2):
                    mm(ps[0:96, 0:N], ones_b[:, 0:96], sqq[:, j, 0:N], j == 0, j == 1, [Bones, Bsqq], [Bps])
                ts("dve", Cq[:, 0:N], ps[0:96, 0:N], 96.0 * EPS / 256.0, 96.0 * EPS * EPS, ALU.mult, ALU.add,
                   [Bps], [BCq])
                S2, BS2 = per_r.next()
                skv, Bskv = per_r.next()
                MR, BMR = per_r.next()
                KRr, BKRr = per_r.next()
                tA, BtA = per_r.next()
                ps, Bps = pb()
                mm(ps[0:96, 0:N], ones_b[:, 0:96], sqkv[:, 0:N], True, True, [Bones, Bsqkv], [Bps])
                ts("dve", tA[:, 0:N], ps[0:96, 0:N], 1.0 / 128.0, EPS, ALU.mult, ALU.add, [Bps], [BtA])
                recip(S2[:, 0:N], tA[:, 0:N], [BtA], [BS2])
                act(skv[:, 0:N], S2[:, 0:N], AF.Sqrt, [BS2], [Bskv])
                ps, Bps = pb()
                mm(ps[0:96, 0:N], onesR_b[:, :], sqr[0:96, 0:N], True, True, [BonesR, Bsqr], [Bps])
                ts("dve", MR[:, 0:N], ps[0:96, 0:N], 96.0 * EPS, None, ALU.add, None, [Bps], [BMR])
                t1, Bt1 = per_r.next()
                t2, Bt2 = per_r.next()
                stt("dve", t1[64:96, 0:N], kr[64:96, 0:N], vec[64:96, 13:14], c[64:96, 0, 0:N], ALU.mult, ALU.mult,
                    [Bkr, Bvec, Bc], [Bt1])
                stt("dve", t2[64:96, 0:N], krot[64:96, 0:N], vec[64:96, 14:15], c[64:96, 1, 0:N], ALU.mult, ALU.mult,
                    [Bkrot, Bvec, Bc], [Bt2])
                tt("pool", KRr[64:96, 0:N], t1[64:96, 0:N], t2[64:96, 0:N], ALU.add, [Bt1, Bt2], [BKRr])
                if DBG < 4:
                    return
                for hh in range(8 if DBG != 5 else 0):
                    pq, Bpq = pb()
                    for j in range(2):
                        mm(pq[0:96, 0:N], wq_b[:, j, hh * 96:(hh + 1) * 96], cqb[:, j, 0:N], j == 0, j == 1,
                           [Bwq, Bcqb], [Bpq])
                    pr, Bpr = pb()
                    for j in range(2):
                        mm(pr[0:96, 0:N], wqr_b[:, j, hh * 96:(hh + 1) * 96], cqb[:, j, 0:N], j == 0, j == 1,
                           [Bwqr, Bcqb], [Bpr])
                    s2, Bs2 = b16_r.next()
                    act(s2[0:96, 0:N], pq[0:96, 0:N], AF.Square, [Bpq], [Bs2])
                    qf, Bqf = qf_r.next()
                    acp(qf[0:96, 0:N], pq[0:96, 0:N], [Bpq], [Bqf])
                    pm, Bpm = pb()
                    mm(pm[0:96, 0:N], ones_b[0:96, 0:96], s2[0:96, 0:N], True, True, [Bones, Bs2], [Bpm])
                    ta, Bta = f32_r.next()
                    tb_, Btb = f32_r.next()
                    rq, Brq = f32_r.next()
                    tt("dve", ta[0:96, 0:N], pm[0:96, 0:N], Cq[:, 0:N], ALU.add, [Bpm, BCq], [Bta])
                    act(tb_[0:96, 0:N], ta[0:96, 0:N], AF.Sqrt, [Bta], [Btb], scale=1.0 / 96.0)
                    recip(rq[0:96, 0:N], tb_[0:96, 0:N], [Btb], [Brq])
                    qo, Bqo = qo_r.next()
                    stt("dve", qo[0:64, 0:N], qf[0:64, 0:N], vec[0:64, 11:12], rq[0:64, 0:N], ALU.mult, ALU.mult,
                        [Bqf, Bvec, Brq], [Bqo])
                    u1, Bu1 = f32_r.next()
                    u2, Bu2 = f32_r.next()
                    stt("dve", u1[64:96, 0:N], qf[64:96, 0:N], vec[64:96, 11:12], c[64:96, 0, 0:N], ALU.mult, ALU.mult,
                        [Bqf, Bvec, Bc], [Bu1])
                    stt("dve", u2[64:96, 0:N], pr[64:96, 0:N], vec[64:96, 12:13], c[64:96, 1, 0:N], ALU.mult, ALU.mult,
                        [Bpr, Bvec, Bc], [Bu2])
                    tt("pool", u1[64:96, 0:N], u1[64:96, 0:N], u2[64:96, 0:N], ALU.add, [Bu2], [Bu1])
                    tt("pool", qo[64:96, 0:N], u1[64:96, 0:N], rq[64:96, 0:N], ALU.mult, [Bu1, Brq], [Bqo])
                    S.dma("sp", q_scr[hh * 96:(hh + 1) * 96, g0:g0 + N], qo[0:96, 0:N], reads=[Bqo])
                    pk, Bpk = pb()
                    mm(pk[:, 0:N], wkv_b[:, hh * 128:(hh + 1) * 128], ckvb[:, 0:N], True, True, [Bwkv, Bckvb], [Bpk])
                    sx, Bsx = sx_r.next()
                    act(sx[0:64, 0:N], pk[0:64, 0:N], AF.Square, [Bpk], [Bsx])
                    kf, Bkf = qf_r.next()
                    acp(kf[0:64, 0:N], pk[0:64, 0:N], [Bpk], [Bkf])
                    pm, Bpm = pb()
                    mm(pm[0:96, 0:N], ones_b[:, 0:96], sx[:, 0:N], True, True, [Bones, Bsx], [Bpm])
                    ta, Bta = f32_r.next()
                    tb_, Btb = f32_r.next()
                    rk, Brk = f32_r.next()
                    tt("dve", ta[0:96, 0:N], pm[0:96, 0:N], S2[:, 0:N], ALU.mult, [Bpm, BS2], [Bta])
                    tt("pool", ta[0:96, 0:N], ta[0:96, 0:N], MR[:, 0:N], ALU.add, [BMR], [Bta])
                    act(tb_[0:96, 0:N], ta[0:96, 0:N], AF.Sqrt, [Bta], [Btb], scale=1.0 / 96.0)
                    recip(rk[0:96, 0:N], tb_[0:96, 0:N], [Btb], [Brk])
                    tt("pool", tb_[0:64, 0:N], rk[0:64, 0:N], skv[0:64, 0:N], ALU.mult, [Brk, Bskv], [Btb])
                    ko, Bko = ko_r.next()
                    stt("dve", ko[0:64, 0:N], kf[0:64, 0:N], vec[0:64, 13:14], tb_[0:64, 0:N], ALU.mult, ALU.mult,
                        [Bkf, Bvec, Btb], [Bko])
                    tt("pool", ko[64:96, 0:N], KRr[64:96, 0:N], rk[64:96, 0:N], ALU.mult, [BKRr, Brk], [Bko])
                    S.dma("sp", k_gath[r * 768 + hh * 96:r * 768 + (hh + 1) * 96, t0:t0 + N], ko[0:96, 0:N], reads=[Bko])
                for j0 in range(0, N, 128):
                    nt = min(128, N - j0)
                    pv, Bpv = pb()
                    mm(pv[:, 0:512], ckvb[:, j0:j0 + 128], wv_b[:, :], True, True, [Bckvb, Bwv], [Bpv])
                    p1, Bp1 = pb()
                    mm(p1[:, 0:2], sqkv[:, j0:j0 + 128], ones_b[:, 0:2], True, True, [Bsqkv, Bones], [Bp1])
                    svt, Bsvt = sv_r.next()
                    act(svt[0:nt, 0:1], p1[0:nt, 0:1], AF.Sqrt, [Bp1], [Bsvt], bias=EPS, scale=1.0 / 128.0)
                    recip(svt[0:nt, 1:2], svt[0:nt, 0:1], [], [Bsvt])
                    vo, Bvo = vo_r.next()
                    ts("dve", vo[0:nt, :], pv[0:nt, 0:512], svt[0:nt, 1:2], None, ALU.mult, None, [Bpv, Bsvt], [Bvo])
                    S.dma("sp", v_gath[r * LT + t0 + j0:r * LT + t0 + j0 + nt, :], vo[0:nt, :], reads=[Bvo])

            cur = load(0)
            for i in range(len(tiles)):
                nxt = load(i + 1) if i + 1 < len(tiles) else None
                compute(i, *cur)
                cur = nxt
            S.flush()

    def phaseC(l):
        with ExitStack() as es:
            nchmax = (TP + 127) // 128
            KT_r = Ring(nc, es, "KT", 2, [96, TP], BF16)
            V_r = Ring(nc, es, "Vt", 2, [128, nchmax, 65], BF16)
            QT_r = Ring(nc, es, "QT", 2, [96, 512], BF16)
            PT_r = Ring(nc, es, "PT", 4, [128, 512], BF16)
            rec_r = Ring(nc, es, "recC", 2, [128, 512], F32)
            for (rt, Brt) in rec_r.items:
                mset("pool", rt[:, :], 0.0, [Brt])
            bcs_r = Ring(nc, es, "bcsC", 2, [65, 512], F32)
            yo_r = Ring(nc, es, "yoC", 2, [65, 512], F32)
            for (V, BV) in V_r.items:
                mset("pool", V[:, :, 0:1], 1.0, [BV])
            stb = PS[0:3]
            ob = PS[3:5]
            bcb = PS[5:7]
            units = [(s, hh) for s in range(2) for hh in range(8)]

            def chunks(T):
                nfull, rem = T // 128, T % 128
                if rem == 0:
                    return [(i * 128, 128) for i in range(nfull)]
                if rem >= 65 or nfull == 0:
                    return [(i * 128, 128) for i in range(nfull)] + [(nfull * 128, rem)]
                tot = 128 + rem
                a = (tot + 1) // 2
                return [(i * 128, 128) for i in range(nfull - 1)] + [((nfull - 1) * 128, a), ((nfull - 1) * 128 + a, tot - a)]

            def loadKV(s, hh):
                T, L = seq_T[s], seg_len[s]
                KT, BK = KT_r.next()
                V, BV = V_r.next()
                for r in range(VR):
                    S.dma("sp", KT[0:96, r * L:(r + 1) * L],
                          k_gath[r * 768 + hh * 96:r * 768 + (hh + 1) * 96, seg_off[s]:seg_off[s] + L],
                          reads=[Bkg], writes=[BK])
                chs = chunks(T)
                for r in range(VR):
                    n0, n1 = r * L, (r + 1) * L
                    base = r * LT + seg_off[s]
                    ci = 0
                    while ci < len(chs):
                        cs0, csz = chs[ci]
                        lo, hi = max(cs0, n0), min(cs0 + csz, n1)
                        if lo >= hi:
                            ci += 1
                            continue
                        if lo == cs0 and hi == cs0 + csz and csz == 128:
                            cj = ci
                            while cj < len(chs) and chs[cj][1] == 128 and chs[cj][0] + 128 <= n1:
                                cj += 1
                            nf = cj - ci
                            S.dma("sp", V[:, ci:ci + nf, 1:65],
                                  v_gath[base + lo - n0:base + lo - n0 + nf * 128, hh * 64:(hh + 1) * 64]
                                  .rearrange("(c p) d -> p c d", p=128), reads=[Bvg], writes=[BV])
                            ci = cj
                        else:
                            S.dma("sp", V[lo - cs0:hi - cs0, ci, 1:65],
                                  v_gath[base + lo - n0:base + hi - n0, hh * 64:(hh + 1) * 64],
                                  reads=[Bvg], writes=[BV])
                            if hi == cs0 + csz:
                                ci += 1
                            else:
                                break
                return KT, BK, V, BV

            ui = [0]

            def unit(s, hh, KT, BK, V, BV):
                T, L = seq_T[s], seg_len[s]
                chs = chunks(T)
                nch = len(chs)
                for r_q0 in [(rr_, qq_) for rr_ in range(VR) for qq_ in range(0, L, 512)]:
                    q0 = r_q0[1]
                    qc = r_q0[0] * LT + seg_off[s] + q0
                    nq = min(512, L - q0)
                    QT, BQ = QT_r.next()
                    S.dma("sp", QT[0:96, 0:nq], q_scr[hh * 96:(hh + 1) * 96, qc:qc + nq], writes=[BQ])
                    o, Bo = ob[ui[0] % 2]
                    bc, Bbc = bcb[ui[0] % 2]
                    ui[0] += 1
                    pts = {}
                    for k in range(nch + 2):
                        if k < nch:
                            K0, KS = chs[k]
                            st, Bst = stb[k % 3]
                            mm(st[0:KS, 0:nq], KT[0:96, K0:K0 + KS], QT[0:96, 0:nq], True, True,
                               [BK, BQ], [Bst])
                            pt, Bpt = PT_r.next()
                            act(pt[0:KS, 0:nq], st[0:KS, 0:nq], AF.Exp, [Bst], [Bpt], scale=SCALE)
                            pts[k] = (pt, Bpt, KS)
                        if k >= 2:
                            kk = k - 2
                            pt, Bpt, KS = pts.pop(kk)
                            mm(o[0:65, 0:nq], V[0:KS, kk, 0:65], pt[0:KS, 0:nq], kk == 0, kk == nch - 1,
                               [BV, Bpt], [Bo])
                    rec, Brec = rec_r.next()
                    recip(rec[0:1, 0:nq], o[0:1, 0:nq], [Bo], [Brec])
                    mm(bc[0:65, 0:nq], ones_f[:, 0:65], rec[:, 0:nq], True, True, [Bonesf, Brec], [Bbc])
                    bcs, Bbcs = bcs_r.next()
                    S.op("act", lambda e, oo=bcs[0:65, 0:nq], ii=bc[0:65, 0:nq]: e.copy(out=oo, in_=ii), [Bbc], [Bbcs])
                    yo, Byo = yo_r.next()
                    tt("dve", yo[0:65, 0:nq], o[0:65, 0:nq], bcs[0:65, 0:nq], ALU.mult, [Bo, Bbcs], [Byo])
                    S.dma("sp", ya_scr[hh * 64:(hh + 1) * 64, qc:qc + nq], yo[1:65, 0:nq], reads=[Byo])

            cur = loadKV(*units[0])
            for i, (s, hh) in enumerate(units):
                nxt = loadKV(*units[i + 1]) if i + 1 < len(units) else None
                unit(s, hh, *cur)
                cur = nxt
            S.flush()

    def phaseB(l):
        with ExitStack() as es:
            LM = LP
            vec, Bvec = tile(es, "vecB", [128, NV], F32)
            lw, Blw = tile(es, "lwB", [128, 16, 128], F32)
            bd_b, Bbd = tile(es, "bdB", [128, 16, 128], BF16)
            coef, Bcoef = tile(es, "coefB", [128, 32], F32)
            carry, Bcarry = tile(es, "carryB", [128, 1], F32)
            S.dma("sp", vec[:, :], vecs[l, :, :], writes=[Bvec])
            S.dma("sp", lw[:, :, :], lruw[l].rearrange("f m p q -> p (f m) q"), writes=[Blw])
            cp("dve", bd_b[:, :, :], lw[:, :, :], [Blw], [Bbd])
            act(coef[:, 0:8], vec[:, 67:75], AF.Exp, [Bvec], [Bcoef], scale=-1.0)
            act(coef[:, 8:16], coef[:, 0:8], AF.Ln, [], [Bcoef], bias=1.0)
            ts("dve", coef[:, 16:24], coef[:, 8:16], -8.0, None, ALU.mult, None, [], [Bcoef])
            ts("dve", coef[:, 24:32], coef[:, 8:16], -16.0, None, ALU.mult, None, [], [Bcoef])
            U_r = Ring(nc, es, "UB", 2, [128, LM + 4], F32)
            G_r = Ring(nc, es, "GB", 2, [128, LM], F32)
            HF_r = Ring(nc, es, "HFB", 2, [128, LM], F32)
            H_r = Ring(nc, es, "HB", 2, [128, LM], F32)
            Y_r = Ring(nc, es, "YB", 2, [128, LM], F32)
            xc, Bxc = tile(es, "xcB", [128, LM], F32)
            xcb, Bxcb = tile(es, "xcbB", [128, LM], BF16)
            rr, Brr = tile(es, "rrB", [128, LM], F32)
            ig, Big = tile(es, "igB", [128, LM], F32)
            aa, Baa = tile(es, "aaB", [128, LM], F32)
            a2, Ba2 = tile(es, "a2B", [128, LM], F32)
            bb, Bbb = tile(es, "bbB", [128, LM], F32)
            BHF = {}

            def loads(s_, m, L, r, direction):
                o0 = seg_off[s_]
                U, BU = U_r.next()
                if r == 0:
                    mset("pool", U[:, 0:2], 0.0, [BU])
                if r == VR - 1:
                    mset("pool", U[:, 2 + L:4 + L], 0.0, [BU])
                S.dma("sp", U[:, 2:2 + L], ug_gath[r * D + m * 128:r * D + (m + 1) * 128, o0:o0 + L],
                      reads=[Bug], writes=[BU])
                if r > 0:
                    S.dma("sp", U[:, 0:2], ug_gath[(r - 1) * D + m * 128:(r - 1) * D + (m + 1) * 128, o0 + L - 2:o0 + L],
                          reads=[Bug], writes=[BU])
                if r < VR - 1:
                    S.dma("sp", U[:, 2 + L:4 + L], ug_gath[(r + 1) * D + m * 128:(r + 1) * D + (m + 1) * 128, o0:o0 + 2],
                          reads=[Bug], writes=[BU])
                Gt = BG = HF = BHFt = None
                if direction == 1:
                    Gt, BG = G_r.next()
                    S.dma("sp", Gt[:, 0:L], ug_gath[r * D + 512 + m * 128:r * D + 512 + (m + 1) * 128, o0:o0 + L],
                          reads=[Bug], writes=[BG])
                    HF, BHFt = HF_r.next()
                    S.dma("sp", HF[:, 0:L], hf_scr[:, r * L:(r + 1) * L], reads=[BHF[(s_, m, r)]], writes=[BHFt])
                return U, BU, Gt, BG, HF, BHFt

            def compute(s_, m, L, r, direction, U, BU, Gt, BG, HF, BHFt):
                o0 = seg_off[s_]
                zi = direction * 4 + m
                ts("dve", xc[:, 0:L], U[:, 0:L], vec[:, 31 + m * 4:32 + m * 4], vec[:, 47 + m:48 + m], ALU.mult, ALU.add,
                   [BU, Bvec], [Bxc])
                for tap in range(1, 4):
                    stt("dve", xc[:, 0:L], U[:, tap:tap + L], vec[:, 31 + m * 4 + tap:32 + m * 4 + tap], xc[:, 0:L],
                        ALU.mult, ALU.add, [BU, Bvec], [Bxc])
                cp("pool", xcb[:, 0:L], xc[:, 0:L], [Bxc], [Bxcb])
                for c0 in range(0, L, 512):
                    n = min(512, L - c0)
                    ps, Bps = pb()
                    mm(ps[:, 0:n], bd_b[:, zi, :], xcb[:, c0:c0 + n], True, True, [Bbd, Bxcb], [Bps])
                    act(rr[:, c0:c0 + n], ps[:, 0:n], AF.Sigmoid, [Bps, Bvec], [Brr], bias=vec[:, 51 + zi:52 + zi])
                    ps, Bps = pb()
                    mm(ps[:, 0:n], bd_b[:, 8 + zi, :], xcb[:, c0:c0 + n], True, True, [Bbd, Bxcb], [Bps])
                    act(ig[:, c0:c0 + n], ps[:, 0:n], AF.Sigmoid, [Bps, Bvec], [Big], bias=vec[:, 59 + zi:60 + zi])
                act(aa[:, 0:L], rr[:, 0:L], AF.Exp, [Brr, Bcoef], [Baa], scale=coef[:, 16 + zi:17 + zi])
                act(a2[:, 0:L], rr[:, 0:L], AF.Exp, [Brr, Bcoef], [Ba2], scale=coef[:, 24 + zi:25 + zi])
                ts("dve", a2[:, 0:L], a2[:, 0:L], -1.0, 1.0, ALU.mult, ALU.add, [], [Ba2])
                act(a2[:, 0:L], a2[:, 0:L], AF.Sqrt, [], [Ba2])
                tt("pool", bb[:, 0:L], a2[:, 0:L], ig[:, 0:L], ALU.mult, [Ba2, Big], [Bbb])
                tt("dve", bb[:, 0:L], bb[:, 0:L], xc[:, 0:L], ALU.mult, [Bxc], [Bbb])
                H, BH = H_r.next()
                if direction == 0:
                    S.op("dve", lambda e, o=H[:, 0:L], d0=aa[:, 0:L], d1=bb[:, 0:L]: e.tensor_tensor_scan(
                        out=o, data0=d0, data1=d1, initial=carry[:, 0:1], op0=ALU.mult, op1=ALU.add),
                        [Baa, Bbb, Bcarry], [BH])
                    cp("dve", carry[:, 0:1], H[:, L - 1:L], [BH], [Bcarry])
                    BHF[(s_, m, r)] = Buf(f"hf{s_}_{m}_{r}_{l}")
                    S.dma("sp", hf_scr[:, r * L:(r + 1) * L], H[:, 0:L], reads=[BH], writes=[BHF[(s_, m, r)]], owner=BH)
                else:
                    S.op("dve", lambda e, o=H[:, 0:L][:, ::-1], d0=aa[:, 0:L][:, ::-1], d1=bb[:, 0:L][:, ::-1]: e.tensor_tensor_scan(
                        out=o, data0=d0, data1=d1, initial=carry[:, 0:1], op0=ALU.mult, op1=ALU.add),
                        [Baa, Bbb, Bcarry], [BH])
                    cp("dve", carry[:, 0:1], H[:, 0:1], [BH], [Bcarry])
                    tt("pool", H[:, 0:L], H[:, 0:L], HF[:, 0:L], ALU.add, [BHFt], [BH])
                    act(Gt[:, 0:L], Gt[:, 0:L], AF.Gelu_apprx_tanh, [], [BG])
                    Y, BY = Y_r.next()
                    tt("dve", Y[:, 0:L], H[:, 0:L], Gt[:, 0:L], ALU.mult, [BH, BG], [BY])
                    S.dma("sp", y_gath[m * 128:(m + 1) * 128, r * LT + o0:r * LT + o0 + L], Y[:, 0:L], reads=[BY])

            for s_ in range(2):
                L = seg_len[s_]
                for m in range(4):
                    for direction in (0, 1):
                        order = list(range(VR)) if direction == 0 else list(range(VR - 1, -1, -1))
                        for idx, r in enumerate(order):
                            if idx == 0:
                                mset("dve", carry[:, 0:1], 0.0, [Bcarry])
                            ld = loads(s_, m, L, r, direction)
                            compute(s_, m, L, r, direction, *ld)
            S.flush()

    def phaseD1(l, src):
        with ExitStack() as es:
            vec, Bvec = tile(es, "vecD", [128, NV], F32)
            wo_b, Bwo = tile(es, "wo_b", [128, 8, D], BF16)
            stg = Ring(nc, es, "wstgD", 2, [128, D], F32)
            S.dma("sp", vec[:, :], vecs[l, :, :], writes=[Bvec])
            for kc in range(8):
                st, Bst = stg.next()
                S.dma("sp", st[:, :], w_out[l, kc * 128:(kc + 1) * 128, :], writes=[Bst])
                ts("dve", wo_b[:, kc, :], st[:, :], vec[:, 15 + kc:16 + kc], None, ALU.mult, None, [Bst, Bvec], [Bwo])
            tiles = [(t0, min(512, NTOK - t0)) for t0 in range(0, NTOK, 512)]
            hT_r = Ring(nc, es, "hTD", 2, [128, 8, 512], F32)
            Y_r = Ring(nc, es, "YD", 2, [128, 8, 512], F32)
            sq_r = Ring(nc, es, "sqD", 2, [128, 8, 512], BF16)
            yb_r = Ring(nc, es, "ybD", 2, [128, 8, 512], BF16)
            f32_r = Ring(nc, es, "f32D", 4, [128, 512], F32)
            hsrc = src.rearrange("(kc p) t -> p kc t", p=128)
            hdst = hT_scr.rearrange("(kc p) t -> p kc t", p=128)
            ygv = y_gath.rearrange("(kc p) t -> p kc t", p=128)
            yav = ya_scr.rearrange("(kc p) t -> p kc t", p=128)
            def load(i):
                t0, N = tiles[i]
                h, Bh = hT_r.next()
                S.dma("sp", h[:, :, 0:N], hsrc[:, :, t0:t0 + N], writes=[Bh])
                Y, BY = Y_r.next()
                S.dma("sp", Y[:, 0:4, 0:N], ygv[:, :, t0:t0 + N], reads=[Byg], writes=[BY])
                S.dma("sp", Y[:, 4:8, 0:N], yav[:, :, t0:t0 + N], writes=[BY])
                return h, Bh, Y, BY

            def compute(i, h, Bh, Y, BY):
                t0, N = tiles[i]
                sq, Bsq = sq_r.next()
                yb, Byb = yb_r.next()
                for kc in range(8):
                    act(sq[:, kc, 0:N], Y[:, kc, 0:N], AF.Square, [BY], [Bsq])
                Rs = []
                for grp in range(2):
                    ps, Bps = pb()
                    for kc in range(4):
                        mm(ps[:, 0:N], ones_b[:, :], sq[:, grp * 4 + kc, 0:N], kc == 0, kc == 3, [Bones, Bsq], [Bps])
                    tmp, Btmp = f32_r.next()
                    R, BR = f32_r.next()
                    rstd_from(ps[:, 0:N], 1.0 / 512.0, EPS, tmp[:, 0:N], R[:, 0:N], Bps, Btmp, BR)
                    Rs.append((R, BR))
                for kc in range(8):
                    R, BR = Rs[kc // 4]
                    tt("dve" if kc % 2 == 0 else "pool", yb[:, kc, 0:N], Y[:, kc, 0:N], R[:, 0:N], ALU.mult,
                       [BY, BR], [Byb])
                for oc in range(8):
                    ps, Bps = pb()
                    for kc in range(8):
                        mm(ps[:, 0:N], wo_b[:, kc, oc * 128:(oc + 1) * 128], yb[:, kc, 0:N], kc == 0, kc == 7,
                           [Bwo, Byb], [Bps])
                    tt("dve", h[:, oc, 0:N], h[:, oc, 0:N], ps[:, 0:N], ALU.add, [Bps], [Bh])
                S.dma("sp", hdst[:, :, t0:t0 + N], h[:, :, 0:N], reads=[Bh])

            cur = load(0)
            for i in range(len(tiles)):
                nxt = load(i + 1) if i + 1 < len(tiles) else None
                compute(i, *cur)
                cur = nxt
            S.flush()

    def phaseD2(l, dst):
        with ExitStack() as es:
            NT = 256
            vec, Bvec = tile(es, "vecE", [128, NV], F32)
            wu_b, Bwu = tile(es, "wu_b", [128, 8, 4 * D], BF16)
            wd_b, Bwd = tile(es, "wd_b", [128, 32, D], BF16)
            S.dma("sp", vec[:, :], vecs[l, :, :], writes=[Bvec])
            with ExitStack() as es2:
                stg = Ring(nc, es2, "wstgE", 2, [128, 4 * D], F32)
                for kc in range(8):
                    st, Bst = stg.next()
                    S.dma("sp", st[:, :], w_up[l, kc * 128:(kc + 1) * 128, :], writes=[Bst])
                    ts("dve" if kc % 2 == 0 else "pool", wu_b[:, kc, :], st[:, :], vec[:, 23 + kc:24 + kc], None,
                       ALU.mult, None, [Bst, Bvec], [Bwu])
                for f4 in range(8):
                    st, Bst = stg.next()
                    S.dma("sp", st[:, :].rearrange("p (f n) -> p f n", n=D),
                          w_down[l, f4 * 512:(f4 + 1) * 512, :].rearrange("(f p) n -> p f n", p=128), writes=[Bst])
                    cp("dve" if f4 % 2 == 0 else "pool", wd_b[:, f4 * 4:(f4 + 1) * 4, :],
                       st[:, :].rearrange("p (f n) -> p f n", n=D), [Bst], [Bwd])
                S.flush()
            tiles = [(t0, min(NT, NTOK - t0)) for t0 in range(0, NTOK, NT)]
            hT_r = Ring(nc, es, "hTE", 2, [128, 8, NT], F32)
            sq_r = Ring(nc, es, "sqE", 2, [128, 8, NT], BF16)
            hb_r = Ring(nc, es, "hbE", 2, [128, 8, NT], BF16)
            ac_r = Ring(nc, es, "acE", 2, [128, 32, NT], BF16)
            f32_r = Ring(nc, es, "f32E", 6, [128, NT], F32)
            hsrc = hT_scr.rearrange("(kc p) t -> p kc t", p=128)
            hdst = dst.rearrange("(kc p) t -> p kc t", p=128)

            def load(i):
                t0, N = tiles[i]
                h, Bh = hT_r.next()
                S.dma("sp", h[:, :, 0:N], hsrc[:, :, t0:t0 + N], writes=[Bh])
                return h, Bh

            def compute(i, h, Bh):
                t0, N = tiles[i]
                sq, Bsq = sq_r.next()
                hb, Bhb = hb_r.next()
                for kc in range(8):
                    act(sq[:, kc, 0:N], h[:, kc, 0:N], AF.Square, [Bh], [Bsq])
                ps, Bps = pb()
                for kc in range(8):
                    mm(ps[:, 0:N], ones_b[:, :], sq[:, kc, 0:N], kc == 0, kc == 7, [Bones, Bsq], [Bps])
                tmp, Btmp = f32_r.next()
                R, BR = f32_r.next()
                rstd_from(ps[:, 0:N], 1.0 / D, EPS, tmp[:, 0:N], R[:, 0:N], Bps, Btmp, BR)
                for kc in range(8):
                    tt("dve" if kc % 2 == 0 else "pool", hb[:, kc, 0:N], h[:, kc, 0:N], R[:, 0:N], ALU.mult,
                       [Bh, BR], [Bhb])
                ac, Bac = ac_r.next()
                for fc in range(32):
                    ps, Bps = pb()
                    for kc in range(8):
                        mm(ps[:, 0:N], wu_b[:, kc, fc * 128:(fc + 1) * 128], hb[:, kc, 0:N], kc == 0, kc == 7,
                           [Bwu, Bhb], [Bps])
                    rl, Brl = f32_r.next()
                    act(rl[:, 0:N], ps[:, 0:N], AF.Relu, [Bps], [Brl])
                    tt("dve" if fc % 2 == 0 else "pool", ac[:, fc, 0:N], rl[:, 0:N], rl[:, 0:N], ALU.mult, [Brl], [Bac])
                for oc in range(8):
                    ps, Bps = pb()
                    for fc in range(32):
                        mm(ps[:, 0:N], wd_b[:, fc, oc * 128:(oc + 1) * 128], ac[:, fc, 0:N], fc == 0, fc == 31,
                           [Bwd, Bac], [Bps])
                    tt("dve", h[:, oc, 0:N], h[:, oc, 0:N], ps[:, 0:N], ALU.add, [Bps], [Bh])
                S.dma("sp", hdst[:, :, t0:t0 + N], h[:, :, 0:N], reads=[Bh])

            cur = load(0)
            for i in range(len(tiles)):
                nxt = load(i + 1) if i + 1 < len(tiles) else None
                compute(i, *cur)
                cur = nxt
            S.flush()

    for l in range(NLAYER):
        src = xT if l == 0 else hT_scr
        if "A" in PHASES:
            phaseA(l, src)
        if "C" in PHASES:
            phaseC(l)
        if "B" in PHASES:
            phaseB(l)
        if "D" in PHASES:
            phaseD1(l, src)
        if "E" in PHASES:
            phaseD2(l, yT if l == NLAYER - 1 else hT_scr)
    S.flush()
    G.close()
    return nc, dict(LT=LT, NTOK=NTOK, seg_len=seg_len, seg_off=seg_off, seq_T=seq_T, LP=LP, LS=LS, TP=TP, TS=TS)


def prep(inp, SEQ, DSEQ, meta):
    LT, NTOK, seg_len, seg_off = meta["LT"], meta["NTOK"], meta["seg_len"], meta["seg_off"]
    f = lambda k: np.asarray(inp[k], dtype=np.float32)
    xp, xs, mt = f("x_prompt"), f("x_sample"), f("meta_tokens")
    nP, nS = xp.shape[0], xs.shape[0]
    inv_freq = (1.0 / (10000.0 ** (np.arange(0, 32, 2, dtype=np.float32) / np.float32(32)))).astype(np.float32)
    qg, kg = f("qk_q_g"), f("qk_k_g")

    def swp(g):
        o = np.zeros(96, np.float32)
        o[64:80] = g[80:96]
        o[80:96] = g[64:80]
        return o

    vecs = np.zeros((2, 128, NV), np.float32)
    lruw = np.zeros((2, 4, 4, 128, 128), np.float32)
    for l in range(2):
        vecs[l, :, 0:8] = f("norm_mix_g")[l].reshape(8, 128).T
        vecs[l, :, 8:10] = f("q_norm_g")[l].reshape(2, 128).T
        vecs[l, :, 10] = f("kv_norm_g")[l]
        vecs[l, 0:96, 11] = qg[l]
        vecs[l, 0:96, 12] = swp(qg[l])
        vecs[l, 0:96, 13] = kg[l]
        vecs[l, 0:96, 14] = swp(kg[l])
        vecs[l, :, 15:19] = f("out_norm_lru_g")[l].reshape(4, 128).T
        vecs[l, :, 19:23] = f("out_norm_attn_g")[l].reshape(4, 128).T
        vecs[l, :, 23:31] = f("norm_ff_g")[l].reshape(8, 128).T
        for m in range(4):
            ch = slice(m * 128, (m + 1) * 128)
            vecs[l, :, 31 + m * 4:35 + m * 4] = f("conv_w")[l][:, ch].T
            vecs[l, :, 47 + m] = f("conv_b")[l][ch]
            for z in range(2):
                vecs[l, :, 51 + z * 4 + m] = f("lru_ba")[l, z][ch]
                vecs[l, :, 59 + z * 4 + m] = f("lru_bx")[l, z][ch]
                vecs[l, :, 67 + z * 4 + m] = f("lru_lambda")[l, z][ch]
                for half in range(2):
                    hs = slice(half * 64, (half + 1) * 64)
                    lruw[l, z, m, hs, hs] = f("lru_wa")[l, z, 2 * m + half]
                    lruw[l, 2 + z, m, hs, hs] = f("lru_wx")[l, z, 2 * m + half]
    shared = {"vecs": vecs, "lruw": lruw, "w_in": f("w_in"), "w_uq": f("w_uq"), "w_ukv": f("w_ukv"),
              "w_out": f("w_out"), "w_up": f("w_up"), "w_down": f("w_down")}
    in_maps = []
    cache = {}
    for c in range(NCORE):
        key = (c % nP, c % nS)
        if key not in cache:
            seqs = [np.concatenate([mt, xp[key[0]]], 0), np.concatenate([mt, xs[key[1]]], 0)]
            xT = np.empty((D, NTOK), np.float32)
            pos = np.empty(NTOK, np.float32)
            for r in range(VR):
                for s in range(2):
                    L = seg_len[s]
                    c0 = r * LT + seg_off[s]
                    xT[:, c0:c0 + L] = seqs[s][r * L:(r + 1) * L].T
                    pos[c0:c0 + L] = np.arange(r * L, (r + 1) * L, dtype=np.float32)
            ang = pos[None, :] * inv_freq[:, None]
            cs = np.empty((2, 32, NTOK), np.float32)
            cs[0, 0:16] = np.cos(ang)
            cs[0, 16:32] = np.cos(ang)
            cs[1, 0:16] = np.sin(ang)
            cs[1, 16:32] = np.sin(ang)
            cache[key] = (xT, cs)
        xT, cs = cache[key]
        m = {"xT": xT, "cs": cs}
        m.update(shared)
        in_maps.append(m)
    return in_maps


_CACHE = {}


def run(inp, SEQ, DSEQ, trace=False):
    key = (SEQ, DSEQ)
    if key not in _CACHE:
        _CACHE[key] = build(SEQ, DSEQ)
    nc, meta = _CACHE[key]
    in_maps = prep(inp, SEQ, DSEQ, meta)
    res = run_bass_kernel_spmd(nc, in_maps, core_ids=list(range(NCORE)), **({"trace": True} if trace else {}))
    B, DB = inp["x_prompt"].shape[0], inp["x_sample"].shape[0]
    TP, TS, LT, LP, LS = meta["TP"], meta["TS"], meta["LT"], meta["LP"], meta["LS"]
    yp = np.empty((B, TP, D), np.float32)
    ys = np.empty((DB, TS, D), np.float32)
    for b in range(B):
        yT = np.asarray(res.results[b]["yT"])
        for r in range(VR):
            yp[b, r * LP:(r + 1) * LP] = yT[:, r * LT:r * LT + LP].T
    for b in range(DB):
        yT = np.asarray(res.results[b]["yT"])
        for r in range(VR):
            ys[b, r * LS:(r + 1) * LS] = yT[:, r * LT + LP:r * LT + LP + LS].T
    return (np.ascontiguousarray(yp[:, 16:]), np.ascontiguousarray(ys[:, 16:])), res


def kernel(**inputs):
    SEQ = inputs["x_prompt"].shape[1]
    DSEQ = inputs["x_sample"].shape[1]
    out, _ = run(inputs, SEQ, DSEQ)
    return out
```

```python
import numpy as np
from contextlib import ExitStack
import concourse.bass as bass
import concourse.mybir as mybir
from concourse.bass_utils import run_bass_kernel_spmd

F32 = mybir.dt.float32
BF16 = mybir.dt.bfloat16
AF = mybir.ActivationFunctionType
ALU = mybir.AluOpType

D = 1024
NCORE = 8
VR = 8
NV = 75
EPS = 1e-6
IN_W = 1440
SCALE = 96 ** -0.5
PHASES = "AGCBDE"
NLAYER = 2
DBG = 99
VAR = 0


class Buf:
    __slots__ = ("name", "w", "r", "dsem")

    def __init__(self, name):
        self.name = name
        self.w = None
        self.r = []
        self.dsem = None


class Eng:
    def __init__(self, name, sem):
        self.name = name
        self.sem = sem
        self.cnt = 0
        self.known = {}
        self.insts = []


class Sched:
    def __init__(self, nc):
        self.nc = nc
        self.sems = {}
        self.E = {}
        for name in ("pe", "act", "dve", "pool", "sp"):
            self.sems[name] = nc.alloc_semaphore(name="sem_" + name)
            self.E[name] = Eng(name, name)
        self.dval = {}

    def _waits(self, eng, reads, writes):
        need = {}
        for b in reads:
            if b.w is not None:
                k, v = b.w
                if need.get(k, 0) < v:
                    need[k] = v
        for b in writes:
            if b.w is not None:
                k, v = b.w
                if need.get(k, 0) < v:
                    need[k] = v
            for (k, v) in b.r:
                if need.get(k, 0) < v:
                    need[k] = v
        out = []
        for k, v in need.items():
            if eng.name == "pe" and k == "pe":
                continue
            if eng.known.get(k, 0) >= v:
                continue
            eng.known[k] = v
            out.append((k, v))
        return out

    def _mark(self, tag, reads, writes):
        for b in reads:
            if len(b.r) > 64:
                m = {}
                for (k, v) in b.r:
                    if m.get(k, 0) < v:
                        m[k] = v
                b.r = list(m.items())
            b.r.append(tag)
        for b in writes:
            b.w = tag
            b.r = []

    def op(self, en, fn, reads=(), writes=()):
        eng = self.E[en]
        waits = self._waits(eng, reads, writes)
        eng.cnt += 1
        tag = (eng.sem, eng.cnt)
        eng.insts.append((waits, fn, (eng.sem, 1)))
        self._mark(tag, reads, writes)
        return tag

    def _own(self, owner):
        if owner.dsem is None:
            key = "d_" + owner.name
            if key not in self.sems:
                self.sems[key] = self.nc.alloc_semaphore(name=key)
                self.dval[key] = 0
            owner.dsem = key

    def dma(self, en, out, in_, reads=(), writes=(), owner=None, slow=False):
        eng = self.E[en]
        waits = self._waits(eng, reads, writes)
        if owner is None:
            owner = writes[0] if writes else reads[0]
        self._own(owner)
        self.dval[owner.dsem] += 16
        tag = (owner.dsem, self.dval[owner.dsem])
        kw = {"allow_slow_non_contiguous": True} if slow else {}
        def _f(e, o=out, i=in_):
            try:
                return e.dma_start(out=o, in_=i, **kw)
            except Exception:
                print("DMA FAIL", en, o, i)
                raise
        eng.insts.append((waits, _f, (owner.dsem, 16)))
        self._mark(tag, reads, writes)
        return tag

    def coll(self, ins, outs, reads, writes, owner):
        eng = self.E["pool"]
        waits = self._waits(eng, reads, writes)
        self._own(owner)
        self.dval[owner.dsem] += 1
        tag = (owner.dsem, self.dval[owner.dsem])
        rg = [list(range(NCORE))]
        eng.insts.append((waits, (lambda e: e.collective_compute("AllGather", ALU.bypass, replica_groups=rg,
                                                                 ins=ins, outs=outs)), (owner.dsem, None)))
        self._mark(tag, reads, writes)
        return tag

    def flush(self):
        eng = self.E["sp"]
        dr = []
        for key, val in self.dval.items():
            if val > 0 and eng.known.get(key, 0) < val:
                eng.known[key] = val
                dr.append((key, val))
        if dr:
            eng.insts.append((dr, None, None))
        sems = self.sems
        with self.nc.Block() as block:
            for name, attr in (("pe", "tensor"), ("act", "scalar"), ("dve", "vector"),
                               ("pool", "gpsimd"), ("sp", "sync")):
                e = self.E[name]
                if not e.insts:
                    continue

                def body(be, insts=e.insts):
                    for waits, fn, inc in insts:
                        for (k, v) in waits:
                            be.wait_ge(sems[k], v)
                        if fn is None:
                            continue
                        ins = fn(be)
                        if inc is not None:
                            if inc[1] is None:
                                ins.then_inc(sems[inc[0]])
                            else:
                                ins.then_inc(sems[inc[0]], inc[1])
                getattr(block, attr)(body)
                e.insts = []


_UID = [0]


def _uname(name):
    _UID[0] += 1
    return f"{name}_u{_UID[0]}"


class Ring:
    def __init__(self, nc, es, name, n, shape, dtype):
        self.items = []
        for i in range(n):
            t = es.enter_context(nc.sbuf_tensor(_uname(f"{name}{i}"), shape, dtype))
            self.items.append((t, Buf(f"{name}{i}")))
        self.i = 0

    def next(self):
        it = self.items[self.i % len(self.items)]
        self.i += 1
        return it


def build(SEQ, DSEQ):
    TP, TS = SEQ + 16, DSEQ + 16
    LP, LS = TP // VR, TS // VR
    assert LP * VR == TP and LS * VR == TS
    seg_len = [LP, LS]
    seq_T = [TP, TS]
    seg_off = [0, LP]
    LT = LP + LS
    NTOK = VR * LT

    nc = bass.Bass("TRN2", target_bir_lowering=False)

    def din(name, shape, dtype=F32):
        return nc.dram_tensor(name, shape, dtype, kind="ExternalInput").ap()

    xT = din("xT", [D, NTOK])
    cs = din("cs", [2, 32, NTOK])
    vecs = din("vecs", [2, 128, NV])
    lruw = din("lruw", [2, 4, 4, 128, 128])
    w_in = din("w_in", [2, D, IN_W])
    w_uq = din("w_uq", [2, 256, 768])
    w_ukv = din("w_ukv", [2, 128, 1024])
    w_out = din("w_out", [2, D, D])
    w_up = din("w_up", [2, D, 4 * D])
    w_down = din("w_down", [2, 4 * D, D])
    yT = nc.dram_tensor("yT", [D, NTOK], F32, kind="ExternalOutput").ap()

    def dscr(name, shape, dtype=F32):
        return nc.dram_tensor(name, shape, dtype, kind="Internal")

    hT_scr = dscr("hT_scr", [D, NTOK]).ap()
    ug_gath = dscr("ug_gath", [VR * D, LT]).ap()
    k_gath = dscr("k_gath", [VR * 768, LT], BF16).ap()
    v_gath = dscr("v_gath", [VR * LT, 512], BF16).ap()
    q_scr = dscr("q_scr", [768, NTOK], BF16).ap()
    hf_scr = dscr("hf_scr", [128, TP]).ap()
    y_gath = dscr("y_gath", [512, VR * LT]).ap()
    ya_scr = dscr("ya_scr", [512, NTOK]).ap()
    Bug, Bkg, Bvg, Byg = Buf("ug_gath"), Buf("k_gath"), Buf("v_gath"), Buf("y_gath")

    S = Sched(nc)
    G = ExitStack()

    def tile(es, name, shape, dtype):
        t = es.enter_context(nc.sbuf_tensor(_uname(name), shape, dtype))
        return t, Buf(name)

    PS = []
    PSP = []
    for i in range(4):
        t = G.enter_context(nc.psum_tensor(f"psp{i}", [128, 1024], F32))
        PSP.append((t, Buf(f"psp{i}")))
        PS.append((t[:, 0:512], Buf(f"ps{2 * i}")))
        PS.append((t[:, 512:1024], Buf(f"ps{2 * i + 1}")))
    pbi = [0]

    def pb():
        it = PS[pbi[0] % 8]
        pbi[0] += 1
        return it

    ones_b, Bones = tile(G, "ones_b", [128, 128], BF16)
    onesR_b, BonesR = tile(G, "onesR_b", [96, 96], BF16)
    ones_f, Bonesf = tile(G, "ones_f", [128, 128], F32)
    S.op("pool", lambda e: e.memset(ones_b[:, :], 1.0), writes=[Bones])
    S.op("pool", lambda e: e.memset(onesR_b[:, :], 0.0), writes=[BonesR])
    S.op("pool", lambda e: e.memset(onesR_b[64:96, :], 1.0), writes=[BonesR])
    S.op("pool", lambda e: e.memset(ones_f[:, :], 0.0), writes=[Bonesf])
    S.op("pool", lambda e: e.memset(ones_f[0:1, :], 1.0), writes=[Bonesf])

    def mm(out, lhsT, rhs, start, stop, reads, writes):
        S.op("pe", lambda e: e.matmul(out, lhsT=lhsT, rhs=rhs, start=start, stop=stop), reads, writes)

    def act(out, in_, func, reads, writes, bias=None, scale=None):
        kw = {}
        if bias is not None:
            kw["bias"] = bias
        if scale is not None:
            kw["scale"] = scale
        S.op("act", lambda e: e.activation(out=out, in_=in_, func=func, **kw), reads, writes)

    def tt(en, out, in0, in1, op, reads, writes):
        S.op(en, lambda e: e.tensor_tensor(out=out, in0=in0, in1=in1, op=op), reads, writes)

    def ts(en, out, in0, s1, s2, op0, op1, reads, writes):
        if s2 is None:
            S.op(en, lambda e: e.tensor_scalar(out=out, in0=in0, scalar1=s1, scalar2=None, op0=op0), reads, writes)
        else:
            S.op(en, lambda e: e.tensor_scalar(out=out, in0=in0, scalar1=s1, scalar2=s2, op0=op0, op1=op1),
                 reads, writes)

    def stt(en, out, in0, scalar, in1, op0, op1, reads, writes):
        S.op(en, lambda e: e.scalar_tensor_tensor(out=out, in0=in0, scalar=scalar, in1=in1, op0=op0, op1=op1),
             reads, writes)

    def cp(en, out, in_, reads, writes):
        S.op(en, lambda e: e.tensor_copy(out=out, in_=in_), reads, writes)

    def acp(out, in_, reads, writes):
        S.op("act", lambda e: e.copy(out=out, in_=in_), reads, writes)

    def recip(out, in_, reads, writes):
        S.op("dve", lambda e: e.reciprocal(out=out, in_=in_), reads, writes)

    def mset(en, ap, val, writes):
        S.op(en, lambda e: e.memset(ap, val), (), writes)

    def rstd_from(ps_ap, scale, bias_c, tmp_ap, out_ap, Bps, Btmp, Bout):
        act(tmp_ap, ps_ap, AF.Sqrt, [Bps], [Btmp], bias=bias_c, scale=scale)
        recip(out_ap, tmp_ap, [Btmp], [Bout])

    def phaseA(l, src):
        with ExitStack() as es:
            win_b, Bwin = tile(es, "win_b", [128, 8, 1632], BF16)
            wq_b, Bwq = tile(es, "wq_b", [128, 2, 768], BF16)
            wqr_b, Bwqr = tile(es, "wqr_b", [128, 2, 768], BF16)
            wkv_b, Bwkv = tile(es, "wkv_b", [128, 1024], BF16)
            wv_b, Bwv = tile(es, "wv_b", [128, 512], BF16)
            vec, Bvec = tile(es, "vecA", [128, NV], F32)
            stg = Ring(nc, es, "wstgA", 2, [128, IN_W], F32)
            S.dma("sp", vec[:, :], vecs[l, :, :], writes=[Bvec])
            mset("pool", win_b[:, :, 1440:1632], 0.0, [Bwin])
            mset("pool", wqr_b[:, :, :], 0.0, [Bwqr])
            for kc in range(8):
                st, Bst = stg.next()
                S.dma("sp", st[:, :], w_in[l, kc * 128:(kc + 1) * 128, :], writes=[Bst])
                g = vec[:, kc:kc + 1]
                ts("dve", win_b[:, kc, 0:1440], st[:, :], g, None, ALU.mult, None, [Bst, Bvec], [Bwin])
                ts("dve", win_b[:, kc, 1504:1536], st[:, 1408:1440], g, None, ALU.mult, None, [Bst, Bvec], [Bwin])
                ts("dve", win_b[:, kc, 1600:1616], st[:, 1424:1440], g, -1.0, ALU.mult, ALU.mult, [Bst, Bvec], [Bwin])
                ts("dve", win_b[:, kc, 1616:1632], st[:, 1408:1424], g, None, ALU.mult, None, [Bst, Bvec], [Bwin])
            for j in range(2):
                st, Bst = stg.next()
                S.dma("sp", st[:, 0:768], w_uq[l, j * 128:(j + 1) * 128, :], writes=[Bst])
                g = vec[:, 8 + j:9 + j]
                ts("dve", wq_b[:, j, :], st[:, 0:768], g, None, ALU.mult, None, [Bst, Bvec], [Bwq])
                sv = st[:, 0:768].rearrange("p (h c) -> p h c", c=96)
                rv = wqr_b[:, j, :].rearrange("p (h c) -> p h c", c=96)
                ts("dve", rv[:, :, 64:80], sv[:, :, 80:96], g, -1.0, ALU.mult, ALU.mult, [Bst, Bvec], [Bwqr])
                ts("dve", rv[:, :, 80:96], sv[:, :, 64:80], g, None, ALU.mult, None, [Bst, Bvec], [Bwqr])
            st, Bst = stg.next()
            S.dma("sp", st[:, 0:1024], w_ukv[l, :, :], writes=[Bst])
            ts("dve", wkv_b[:, :], st[:, 0:1024], vec[:, 10:11], None, ALU.mult, None, [Bst, Bvec], [Bwkv])
            cp("pool", wv_b[:, :].rearrange("p (h c) -> p h c", c=64),
               wkv_b[:, :].rearrange("p (h c) -> p h c", c=128)[:, :, 64:128], [Bwkv], [Bwv])

            tiles = [(r, t0, min(512, LT - t0)) for r in range(VR) for t0 in range(0, LT, 512)]
            hT_r = Ring(nc, es, "hTA", 2, [128, 8, 512], F32)
            cs_r = Ring(nc, es, "csA", 2, [96, 2, 512], F32)
            sq_r = Ring(nc, es, "sqA", 2, [128, 8, 512], BF16)
            hb_r = Ring(nc, es, "hbA", 2, [128, 8, 512], BF16)
            f32_r = Ring(nc, es, "f32A", 6, [128, 512], F32)
            stgo_r = Ring(nc, es, "stgoA", 3, [128, 512], F32)
            b16_r = Ring(nc, es, "b16A", 6, [128, 512], BF16)
            qo_r = Ring(nc, es, "qoA", 3, [96, 512], BF16)
            ko_r = Ring(nc, es, "koA", 3, [96, 512], BF16)
            vo_r = Ring(nc, es, "voA", 2, [128, 512], BF16)
            cqb_r = Ring(nc, es, "cqbA", 2, [128, 2, 512], BF16)
            sqq_r = Ring(nc, es, "sqqA", 2, [128, 2, 512], BF16)
            ckvb_r = Ring(nc, es, "ckvbA", 2, [128, 512], BF16)
            sqkv_r = Ring(nc, es, "sqkvA", 2, [128, 512], BF16)
            per_r = Ring(nc, es, "perA", 10, [96, 512], F32)
            sv_r = Ring(nc, es, "svA", 4, [128, 2], F32)
            qf_r = Ring(nc, es, "qfA", 3, [96, 512], F32)
            sx_r = Ring(nc, es, "sxA", 2, [128, 512], BF16)
            for (sxt, Bsxt) in sx_r.items:
                mset("pool", sxt[:, :], 0.0, [Bsxt])
            hsrc = src.rearrange("(kc p) t -> p kc t", p=128)
            csrc = cs.rearrange("two r t -> r two t")

            def load(i):
                r, t0, N = tiles[i]
                g0 = r * LT + t0
                h, Bh = hT_r.next()
                S.dma("sp", h[:, :, 0:N], hsrc[:, :, g0:g0 + N], writes=[Bh])
                c, Bc = cs_r.next()
                S.dma("sp", c[64:96, :, 0:N], csrc[:, :, g0:g0 + N], writes=[Bc])
                return h, Bh, c, Bc

            def compute(i, h, Bh, c, Bc):
                r, t0, N = tiles[i]
                g0 = r * LT + t0
                if DBG < 1:
                    return
                sq, Bsq = sq_r.next()
                hb, Bhb = hb_r.next()
                for kc in range(8):
                    act(sq[:, kc, 0:N], h[:, kc, 0:N], AF.Square, [Bh], [Bsq])
                ps, Bps = pb()
                for kc in range(8):
                    mm(ps[:, 0:N], ones_b[:, :], sq[:, kc, 0:N], kc == 0, kc == 7, [Bones, Bsq], [Bps])
                tmp, Btmp = f32_r.next()
                R, BR = f32_r.next()
                rstd_from(ps[:, 0:N], 1.0 / D, EPS, tmp[:, 0:N], R[:, 0:N], Bps, Btmp, BR)
                for kc in range(8):
                    tt("dve", hb[:, kc, 0:N], h[:, kc, 0:N], R[:, 0:N], ALU.mult, [Bh, BR], [Bhb])
                for oc in range(8):
                    ps, Bps = pb()
                    for kc in range(8):
                        mm(ps[:, 0:N], win_b[:, kc, oc * 128:(oc + 1) * 128], hb[:, kc, 0:N], kc == 0, kc == 7,
                           [Bwin, Bhb], [Bps])
                    so, Bso = stgo_r.next()
                    if oc % 2 == 0:
                        S.op("act", lambda e, o=so[:, 0:N], p=ps[:, 0:N]: e.copy(out=o, in_=p), [Bps], [Bso])
                    else:
                        cp("dve", so[:, 0:N], ps[:, 0:N], [Bps], [Bso])
                    S.dma("sp", ug_gath[r * D + oc * 128:r * D + (oc + 1) * 128, t0:t0 + N], so[:, 0:N], reads=[Bso])
                if DBG < 2:
                    return
                cqb, Bcqb = cqb_r.next()
                sqq, Bsqq = sqq_r.next()
                for j in range(2):
                    ps, Bps = pb()
                    for kc in range(8):
                        mm(ps[:, 0:N], win_b[:, kc, 1024 + j * 128:1152 + j * 128], hb[:, kc, 0:N], kc == 0, kc == 7,
                           [Bwin, Bhb], [Bps])
                    act(sqq[:, j, 0:N], ps[:, 0:N], AF.Square, [Bps], [Bsqq])
                    acp(cqb[:, j, 0:N], ps[:, 0:N], [Bps], [Bcqb])
                ckvb, Bckvb = ckvb_r.next()
                sqkv, Bsqkv = sqkv_r.next()
                ps, Bps = pb()
                for kc in range(8):
                    mm(ps[:, 0:N], win_b[:, kc, 1280:1408], hb[:, kc, 0:N], kc == 0, kc == 7, [Bwin, Bhb], [Bps])
                act(sqkv[:, 0:N], ps[:, 0:N], AF.Square, [Bps], [Bsqkv])
                acp(ckvb[:, 0:N], ps[:, 0:N], [Bps], [Bckvb])
                if DBG == 2:
                    return
                kr, Bkr = per_r.next()
                krot, Bkrot = per_r.next()
                sqr, Bsqr = b16_r.next()
                ps, Bps = pb()
                for kc in range(8):
                    mm(ps[0:96, 0:N], win_b[:, kc, 1440:1536], hb[:, kc, 0:N], kc == 0, kc == 7, [Bwin, Bhb], [Bps])
                act(sqr[0:96, 0:N], ps[0:96, 0:N], AF.Square, [Bps], [Bsqr])
                acp(kr[64:96, 0:N], ps[64:96, 0:N], [Bps], [Bkr])
                ps, Bps = pb()
                for kc in range(8):
                    mm(ps[0:96, 0:N], win_b[:, kc, 1536:1632], hb[:, kc, 0:N], kc == 0, kc == 7, [Bwin, Bhb], [Bps])
                cp("dve", krot[64:96, 0:N], ps[64:96, 0:N], [Bps], [Bkrot])
                if DBG < 3:
                    return
                Cq, BCq = per_r.next()
                ps, Bps = pb()
                for j in range(2):
                    mm(ps[0:96, 0:N], ones_b[:, 0:96], sqq[:, j, 0:N], j == 0, j == 1, [Bones, Bsqq], [Bps])
                ts("dve", Cq[:, 0:N], ps[0:96, 0:N], 96.0 * EPS / 256.0, 96.0 * EPS * EPS, ALU.mult, ALU.add,
                   [Bps], [BCq])
                S2, BS2 = per_r.next()
                skv, Bskv = per_r.next()
                MR, BMR = per_r.next()
                KRr, BKRr = per_r.next()
                tA, BtA = per_r.next()
                ps, Bps = pb()
                mm(ps[0:96, 0:N], ones_b[:, 0:96], sqkv[:, 0:N], True, True, [Bones, Bsqkv], [Bps])
                ts("dve", tA[:, 0:N], ps[0:96, 0:N], 1.0 / 128.0, EPS, ALU.mult, ALU.add, [Bps], [BtA])
                recip(S2[:, 0:N], tA[:, 0:N], [BtA], [BS2])
                act(skv[:, 0:N], S2[:, 0:N], AF.Sqrt, [BS2], [Bskv])
                ps, Bps = pb()
                mm(ps[0:96, 0:N], onesR_b[:, :], sqr[0:96, 0:N], True, True, [BonesR, Bsqr], [Bps])
                ts("dve", MR[:, 0:N], ps[0:96, 0:N], 96.0 * EPS, None, ALU.add, None, [Bps], [BMR])
                t1, Bt1 = per_r.next()
                t2, Bt2 = per_r.next()
                stt("dve", t1[64:96, 0:N], kr[64:96, 0:N], vec[64:96, 13:14], c[64:96, 0, 0:N], ALU.mult, ALU.mult,
                    [Bkr, Bvec, Bc], [Bt1])
                stt("dve", t2[64:96, 0:N], krot[64:96, 0:N], vec[64:96, 14:15], c[64:96, 1, 0:N], ALU.mult, ALU.mult,
                    [Bkrot, Bvec, Bc], [Bt2])
                tt("pool", KRr[64:96, 0:N], t1[64:96, 0:N], t2[64:96, 0:N], ALU.add, [Bt1, Bt2], [BKRr])
                if DBG < 4:
                    return
                for hh in range(8 if DBG != 5 else 0):
                    pq, Bpq = pb()
                    for j in range(2):
                        mm(pq[0:96, 0:N], wq_b[:, j, hh * 96:(hh + 1) * 96], cqb[:, j, 0:N], j == 0, j == 1,
                           [Bwq, Bcqb], [Bpq])
                    pr, Bpr = pb()
                    for j in range(2):
                        mm(pr[0:96, 0:N], wqr_b[:, j, hh * 96:(hh + 1) * 96], cqb[:, j, 0:N], j == 0, j == 1,
                           [Bwqr, Bcqb], [Bpr])
                    s2, Bs2 = b16_r.next()
                    act(s2[0:96, 0:N], pq[0:96, 0:N], AF.Square, [Bpq], [Bs2])
                    qf, Bqf = qf_r.next()
                    acp(qf[0:96, 0:N], pq[0:96, 0:N], [Bpq], [Bqf])
                    pm, Bpm = pb()
                    mm(pm[0:96, 0:N], ones_b[0:96, 0:96], s2[0:96, 0:N], True, True, [Bones, Bs2], [Bpm])
                    ta, Bta = f32_r.next()
                    tb_, Btb = f32_r.next()
                    rq, Brq = f32_r.next()
                    tt("dve", ta[0:96, 0:N], pm[0:96, 0:N], Cq[:, 0:N], ALU.add, [Bpm, BCq], [Bta])
                    act(tb_[0:96, 0:N], ta[0:96, 0:N], AF.Sqrt, [Bta], [Btb], scale=1.0 / 96.0)
                    recip(rq[0:96, 0:N], tb_[0:96, 0:N], [Btb], [Brq])
                    qo, Bqo = qo_r.next()
                    stt("dve", qo[0:64, 0:N], qf[0:64, 0:N], vec[0:64, 11:12], rq[0:64, 0:N], ALU.mult, ALU.mult,
                        [Bqf, Bvec, Brq], [Bqo])
                    u1, Bu1 = f32_r.next()
                    u2, Bu2 = f32_r.next()
                    stt("dve", u1[64:96, 0:N], qf[64:96, 0:N], vec[64:96, 11:12], c[64:96, 0, 0:N], ALU.mult, ALU.mult,
                        [Bqf, Bvec, Bc], [Bu1])
                    stt("dve", u2[64:96, 0:N], pr[64:96, 0:N], vec[64:96, 12:13], c[64:96, 1, 0:N], ALU.mult, ALU.mult,
                        [Bpr, Bvec, Bc], [Bu2])
                    tt("pool", u1[64:96, 0:N], u1[64:96, 0:N], u2[64:96, 0:N], ALU.add, [Bu2], [Bu1])
                    tt("pool", qo[64:96, 0:N], u1[64:96, 0:N], rq[64:96, 0:N], ALU.mult, [Bu1, Brq], [Bqo])
                    S.dma("sp", q_scr[hh * 96:(hh + 1) * 96, g0:g0 + N], qo[0:96, 0:N], reads=[Bqo])
                    pk, Bpk = pb()
                    mm(pk[:, 0:N], wkv_b[:, hh * 128:(hh + 1) * 128], ckvb[:, 0:N], True, True, [Bwkv, Bckvb], [Bpk])
                    sx, Bsx = sx_r.next()
                    act(sx[0:64, 0:N], pk[0:64, 0:N], AF.Square, [Bpk], [Bsx])
                    kf, Bkf = qf_r.next()
                    acp(kf[0:64, 0:N], pk[0:64, 0:N], [Bpk], [Bkf])
                    pm, Bpm = pb()
                    mm(pm[0:96, 0:N], ones_b[:, 0:96], sx[:, 0:N], True, True, [Bones, Bsx], [Bpm])
                    ta, Bta = f32_r.next()
                    tb_, Btb = f32_r.next()
                    rk, Brk = f32_r.next()
                    tt("dve", ta[0:96, 0:N], pm[0:96, 0:N], S2[:, 0:N], ALU.mult, [Bpm, BS2], [Bta])
                    tt("pool", ta[0:96, 0:N], ta[0:96, 0:N], MR[:, 0:N], ALU.add, [BMR], [Bta])
                    act(tb_[0:96, 0:N], ta[0:96, 0:N], AF.Sqrt, [Bta], [Btb], scale=1.0 / 96.0)
                    recip(rk[0:96, 0:N], tb_[0:96, 0:N], [Btb], [Brk])
                    tt("pool", tb_[0:64, 0:N], rk[0:64, 0:N], skv[0:64, 0:N], ALU.mult, [Brk, Bskv], [Btb])
                    ko, Bko = ko_r.next()
                    stt("dve", ko[0:64, 0:N], kf[0:64, 0:N], vec[0:64, 13:14], tb_[0:64, 0:N], ALU.mult, ALU.mult,
                        [Bkf, Bvec, Btb], [Bko])
                    tt("pool", ko[64:96, 0:N], KRr[64:96, 0:N], rk[64:96, 0:N], ALU.mult, [BKRr, Brk], [Bko])
                    S.dma("sp", k_gath[r * 768 + hh * 96:r * 768 + (hh + 1) * 96, t0:t0 + N], ko[0:96, 0:N], reads=[Bko])
                for j0 in range(0, N, 128):
                    nt = min(128, N - j0)
                    pv, Bpv = pb()
                    mm(pv[:, 0:512], ckvb[:, j0:j0 + 128], wv_b[:, :], True, True, [Bckvb, Bwv], [Bpv])
                    p1, Bp1 = pb()
                    mm(p1[:, 0:2], sqkv[:, j0:j0 + 128], ones_b[:, 0:2], True, True, [Bsqkv, Bones], [Bp1])
                    svt, Bsvt = sv_r.next()
                    act(svt[0:nt, 0:1], p1[0:nt, 0:1], AF.Sqrt, [Bp1], [Bsvt], bias=EPS, scale=1.0 / 128.0)
                    recip(svt[0:nt, 1:2], svt[0:nt, 0:1], [], [Bsvt])
                    vo, Bvo = vo_r.next()
                    ts("dve", vo[0:nt, :], pv[0:nt, 0:512], svt[0:nt, 1:2], None, ALU.mult, None, [Bpv, Bsvt], [Bvo])
                    S.dma("sp", v_gath[r * LT + t0 + j0:r * LT + t0 + j0 + nt, :], vo[0:nt, :], reads=[Bvo])

            cur = load(0)
            for i in range(len(tiles)):
                nxt = load(i + 1) if i + 1 < len(tiles) else None
                compute(i, *cur)
                cur = nxt
            S.flush()

    def phaseC(l):
        with ExitStack() as es:
            nchmax = (TP + 127) // 128
            KT_r = Ring(nc, es, "KT", 2, [96, TP], BF16)
            V_r = Ring(nc, es, "Vt", 2, [128, nchmax, 65], BF16)
            QT_r = Ring(nc, es, "QT", 3, [96, 512], BF16)
            PT_r = Ring(nc, es, "PT", 3, [128, 2, 512], BF16)
            rec_r = Ring(nc, es, "recC", 2, [128, 512], F32)
            for (rt, Brt) in rec_r.items:
                mset("pool", rt[:, :], 0.0, [Brt])
            bcs_r = Ring(nc, es, "bcsC", 2, [65, 512], F32)
            yo_r = Ring(nc, es, "yoC", 2, [65, 512], F32)
            for (V, BV) in V_r.items:
                mset("pool", V[:, :, 0:1], 1.0, [BV])
            stg = PSP[0:2]
            ob = PS[4:6]
            bcb = PS[6:8]
            pend = [None]

            def groups(chs):
                g = []
                i = 0
                while i < len(chs):
                    if i + 1 < len(chs) and chs[i][1] == chs[i + 1][1]:
                        g.append((i, 2))
                        i += 2
                    else:
                        g.append((i, 1))
                        i += 1
                return g
            units = [(s, hh) for s in range(2) for hh in range(8)]

            def chunks(T):
                nfull, rem = T // 128, T % 128
                if rem == 0:
                    return [(i * 128, 128) for i in range(nfull)]
                if rem >= 65 or nfull == 0:
                    return [(i * 128, 128) for i in range(nfull)] + [(nfull * 128, rem)]
                tot = 128 + rem
                a = (tot + 1) // 2
                return [(i * 128, 128) for i in range(nfull - 1)] + [((nfull - 1) * 128, a), ((nfull - 1) * 128 + a, tot - a)]

            def loadKV(s, hh):
                T, L = seq_T[s], seg_len[s]
                KT, BK = KT_r.next()
                V, BV = V_r.next()
                for r in range(VR):
                    S.dma("sp", KT[0:96, r * L:(r + 1) * L],
                          k_gath[r * 768 + hh * 96:r * 768 + (hh + 1) * 96, seg_off[s]:seg_off[s] + L],
                          reads=[Bkg], writes=[BK])
                chs = chunks(T)
                for r in range(VR):
                    n0, n1 = r * L, (r + 1) * L
                    base = r * LT + seg_off[s]
                    ci = 0
                    while ci < len(chs):
                        cs0, csz = chs[ci]
                        lo, hi = max(cs0, n0), min(cs0 + csz, n1)
                        if lo >= hi:
                            ci += 1
                            continue
                        if lo == cs0 and hi == cs0 + csz and csz == 128:
                            cj = ci
                            while cj < len(chs) and chs[cj][1] == 128 and chs[cj][0] + 128 <= n1:
                                cj += 1
                            nf = cj - ci
                            S.dma("sp", V[:, ci:ci + nf, 1:65],
                                  v_gath[base + lo - n0:base + lo - n0 + nf * 128, hh * 64:(hh + 1) * 64]
                                  .rearrange("(c p) d -> p c d", p=128), reads=[Bvg], writes=[BV])
                            ci = cj
                        else:
                            S.dma("sp", V[lo - cs0:hi - cs0, ci, 1:65],
                                  v_gath[base + lo - n0:base + hi - n0, hh * 64:(hh + 1) * 64],
                                  reads=[Bvg], writes=[BV])
                            if hi == cs0 + csz:
                                ci += 1
                            else:
                                break
                return KT, BK, V, BV

            ui = [0]

            def unit(s, hh, KT, BK, V, BV):
                T, L = seq_T[s], seg_len[s]
                chs = chunks(T)
                grp = groups(chs)
                ng = len(grp)
                qts = [(rr_ * LT + seg_off[s] + qq_, min(512, L - qq_)) for rr_ in range(VR) for qq_ in range(0, L, 512)]

                def loadQ(j):
                    qc, nq = qts[j]
                    QT, BQ = QT_r.next()
                    S.dma("sp", QT[0:96, 0:nq], q_scr[hh * 96:(hh + 1) * 96, qc:qc + nq], writes=[BQ])
                    return QT, BQ

                curq = loadQ(0)
                for j, (qc, nq) in enumerate(qts):
                    nxtq = loadQ(j + 1) if j + 1 < len(qts) else None
                    QT, BQ = curq
                    curq = nxtq
                    o, Bo = ob[ui[0] % 2]
                    bc, Bbc = bcb[ui[0] % 2]
                    ui[0] += 1
                    pts = {}
                    for g in range(ng + 2):
                        if g == 2 and pend[0] is not None:
                            pend[0]()
                            pend[0] = None
                        if g < ng:
                            k0, cnt = grp[g]
                            KS = chs[k0][1]
                            st, Bst = stg[g % 2]
                            for c in range(cnt):
                                K0 = chs[k0 + c][0]
                                mm(st[0:KS, c * 512:c * 512 + nq], KT[0:96, K0:K0 + KS], QT[0:96, 0:nq], True, True,
                                   [BK, BQ], [Bst])
                            pt, Bpt = PT_r.next()
                            if cnt == 2:
                                act(pt[0:KS, :, 0:nq], st[:, :].rearrange("p (c n) -> p c n", c=2)[0:KS, :, 0:nq],
                                    AF.Exp, [Bst], [Bpt], scale=SCALE)
                            else:
                                act(pt[0:KS, 0, 0:nq], st[0:KS, 0:nq], AF.Exp, [Bst], [Bpt], scale=SCALE)
                            pts[g] = (pt, Bpt, KS, k0, cnt)
                        if g >= 2:
                            pt, Bpt, KS, k0, cnt = pts.pop(g - 2)
                            for c in range(cnt):
                                kk = k0 + c
                                mm(o[0:65, 0:nq], V[0:KS, kk, 0:65], pt[0:KS, c, 0:nq], kk == 0, kk == len(chs) - 1,
                                   [BV, Bpt], [Bo])

                    def fin(o=o, Bo=Bo, bc=bc, Bbc=Bbc, nq=nq, qc=qc):
                        rec, Brec = rec_r.next()
                        recip(rec[0:1, 0:nq], o[0:1, 0:nq], [Bo], [Brec])
                        mm(bc[0:65, 0:nq], ones_f[:, 0:65], rec[:, 0:nq], True, True, [Bonesf, Brec], [Bbc])
                        bcs, Bbcs = bcs_r.next()
                        acp(bcs[0:65, 0:nq], bc[0:65, 0:nq], [Bbc], [Bbcs])
                        yo, Byo = yo_r.next()
                        tt("dve", yo[0:65, 0:nq], o[0:65, 0:nq], bcs[0:65, 0:nq], ALU.mult, [Bo, Bbcs], [Byo])
                        S.dma("sp", ya_scr[hh * 64:(hh + 1) * 64, qc:qc + nq], yo[1:65, 0:nq], reads=[Byo])
                    if pend[0] is not None:
                        pend[0]()
                    pend[0] = fin

            cur = loadKV(*units[0])
            for i, (s, hh) in enumerate(units):
                nxt = loadKV(*units[i + 1]) if i + 1 < len(units) else None
                unit(s, hh, *cur)
                cur = nxt
            if pend[0] is not None:
                pend[0]()
                pend[0] = None
            S.flush()

    def phaseB(l):
        with ExitStack() as es:
            LM = LP
            vec, Bvec = tile(es, "vecB", [128, NV], F32)
            lw, Blw = tile(es, "lwB", [128, 16, 128], F32)
            bd_b, Bbd = tile(es, "bdB", [128, 16, 128], BF16)
            coef, Bcoef = tile(es, "coefB", [128, 32], F32)
            carry, Bcarry = tile(es, "carryB", [128, 1], F32)
            S.dma("sp", vec[:, :], vecs[l, :, :], writes=[Bvec])
            S.dma("sp", lw[:, :, :], lruw[l].rearrange("f m p q -> p (f m) q"), writes=[Blw])
            cp("dve", bd_b[:, :, :], lw[:, :, :], [Blw], [Bbd])
            act(coef[:, 0:8], vec[:, 67:75], AF.Exp, [Bvec], [Bcoef], scale=-1.0)
            act(coef[:, 8:16], coef[:, 0:8], AF.Ln, [], [Bcoef], bias=1.0)
            ts("dve", coef[:, 16:24], coef[:, 8:16], -8.0, None, ALU.mult, None, [], [Bcoef])
            ts("dve", coef[:, 24:32], coef[:, 8:16], -16.0, None, ALU.mult, None, [], [Bcoef])
            U_r = Ring(nc, es, "UB", 2, [128, LM + 4], F32)
            G_r = Ring(nc, es, "GB", 2, [128, LM], F32)
            HF_r = Ring(nc, es, "HFB", 2, [128, LM], F32)
            H_r = Ring(nc, es, "HB", 2, [128, LM], F32)
            Y_r = Ring(nc, es, "YB", 2, [128, LM], F32)
            xc, Bxc = tile(es, "xcB", [128, LM], F32)
            xcb, Bxcb = tile(es, "xcbB", [128, LM], BF16)
            rr, Brr = tile(es, "rrB", [128, LM], F32)
            ig, Big = tile(es, "igB", [128, LM], F32)
            aa, Baa = tile(es, "aaB", [128, LM], F32)
            a2, Ba2 = tile(es, "a2B", [128, LM], F32)
            bb, Bbb = tile(es, "bbB", [128, LM], F32)
            BHF = {}

            def loads(s_, m, L, r, direction):
                o0 = seg_off[s_]
                U, BU = U_r.next()
                if r == 0:
                    mset("pool", U[:, 0:2], 0.0, [BU])
                if r == VR - 1:
                    mset("pool", U[:, 2 + L:4 + L], 0.0, [BU])
                S.dma("sp", U[:, 2:2 + L], ug_gath[r * D + m * 128:r * D + (m + 1) * 128, o0:o0 + L],
                      reads=[Bug], writes=[BU])
                if r > 0:
                    S.dma("sp", U[:, 0:2], ug_gath[(r - 1) * D + m * 128:(r - 1) * D + (m + 1) * 128, o0 + L - 2:o0 + L],
                          reads=[Bug], writes=[BU])
                if r < VR - 1:
                    S.dma("sp", U[:, 2 + L:4 + L], ug_gath[(r + 1) * D + m * 128:(r + 1) * D + (m + 1) * 128, o0:o0 + 2],
                          reads=[Bug], writes=[BU])
                Gt = BG = HF = BHFt = None
                if direction == 1:
                    Gt, BG = G_r.next()
                    S.dma("sp", Gt[:, 0:L], ug_gath[r * D + 512 + m * 128:r * D + 512 + (m + 1) * 128, o0:o0 + L],
                          reads=[Bug], writes=[BG])
                    HF, BHFt = HF_r.next()
                    S.dma("sp", HF[:, 0:L], hf_scr[:, r * L:(r + 1) * L], reads=[BHF[(s_, m, r)]], writes=[BHFt])
                return U, BU, Gt, BG, HF, BHFt

            def compute(s_, m, L, r, direction, U, BU, Gt, BG, HF, BHFt):
                o0 = seg_off[s_]
                zi = direction * 4 + m
                ts("dve", xc[:, 0:L], U[:, 0:L], vec[:, 31 + m * 4:32 + m * 4], vec[:, 47 + m:48 + m], ALU.mult, ALU.add,
                   [BU, Bvec], [Bxc])
                for tap in range(1, 4):
                    stt("dve", xc[:, 0:L], U[:, tap:tap + L], vec[:, 31 + m * 4 + tap:32 + m * 4 + tap], xc[:, 0:L],
                        ALU.mult, ALU.add, [BU, Bvec], [Bxc])
                cp("pool", xcb[:, 0:L], xc[:, 0:L], [Bxc], [Bxcb])
                for c0 in range(0, L, 512):
                    n = min(512, L - c0)
                    ps, Bps = pb()
                    mm(ps[:, 0:n], bd_b[:, zi, :], xcb[:, c0:c0 + n], True, True, [Bbd, Bxcb], [Bps])
                    act(rr[:, c0:c0 + n], ps[:, 0:n], AF.Sigmoid, [Bps, Bvec], [Brr], bias=vec[:, 51 + zi:52 + zi])
                    ps, Bps = pb()
                    mm(ps[:, 0:n], bd_b[:, 8 + zi, :], xcb[:, c0:c0 + n], True, True, [Bbd, Bxcb], [Bps])
                    act(ig[:, c0:c0 + n], ps[:, 0:n], AF.Sigmoid, [Bps, Bvec], [Big], bias=vec[:, 59 + zi:60 + zi])
                act(aa[:, 0:L], rr[:, 0:L], AF.Exp, [Brr, Bcoef], [Baa], scale=coef[:, 16 + zi:17 + zi])
                act(a2[:, 0:L], rr[:, 0:L], AF.Exp, [Brr, Bcoef], [Ba2], scale=coef[:, 24 + zi:25 + zi])
                ts("dve", a2[:, 0:L], a2[:, 0:L], -1.0, 1.0, ALU.mult, ALU.add, [], [Ba2])
                act(a2[:, 0:L], a2[:, 0:L], AF.Sqrt, [], [Ba2])
                tt("pool", bb[:, 0:L], a2[:, 0:L], ig[:, 0:L], ALU.mult, [Ba2, Big], [Bbb])
                tt("dve", bb[:, 0:L], bb[:, 0:L], xc[:, 0:L], ALU.mult, [Bxc], [Bbb])
                H, BH = H_r.next()
                if direction == 0:
                    S.op("dve", lambda e, o=H[:, 0:L], d0=aa[:, 0:L], d1=bb[:, 0:L]: e.tensor_tensor_scan(
                        out=o, data0=d0, data1=d1, initial=carry[:, 0:1], op0=ALU.mult, op1=ALU.add),
                        [Baa, Bbb, Bcarry], [BH])
                    cp("dve", carry[:, 0:1], H[:, L - 1:L], [BH], [Bcarry])
                    BHF[(s_, m, r)] = Buf(f"hf{s_}_{m}_{r}_{l}")
                    S.dma("sp", hf_scr[:, r * L:(r + 1) * L], H[:, 0:L], reads=[BH], writes=[BHF[(s_, m, r)]], owner=BH)
                else:
                    S.op("dve", lambda e, o=H[:, 0:L][:, ::-1], d0=aa[:, 0:L][:, ::-1], d1=bb[:, 0:L][:, ::-1]: e.tensor_tensor_scan(
                        out=o, data0=d0, data1=d1, initial=carry[:, 0:1], op0=ALU.mult, op1=ALU.add),
                        [Baa, Bbb, Bcarry], [BH])
                    cp("dve", carry[:, 0:1], H[:, 0:1], [BH], [Bcarry])
                    tt("pool", H[:, 0:L], H[:, 0:L], HF[:, 0:L], ALU.add, [BHFt], [BH])
                    act(Gt[:, 0:L], Gt[:, 0:L], AF.Gelu_apprx_tanh, [], [BG])
                    Y, BY = Y_r.next()
                    tt("dve", Y[:, 0:L], H[:, 0:L], Gt[:, 0:L], ALU.mult, [BH, BG], [BY])
                    S.dma("sp", y_gath[m * 128:(m + 1) * 128, r * LT + o0:r * LT + o0 + L], Y[:, 0:L], reads=[BY])

            work = []
            for s_ in range(2):
                L = seg_len[s_]
                for m in range(4):
                    for direction in (0, 1):
                        order = list(range(VR)) if direction == 0 else list(range(VR - 1, -1, -1))
                        for idx, r in enumerate(order):
                            work.append((s_, m, L, r, direction, idx == 0))
            ld = loads(*work[0][:5])
            for i, w in enumerate(work):
                nld = None
                pre = i + 1 < len(work) and not (work[i + 1][4] == 1 and w[4] == 0)
                if pre:
                    nld = loads(*work[i + 1][:5])
                if w[5]:
                    mset("dve", carry[:, 0:1], 0.0, [Bcarry])
                compute(*w[:5], *ld)
                if i + 1 < len(work) and not pre:
                    nld = loads(*work[i + 1][:5])
                ld = nld
            S.flush()

    def phaseD1(l, src):
        with ExitStack() as es:
            vec, Bvec = tile(es, "vecD", [128, NV], F32)
            wo_b, Bwo = tile(es, "wo_b", [128, 8, D], BF16)
            stg = Ring(nc, es, "wstgD", 2, [128, D], F32)
            S.dma("sp", vec[:, :], vecs[l, :, :], writes=[Bvec])
            for kc in range(8):
                st, Bst = stg.next()
                S.dma("sp", st[:, :], w_out[l, kc * 128:(kc + 1) * 128, :], writes=[Bst])
                ts("dve", wo_b[:, kc, :], st[:, :], vec[:, 15 + kc:16 + kc], None, ALU.mult, None, [Bst, Bvec], [Bwo])
            tiles = [(t0, min(512, NTOK - t0)) for t0 in range(0, NTOK, 512)]
            hT_r = Ring(nc, es, "hTD", 2, [128, 8, 512], F32)
            Y_r = Ring(nc, es, "YD", 2, [128, 8, 512], F32)
            sq_r = Ring(nc, es, "sqD", 2, [128, 8, 512], BF16)
            yb_r = Ring(nc, es, "ybD", 2, [128, 8, 512], BF16)
            f32_r = Ring(nc, es, "f32D", 4, [128, 512], F32)
            hsrc = src.rearrange("(kc p) t -> p kc t", p=128)
            hdst = hT_scr.rearrange("(kc p) t -> p kc t", p=128)
            ygv = y_gath.rearrange("(kc p) t -> p kc t", p=128)
            yav = ya_scr.rearrange("(kc p) t -> p kc t", p=128)
            def load(i):
                t0, N = tiles[i]
                h, Bh = hT_r.next()
                S.dma("sp", h[:, :, 0:N], hsrc[:, :, t0:t0 + N], writes=[Bh])
                Y, BY = Y_r.next()
                S.dma("sp", Y[:, 0:4, 0:N], ygv[:, :, t0:t0 + N], reads=[Byg], writes=[BY])
                S.dma("sp", Y[:, 4:8, 0:N], yav[:, :, t0:t0 + N], writes=[BY])
                return h, Bh, Y, BY

            def compute(i, h, Bh, Y, BY):
                t0, N = tiles[i]
                sq, Bsq = sq_r.next()
                yb, Byb = yb_r.next()
                for kc in range(8):
                    act(sq[:, kc, 0:N], Y[:, kc, 0:N], AF.Square, [BY], [Bsq])
                Rs = []
                for grp in range(2):
                    ps, Bps = pb()
                    for kc in range(4):
                        mm(ps[:, 0:N], ones_b[:, :], sq[:, grp * 4 + kc, 0:N], kc == 0, kc == 3, [Bones, Bsq], [Bps])
                    tmp, Btmp = f32_r.next()
                    R, BR = f32_r.next()
                    rstd_from(ps[:, 0:N], 1.0 / 512.0, EPS, tmp[:, 0:N], R[:, 0:N], Bps, Btmp, BR)
                    Rs.append((R, BR))
                for kc in range(8):
                    R, BR = Rs[kc // 4]
                    tt("dve" if kc % 2 == 0 else "pool", yb[:, kc, 0:N], Y[:, kc, 0:N], R[:, 0:N], ALU.mult,
                       [BY, BR], [Byb])
                for oc in range(8):
                    ps, Bps = pb()
                    for kc in range(8):
                        mm(ps[:, 0:N], wo_b[:, kc, oc * 128:(oc + 1) * 128], yb[:, kc, 0:N], kc == 0, kc == 7,
                           [Bwo, Byb], [Bps])
                    tt("dve", h[:, oc, 0:N], h[:, oc, 0:N], ps[:, 0:N], ALU.add, [Bps], [Bh])
                S.dma("sp", hdst[:, :, t0:t0 + N], h[:, :, 0:N], reads=[Bh])

            cur = load(0)
            for i in range(len(tiles)):
                nxt = load(i + 1) if i + 1 < len(tiles) else None
                compute(i, *cur)
                cur = nxt
            S.flush()

    def phaseD2(l, dst):
        with ExitStack() as es:
            NT = 256
            vec, Bvec = tile(es, "vecE", [128, NV], F32)
            wu_b, Bwu = tile(es, "wu_b", [128, 8, 4 * D], BF16)
            wd_b, Bwd = tile(es, "wd_b", [128, 32, D], BF16)
            S.dma("sp", vec[:, :], vecs[l, :, :], writes=[Bvec])
            with ExitStack() as es2:
                stg = Ring(nc, es2, "wstgE", 2, [128, 4 * D], F32)
                for kc in range(8):
                    st, Bst = stg.next()
                    S.dma("sp", st[:, :], w_up[l, kc * 128:(kc + 1) * 128, :], writes=[Bst])
                    ts("dve" if kc % 2 == 0 else "pool", wu_b[:, kc, :], st[:, :], vec[:, 23 + kc:24 + kc], None,
                       ALU.mult, None, [Bst, Bvec], [Bwu])
                for f4 in range(8):
                    st, Bst = stg.next()
                    S.dma("sp", st[:, :].rearrange("p (f n) -> p f n", n=D),
                          w_down[l, f4 * 512:(f4 + 1) * 512, :].rearrange("(f p) n -> p f n", p=128), writes=[Bst])
                    cp("dve" if f4 % 2 == 0 else "pool", wd_b[:, f4 * 4:(f4 + 1) * 4, :],
                       st[:, :].rearrange("p (f n) -> p f n", n=D), [Bst], [Bwd])
                S.flush()
            tiles = [(t0, min(NT, NTOK - t0)) for t0 in range(0, NTOK, NT)]
            hT_r = Ring(nc, es, "hTE", 2, [128, 8, NT], F32)
            sq_r = Ring(nc, es, "sqE", 2, [128, 8, NT], BF16)
            hb_r = Ring(nc, es, "hbE", 2, [128, 8, NT], BF16)
            ac_r = Ring(nc, es, "acE", 2, [128, 32, NT], BF16)
            f32_r = Ring(nc, es, "f32E", 6, [128, NT], F32)
            hsrc = hT_scr.rearrange("(kc p) t -> p kc t", p=128)
            hdst = dst.rearrange("(kc p) t -> p kc t", p=128)

            def load(i):
                t0, N = tiles[i]
                h, Bh = hT_r.next()
                S.dma("sp", h[:, :, 0:N], hsrc[:, :, t0:t0 + N], writes=[Bh])
                return h, Bh

            def compute(i, h, Bh):
                t0, N = tiles[i]
                sq, Bsq = sq_r.next()
                hb, Bhb = hb_r.next()
                for kc in range(8):
                    act(sq[:, kc, 0:N], h[:, kc, 0:N], AF.Square, [Bh], [Bsq])
                ps, Bps = pb()
                for kc in range(8):
                    mm(ps[:, 0:N], ones_b[:, :], sq[:, kc, 0:N], kc == 0, kc == 7, [Bones, Bsq], [Bps])
                tmp, Btmp = f32_r.next()
                R, BR = f32_r.next()
                rstd_from(ps[:, 0:N], 1.0 / D, EPS, tmp[:, 0:N], R[:, 0:N], Bps, Btmp, BR)
                for kc in range(8):
                    tt("dve" if kc % 2 == 0 else "pool", hb[:, kc, 0:N], h[:, kc, 0:N], R[:, 0:N], ALU.mult,
                       [Bh, BR], [Bhb])
                ac, Bac = ac_r.next()
                for fc in range(32):
                    ps, Bps = pb()
                    for kc in range(8):
                        mm(ps[:, 0:N], wu_b[:, kc, fc * 128:(fc + 1) * 128], hb[:, kc, 0:N], kc == 0, kc == 7,
                           [Bwu, Bhb], [Bps])
                    rl, Brl = f32_r.next()
                    act(rl[:, 0:N], ps[:, 0:N], AF.Relu, [Bps], [Brl])
                    tt("dve" if fc % 2 == 0 else "pool", ac[:, fc, 0:N], rl[:, 0:N], rl[:, 0:N], ALU.mult, [Brl], [Bac])
                for oc in range(8):
                    ps, Bps = pb()
                    for fc in range(32):
                        mm(ps[:, 0:N], wd_b[:, fc, oc * 128:(oc + 1) * 128], ac[:, fc, 0:N], fc == 0, fc == 31,
                           [Bwd, Bac], [Bps])
                    tt("dve", h[:, oc, 0:N], h[:, oc, 0:N], ps[:, 0:N], ALU.add, [Bps], [Bh])
                S.dma("sp", hdst[:, :, t0:t0 + N], h[:, :, 0:N], reads=[Bh])

            cur = load(0)
            for i in range(len(tiles)):
                nxt = load(i + 1) if i + 1 < len(tiles) else None
                compute(i, *cur)
                cur = nxt
            S.flush()

    for l in range(NLAYER):
        src = xT if l == 0 else hT_scr
        if "A" in PHASES:
            phaseA(l, src)
        if "C" in PHASES:
            phaseC(l)
        if "B" in PHASES:
            phaseB(l)
        if "D" in PHASES:
            phaseD1(l, src)
        if "E" in PHASES:
            phaseD2(l, yT if l == NLAYER - 1 else hT_scr)
    S.flush()
    G.close()
    return nc, dict(LT=LT, NTOK=NTOK, seg_len=seg_len, seg_off=seg_off, seq_T=seq_T, LP=LP, LS=LS, TP=TP, TS=TS)


def prep(inp, SEQ, DSEQ, meta):
    LT, NTOK, seg_len, seg_off = meta["LT"], meta["NTOK"], meta["seg_len"], meta["seg_off"]
    f = lambda k: np.asarray(inp[k], dtype=np.float32)
    xp, xs, mt = f("x_prompt"), f("x_sample"), f("meta_tokens")
    nP, nS = xp.shape[0], xs.shape[0]
    inv_freq = (1.0 / (10000.0 ** (np.arange(0, 32, 2, dtype=np.float32) / np.float32(32)))).astype(np.float32)
    qg, kg = f("qk_q_g"), f("qk_k_g")

    def swp(g):
        o = np.zeros(96, np.float32)
        o[64:80] = g[80:96]
        o[80:96] = g[64:80]
        return o

    vecs = np.zeros((2, 128, NV), np.float32)
    lruw = np.zeros((2, 4, 4, 128, 128), np.float32)
    for l in range(2):
        vecs[l, :, 0:8] = f("norm_mix_g")[l].reshape(8, 128).T
        vecs[l, :, 8:10] = f("q_norm_g")[l].reshape(2, 128).T
        vecs[l, :, 10] = f("kv_norm_g")[l]
        vecs[l, 0:96, 11] = qg[l]
        vecs[l, 0:96, 12] = swp(qg[l])
        vecs[l, 0:96, 13] = kg[l]
        vecs[l, 0:96, 14] = swp(kg[l])
        vecs[l, :, 15:19] = f("out_norm_lru_g")[l].reshape(4, 128).T
        vecs[l, :, 19:23] = f("out_norm_attn_g")[l].reshape(4, 128).T
        vecs[l, :, 23:31] = f("norm_ff_g")[l].reshape(8, 128).T
        for m in range(4):
            ch = slice(m * 128, (m + 1) * 128)
            vecs[l, :, 31 + m * 4:35 + m * 4] = f("conv_w")[l][:, ch].T
            vecs[l, :, 47 + m] = f("conv_b")[l][ch]
            for z in range(2):
                vecs[l, :, 51 + z * 4 + m] = f("lru_ba")[l, z][ch]
                vecs[l, :, 59 + z * 4 + m] = f("lru_bx")[l, z][ch]
                vecs[l, :, 67 + z * 4 + m] = f("lru_lambda")[l, z][ch]
                for half in range(2):
                    hs = slice(half * 64, (half + 1) * 64)
                    lruw[l, z, m, hs, hs] = f("lru_wa")[l, z, 2 * m + half]
                    lruw[l, 2 + z, m, hs, hs] = f("lru_wx")[l, z, 2 * m + half]
    shared = {"vecs": vecs, "lruw": lruw, "w_in": f("w_in"), "w_uq": f("w_uq"), "w_ukv": f("w_ukv"),
              "w_out": f("w_out"), "w_up": f("w_up"), "w_down": f("w_down")}
    in_maps = []
    cache = {}
    for c in range(NCORE):
        key = (c % nP, c % nS)
        if key not in cache:
            seqs = [np.concatenate([mt, xp[key[0]]], 0), np.concatenate([mt, xs[key[1]]], 0)]
            xT = np.empty((D, NTOK), np.float32)
            pos = np.empty(NTOK, np.float32)
            for r in range(VR):
                for s in range(2):
                    L = seg_len[s]
                    c0 = r * LT + seg_off[s]
                    xT[:, c0:c0 + L] = seqs[s][r * L:(r + 1) * L].T
                    pos[c0:c0 + L] = np.arange(r * L, (r + 1) * L, dtype=np.float32)
            ang = pos[None, :] * inv_freq[:, None]
            cs = np.empty((2, 32, NTOK), np.float32)
            cs[0, 0:16] = np.cos(ang)
            cs[0, 16:32] = np.cos(ang)
            cs[1, 0:16] = np.sin(ang)
            cs[1, 16:32] = np.sin(ang)
            cache[key] = (xT, cs)
        xT, cs = cache[key]
        m = {"xT": xT, "cs": cs}
        m.update(shared)
        in_maps.append(m)
    return in_maps


_CACHE = {}


def run(inp, SEQ, DSEQ, trace=False):
    key = (SEQ, DSEQ)
    if key not in _CACHE:
        _CACHE[key] = build(SEQ, DSEQ)
    nc, meta = _CACHE[key]
    in_maps = prep(inp, SEQ, DSEQ, meta)
    res = run_bass_kernel_spmd(nc, in_maps, core_ids=list(range(NCORE)), **({"trace": True} if trace else {}))
    B, DB = inp["x_prompt"].shape[0], inp["x_sample"].shape[0]
    TP, TS, LT, LP, LS = meta["TP"], meta["TS"], meta["LT"], meta["LP"], meta["LS"]
    yp = np.empty((B, TP, D), np.float32)
    ys = np.empty((DB, TS, D), np.float32)
    for b in range(B):
        yT = np.asarray(res.results[b]["yT"])
        for r in range(VR):
            yp[b, r * LP:(r + 1) * LP] = yT[:, r * LT:r * LT + LP].T
    for b in range(DB):
        yT = np.asarray(res.results[b]["yT"])
        for r in range(VR):
            ys[b, r * LS:(r + 1) * LS] = yT[:, r * LT + LP:r * LT + LP + LS].T
    return (np.ascontiguousarray(yp[:, 16:]), np.ascontiguousarray(ys[:, 16:])), res


def kernel(**inputs):
    SEQ = inputs["x_prompt"].shape[1]
    DSEQ = inputs["x_sample"].shape[1]
    out, _ = run(inputs, SEQ, DSEQ)
    return out
```

```python
import numpy as np
from contextlib import ExitStack
import concourse.bass as bass
import concourse.mybir as mybir
from concourse.bass_utils import run_bass_kernel_spmd

F32 = mybir.dt.float32
BF16 = mybir.dt.bfloat16
AF = mybir.ActivationFunctionType
ALU = mybir.AluOpType

D = 1024
NCORE = 8
VR = 1
BR = 8
NV = 75
EPS = 1e-6
IN_W = 1440
SCALE = 96 ** -0.5
PHASES = "AGCBDE"
NLAYER = 2
DBG = 99
VAR = 0


class Buf:
    __slots__ = ("name", "w", "r", "dsem")

    def __init__(self, name):
        self.name = name
        self.w = None
        self.r = []
        self.dsem = None


class Eng:
    def __init__(self, name, sem):
        self.name = name
        self.sem = sem
        self.cnt = 0
        self.known = {}
        self.insts = []


class Sched:
    def __init__(self, nc):
        self.nc = nc
        self.sems = {}
        self.E = {}
        for name in ("pe", "act", "dve", "pool", "sp"):
            self.sems[name] = nc.alloc_semaphore(name="sem_" + name)
            self.E[name] = Eng(name, name)
        self.dval = {}

    def _waits(self, eng, reads, writes):
        need = {}
        for b in reads:
            if b.w is not None:
                k, v = b.w
                if need.get(k, 0) < v:
                    need[k] = v
        for b in writes:
            if b.w is not None:
                k, v = b.w
                if need.get(k, 0) < v:
                    need[k] = v
            for (k, v) in b.r:
                if need.get(k, 0) < v:
                    need[k] = v
        out = []
        for k, v in need.items():
            if eng.name == "pe" and k == "pe":
                continue
            if eng.known.get(k, 0) >= v:
                continue
            eng.known[k] = v
            out.append((k, v))
        return out

    def _mark(self, tag, reads, writes):
        for b in reads:
            if len(b.r) > 64:
                m = {}
                for (k, v) in b.r:
                    if m.get(k, 0) < v:
                        m[k] = v
                b.r = list(m.items())
            b.r.append(tag)
        for b in writes:
            b.w = tag
            b.r = []

    def op(self, en, fn, reads=(), writes=()):
        eng = self.E[en]
        waits = self._waits(eng, reads, writes)
        eng.cnt += 1
        tag = (eng.sem, eng.cnt)
        eng.insts.append((waits, fn, (eng.sem, 1)))
        self._mark(tag, reads, writes)
        return tag

    def _own(self, owner):
        if owner.dsem is None:
            key = "d_" + owner.name
            if key not in self.sems:
                self.sems[key] = self.nc.alloc_semaphore(name=key)
                self.dval[key] = 0
            owner.dsem = key

    def dma(self, en, out, in_, reads=(), writes=(), owner=None, slow=False):
        eng = self.E[en]
        waits = self._waits(eng, reads, writes)
        if owner is None:
            owner = writes[0] if writes else reads[0]
        self._own(owner)
        self.dval[owner.dsem] += 16
        tag = (owner.dsem, self.dval[owner.dsem])
        kw = {"allow_slow_non_contiguous": True} if slow else {}
        def _f(e, o=out, i=in_):
            try:
                return e.dma_start(out=o, in_=i, **kw)
            except Exception:
                print("DMA FAIL", en, o, i)
                raise
        eng.insts.append((waits, _f, (owner.dsem, 16)))
        self._mark(tag, reads, writes)
        return tag

    def coll(self, ins, outs, reads, writes, owner):
        eng = self.E["pool"]
        waits = self._waits(eng, reads, writes)
        self._own(owner)
        self.dval[owner.dsem] += 1
        tag = (owner.dsem, self.dval[owner.dsem])
        rg = [list(range(NCORE))]
        eng.insts.append((waits, (lambda e: e.collective_compute("AllGather", ALU.bypass, replica_groups=rg,
                                                                 ins=ins, outs=outs)), (owner.dsem, None)))
        self._mark(tag, reads, writes)
        return tag

    def flush(self):
        eng = self.E["sp"]
        dr = []
        for key, val in self.dval.items():
            if val > 0 and eng.known.get(key, 0) < val:
                eng.known[key] = val
                dr.append((key, val))
        if dr:
            eng.insts.append((dr, None, None))
        sems = self.sems
        with self.nc.Block() as block:
            for name, attr in (("pe", "tensor"), ("act", "scalar"), ("dve", "vector"),
                               ("pool", "gpsimd"), ("sp", "sync")):
                e = self.E[name]
                if not e.insts:
                    continue

                def body(be, insts=e.insts):
                    for waits, fn, inc in insts:
                        for (k, v) in waits:
                            be.wait_ge(sems[k], v)
                        if fn is None:
                            continue
                        ins = fn(be)
                        if inc is not None:
                            if inc[1] is None:
                                ins.then_inc(sems[inc[0]])
                            else:
                                ins.then_inc(sems[inc[0]], inc[1])
                getattr(block, attr)(body)
                e.insts = []


_UID = [0]


def _uname(name):
    _UID[0] += 1
    return f"{name}_u{_UID[0]}"


class Ring:
    def __init__(self, nc, es, name, n, shape, dtype):
        self.items = []
        for i in range(n):
            t = es.enter_context(nc.sbuf_tensor(_uname(f"{name}{i}"), shape, dtype))
            self.items.append((t, Buf(f"{name}{i}")))
        self.i = 0

    def next(self):
        it = self.items[self.i % len(self.items)]
        self.i += 1
        return it


def build(SEQ, DSEQ):
    TP, TS = SEQ + 16, DSEQ + 16
    LP, LS = TP // VR, TS // VR
    assert LP * VR == TP and LS * VR == TS and TP % BR == 0 and TS % BR == 0 and VR == 1
    seg_len = [LP, LS]
    seq_T = [TP, TS]
    seg_off = [0, LP]
    LT = LP + LS
    NTOK = VR * LT

    nc = bass.Bass("TRN2", target_bir_lowering=False)

    def din(name, shape, dtype=F32):
        return nc.dram_tensor(name, shape, dtype, kind="ExternalInput").ap()

    xT = din("xT", [D, NTOK])
    cs = din("cs", [2, 32, NTOK])
    vecs = din("vecs", [2, 128, NV])
    lruw = din("lruw", [2, 4, 4, 128, 128])
    w_in = din("w_in", [2, D, IN_W])
    w_uq = din("w_uq", [2, 256, 768])
    w_ukv = din("w_ukv", [2, 128, 1024])
    w_out = din("w_out", [2, D, D])
    w_up = din("w_up", [2, D, 4 * D])
    w_down = din("w_down", [2, 4 * D, D])
    yT = nc.dram_tensor("yT", [D, NTOK], F32, kind="ExternalOutput").ap()

    def dscr(name, shape, dtype=F32):
        return nc.dram_tensor(name, shape, dtype, kind="Internal")

    hT_scr = dscr("hT_scr", [D, NTOK]).ap()
    ug_gath = dscr("ug_gath", [VR * D, LT]).ap()
    k_gath = dscr("k_gath", [VR * 768, LT], BF16).ap()
    v_gath = dscr("v_gath", [VR * LT, 512], BF16).ap()
    q_scr = dscr("q_scr", [768, NTOK], BF16).ap()
    hf_scr = dscr("hf_scr", [128, TP]).ap()
    y_gath = dscr("y_gath", [512, VR * LT]).ap()
    ya_scr = dscr("ya_scr", [512, NTOK]).ap()
    Bug, Bkg, Bvg, Byg = Buf("ug_gath"), Buf("k_gath"), Buf("v_gath"), Buf("y_gath")

    S = Sched(nc)
    G = ExitStack()

    def tile(es, name, shape, dtype):
        t = es.enter_context(nc.sbuf_tensor(_uname(name), shape, dtype))
        return t, Buf(name)

    PS = []
    PSP = []
    for i in range(4):
        t = G.enter_context(nc.psum_tensor(f"psp{i}", [128, 1024], F32))
        PSP.append((t, Buf(f"psp{i}")))
        PS.append((t[:, 0:512], Buf(f"ps{2 * i}")))
        PS.append((t[:, 512:1024], Buf(f"ps{2 * i + 1}")))
    pbi = [0]

    def pb():
        it = PS[pbi[0] % 8]
        pbi[0] += 1
        return it

    ones_b, Bones = tile(G, "ones_b", [128, 128], BF16)
    onesR_b, BonesR = tile(G, "onesR_b", [96, 96], BF16)
    ones_f, Bonesf = tile(G, "ones_f", [128, 128], F32)
    S.op("pool", lambda e: e.memset(ones_b[:, :], 1.0), writes=[Bones])
    S.op("pool", lambda e: e.memset(onesR_b[:, :], 0.0), writes=[BonesR])
    S.op("pool", lambda e: e.memset(onesR_b[64:96, :], 1.0), writes=[BonesR])
    S.op("pool", lambda e: e.memset(ones_f[:, :], 0.0), writes=[Bonesf])
    S.op("pool", lambda e: e.memset(ones_f[0:1, :], 1.0), writes=[Bonesf])

    def mm(out, lhsT, rhs, start, stop, reads, writes):
        S.op("pe", lambda e: e.matmul(out, lhsT=lhsT, rhs=rhs, start=start, stop=stop), reads, writes)

    def act(out, in_, func, reads, writes, bias=None, scale=None):
        kw = {}
        if bias is not None:
            kw["bias"] = bias
        if scale is not None:
            kw["scale"] = scale
        S.op("act", lambda e: e.activation(out=out, in_=in_, func=func, **kw), reads, writes)

    def tt(en, out, in0, in1, op, reads, writes):
        S.op(en, lambda e: e.tensor_tensor(out=out, in0=in0, in1=in1, op=op), reads, writes)

    def ts(en, out, in0, s1, s2, op0, op1, reads, writes):
        if s2 is None:
            S.op(en, lambda e: e.tensor_scalar(out=out, in0=in0, scalar1=s1, scalar2=None, op0=op0), reads, writes)
        else:
            S.op(en, lambda e: e.tensor_scalar(out=out, in0=in0, scalar1=s1, scalar2=s2, op0=op0, op1=op1),
                 reads, writes)

    def stt(en, out, in0, scalar, in1, op0, op1, reads, writes):
        S.op(en, lambda e: e.scalar_tensor_tensor(out=out, in0=in0, scalar=scalar, in1=in1, op0=op0, op1=op1),
             reads, writes)

    def cp(en, out, in_, reads, writes):
        S.op(en, lambda e: e.tensor_copy(out=out, in_=in_), reads, writes)

    def acp(out, in_, reads, writes):
        S.op("act", lambda e: e.copy(out=out, in_=in_), reads, writes)

    def recip(out, in_, reads, writes):
        S.op("dve", lambda e: e.reciprocal(out=out, in_=in_), reads, writes)

    def mset(en, ap, val, writes):
        S.op(en, lambda e: e.memset(ap, val), (), writes)

    def rstd_from(ps_ap, scale, bias_c, tmp_ap, out_ap, Bps, Btmp, Bout):
        act(tmp_ap, ps_ap, AF.Sqrt, [Bps], [Btmp], bias=bias_c, scale=scale)
        recip(out_ap, tmp_ap, [Btmp], [Bout])

    def phaseA(l, src):
        with ExitStack() as es:
            win_b, Bwin = tile(es, "win_b", [128, 8, 1632], BF16)
            wq_b, Bwq = tile(es, "wq_b", [128, 2, 768], BF16)
            wqr_b, Bwqr = tile(es, "wqr_b", [128, 2, 768], BF16)
            wkv_b, Bwkv = tile(es, "wkv_b", [128, 1024], BF16)
            wv_b, Bwv = tile(es, "wv_b", [128, 512], BF16)
            vec, Bvec = tile(es, "vecA", [128, NV], F32)
            stg = Ring(nc, es, "wstgA", 2, [128, IN_W], F32)
            S.dma("sp", vec[:, :], vecs[l, :, :], writes=[Bvec])
            mset("pool", win_b[:, :, 1440:1632], 0.0, [Bwin])
            mset("pool", wqr_b[:, :, :], 0.0, [Bwqr])
            for kc in range(8):
                st, Bst = stg.next()
                S.dma("sp", st[:, :], w_in[l, kc * 128:(kc + 1) * 128, :], writes=[Bst])
                g = vec[:, kc:kc + 1]
                ts("dve", win_b[:, kc, 0:1440], st[:, :], g, None, ALU.mult, None, [Bst, Bvec], [Bwin])
                ts("dve", win_b[:, kc, 1504:1536], st[:, 1408:1440], g, None, ALU.mult, None, [Bst, Bvec], [Bwin])
                ts("dve", win_b[:, kc, 1600:1616], st[:, 1424:1440], g, -1.0, ALU.mult, ALU.mult, [Bst, Bvec], [Bwin])
                ts("dve", win_b[:, kc, 1616:1632], st[:, 1408:1424], g, None, ALU.mult, None, [Bst, Bvec], [Bwin])
            for j in range(2):
                st, Bst = stg.next()
                S.dma("sp", st[:, 0:768], w_uq[l, j * 128:(j + 1) * 128, :], writes=[Bst])
                g = vec[:, 8 + j:9 + j]
                ts("dve", wq_b[:, j, :], st[:, 0:768], g, None, ALU.mult, None, [Bst, Bvec], [Bwq])
                sv = st[:, 0:768].rearrange("p (h c) -> p h c", c=96)
                rv = wqr_b[:, j, :].rearrange("p (h c) -> p h c", c=96)
                ts("dve", rv[:, :, 64:80], sv[:, :, 80:96], g, -1.0, ALU.mult, ALU.mult, [Bst, Bvec], [Bwqr])
                ts("dve", rv[:, :, 80:96], sv[:, :, 64:80], g, None, ALU.mult, None, [Bst, Bvec], [Bwqr])
            st, Bst = stg.next()
            S.dma("sp", st[:, 0:1024], w_ukv[l, :, :], writes=[Bst])
            ts("dve", wkv_b[:, :], st[:, 0:1024], vec[:, 10:11], None, ALU.mult, None, [Bst, Bvec], [Bwkv])
            cp("pool", wv_b[:, :].rearrange("p (h c) -> p h c", c=64),
               wkv_b[:, :].rearrange("p (h c) -> p h c", c=128)[:, :, 64:128], [Bwkv], [Bwv])

            tiles = [(r, t0, min(512, LT - t0)) for r in range(VR) for t0 in range(0, LT, 512)]
            hT_r = Ring(nc, es, "hTA", 2, [128, 8, 512], F32)
            cs_r = Ring(nc, es, "csA", 2, [96, 2, 512], F32)
            sq_r = Ring(nc, es, "sqA", 2, [128, 8, 512], BF16)
            hb_r = Ring(nc, es, "hbA", 2, [128, 8, 512], BF16)
            f32_r = Ring(nc, es, "f32A", 6, [128, 512], F32)
            stgo_r = Ring(nc, es, "stgoA", 3, [128, 512], F32)
            b16_r = Ring(nc, es, "b16A", 6, [128, 512], BF16)
            qo_r = Ring(nc, es, "qoA", 3, [96, 512], BF16)
            ko_r = Ring(nc, es, "koA", 3, [96, 512], BF16)
            vo_r = Ring(nc, es, "voA", 2, [128, 512], BF16)
            cqb_r = Ring(nc, es, "cqbA", 2, [128, 2, 512], BF16)
            sqq_r = Ring(nc, es, "sqqA", 2, [128, 2, 512], BF16)
            ckvb_r = Ring(nc, es, "ckvbA", 2, [128, 512], BF16)
            sqkv_r = Ring(nc, es, "sqkvA", 2, [128, 512], BF16)
            per_r = Ring(nc, es, "perA", 10, [96, 512], F32)
            sv_r = Ring(nc, es, "svA", 4, [128, 2], F32)
            qf_r = Ring(nc, es, "qfA", 3, [96, 512], F32)
            sx_r = Ring(nc, es, "sxA", 2, [128, 512], BF16)
            for (sxt, Bsxt) in sx_r.items:
                mset("pool", sxt[:, :], 0.0, [Bsxt])
            hsrc = src.rearrange("(kc p) t -> p kc t", p=128)
            csrc = cs.rearrange("two r t -> r two t")

            def load(i):
                r, t0, N = tiles[i]
                g0 = r * LT + t0
                h, Bh = hT_r.next()
                S.dma("sp", h[:, :, 0:N], hsrc[:, :, g0:g0 + N], writes=[Bh])
                c, Bc = cs_r.next()
                S.dma("sp", c[64:96, :, 0:N], csrc[:, :, g0:g0 + N], writes=[Bc])
                return h, Bh, c, Bc

            def compute(i, h, Bh, c, Bc):
                r, t0, N = tiles[i]
                g0 = r * LT + t0
                if DBG < 1:
                    return
                sq, Bsq = sq_r.next()
                hb, Bhb = hb_r.next()
                for kc in range(8):
                    act(sq[:, kc, 0:N], h[:, kc, 0:N], AF.Square, [Bh], [Bsq])
                ps, Bps = pb()
                for kc in range(8):
                    mm(ps[:, 0:N], ones_b[:, :], sq[:, kc, 0:N], kc == 0, kc == 7, [Bones, Bsq], [Bps])
                tmp, Btmp = f32_r.next()
                R, BR = f32_r.next()
                rstd_from(ps[:, 0:N], 1.0 / D, EPS, tmp[:, 0:N], R[:, 0:N], Bps, Btmp, BR)
                for kc in range(8):
                    tt("dve", hb[:, kc, 0:N], h[:, kc, 0:N], R[:, 0:N], ALU.mult, [Bh, BR], [Bhb])
                for oc in range(8):
                    ps, Bps = pb()
                    for kc in range(8):
                        mm(ps[:, 0:N], win_b[:, kc, oc * 128:(oc + 1) * 128], hb[:, kc, 0:N], kc == 0, kc == 7,
                           [Bwin, Bhb], [Bps])
                    so, Bso = stgo_r.next()
                    if oc % 2 == 0:
                        S.op("act", lambda e, o=so[:, 0:N], p=ps[:, 0:N]: e.copy(out=o, in_=p), [Bps], [Bso])
                    else:
                        cp("dve", so[:, 0:N], ps[:, 0:N], [Bps], [Bso])
                    S.dma("sp", ug_gath[r * D + oc * 128:r * D + (oc + 1) * 128, t0:t0 + N], so[:, 0:N], reads=[Bso])
                if DBG < 2:
                    return
                cqb, Bcqb = cqb_r.next()
                sqq, Bsqq = sqq_r.next()
                for j in range(2):
                    ps, Bps = pb()
                    for kc in range(8):
                        mm(ps[:, 0:N], win_b[:, kc, 1024 + j * 128:1152 + j * 128], hb[:, kc, 0:N], kc == 0, kc == 7,
                           [Bwin, Bhb], [Bps])
                    act(sqq[:, j, 0:N], ps[:, 0:N], AF.Square, [Bps], [Bsqq])
                    acp(cqb[:, j, 0:N], ps[:, 0:N], [Bps], [Bcqb])
                ckvb, Bckvb = ckvb_r.next()
                sqkv, Bsqkv = sqkv_r.next()
                ps, Bps = pb()
                for kc in range(8):
                    mm(ps[:, 0:N], win_b[:, kc, 1280:1408], hb[:, kc, 0:N], kc == 0, kc == 7, [Bwin, Bhb], [Bps])
                act(sqkv[:, 0:N], ps[:, 0:N], AF.Square, [Bps], [Bsqkv])
                acp(ckvb[:, 0:N], ps[:, 0:N], [Bps], [Bckvb])
                if DBG == 2:
                    return
                kr, Bkr = per_r.next()
                krot, Bkrot = per_r.next()
                sqr, Bsqr = b16_r.next()
                ps, Bps = pb()
                for kc in range(8):
                    mm(ps[0:96, 0:N], win_b[:, kc, 1440:1536], hb[:, kc, 0:N], kc == 0, kc == 7, [Bwin, Bhb], [Bps])
                act(sqr[0:96, 0:N], ps[0:96, 0:N], AF.Square, [Bps], [Bsqr])
                acp(kr[64:96, 0:N], ps[64:96, 0:N], [Bps], [Bkr])
                ps, Bps = pb()
                for kc in range(8):
                    mm(ps[0:96, 0:N], win_b[:, kc, 1536:1632], hb[:, kc, 0:N], kc == 0, kc == 7, [Bwin, Bhb], [Bps])
                cp("dve", krot[64:96, 0:N], ps[64:96, 0:N], [Bps], [Bkrot])
                if DBG < 3:
                    return
                Cq, BCq = per_r.next()
                ps, Bps = pb()
                for j in range(2):
                    mm(ps[0:96, 0:N], ones_b[:, 0:96], sqq[:, j, 0:N], j == 0, j == 1, [Bones, Bsqq], [Bps])
                ts("dve", Cq[:, 0:N], ps[0:96, 0:N], 96.0 * EPS / 256.0, 96.0 * EPS * EPS, ALU.mult, ALU.add,
                   [Bps], [BCq])
                S2, BS2 = per_r.next()
                skv, Bskv = per_r.next()
                MR, BMR = per_r.next()
                KRr, BKRr = per_r.next()
                tA, BtA = per_r.next()
                ps, Bps = pb()
                mm(ps[0:96, 0:N], ones_b[:, 0:96], sqkv[:, 0:N], True, True, [Bones, Bsqkv], [Bps])
                ts("dve", tA[:, 0:N], ps[0:96, 0:N], 1.0 / 128.0, EPS, ALU.mult, ALU.add, [Bps], [BtA])
                recip(S2[:, 0:N], tA[:, 0:N], [BtA], [BS2])
                act(skv[:, 0:N], S2[:, 0:N], AF.Sqrt, [BS2], [Bskv])
                ps, Bps = pb()
                mm(ps[0:96, 0:N], onesR_b[:, :], sqr[0:96, 0:N], True, True, [BonesR, Bsqr], [Bps])
                ts("dve", MR[:, 0:N], ps[0:96, 0:N], 96.0 * EPS, None, ALU.add, None, [Bps], [BMR])
                t1, Bt1 = per_r.next()
                t2, Bt2 = per_r.next()
                stt("dve", t1[64:96, 0:N], kr[64:96, 0:N], vec[64:96, 13:14], c[64:96, 0, 0:N], ALU.mult, ALU.mult,
                    [Bkr, Bvec, Bc], [Bt1])
                stt("dve", t2[64:96, 0:N], krot[64:96, 0:N], vec[64:96, 14:15], c[64:96, 1, 0:N], ALU.mult, ALU.mult,
                    [Bkrot, Bvec, Bc], [Bt2])
                tt("pool", KRr[64:96, 0:N], t1[64:96, 0:N], t2[64:96, 0:N], ALU.add, [Bt1, Bt2], [BKRr])
                if DBG < 4:
                    return
                for hh in range(8 if DBG != 5 else 0):
                    pq, Bpq = pb()
                    for j in range(2):
                        mm(pq[0:96, 0:N], wq_b[:, j, hh * 96:(hh + 1) * 96], cqb[:, j, 0:N], j == 0, j == 1,
                           [Bwq, Bcqb], [Bpq])
                    pr, Bpr = pb()
                    for j in range(2):
                        mm(pr[0:96, 0:N], wqr_b[:, j, hh * 96:(hh + 1) * 96], cqb[:, j, 0:N], j == 0, j == 1,
                           [Bwqr, Bcqb], [Bpr])
                    s2, Bs2 = b16_r.next()
                    act(s2[0:96, 0:N], pq[0:96, 0:N], AF.Square, [Bpq], [Bs2])
                    qf, Bqf = qf_r.next()
                    acp(qf[0:96, 0:N], pq[0:96, 0:N], [Bpq], [Bqf])
                    pm, Bpm = pb()
                    mm(pm[0:96, 0:N], ones_b[0:96, 0:96], s2[0:96, 0:N], True, True, [Bones, Bs2], [Bpm])
                    ta, Bta = f32_r.next()
                    tb_, Btb = f32_r.next()
                    rq, Brq = f32_r.next()
                    tt("dve", ta[0:96, 0:N], pm[0:96, 0:N], Cq[:, 0:N], ALU.add, [Bpm, BCq], [Bta])
                    act(tb_[0:96, 0:N], ta[0:96, 0:N], AF.Sqrt, [Bta], [Btb], scale=1.0 / 96.0)
                    recip(rq[0:96, 0:N], tb_[0:96, 0:N], [Btb], [Brq])
                    qo, Bqo = qo_r.next()
                    stt("dve", qo[0:64, 0:N], qf[0:64, 0:N], vec[0:64, 11:12], rq[0:64, 0:N], ALU.mult, ALU.mult,
                        [Bqf, Bvec, Brq], [Bqo])
                    u1, Bu1 = f32_r.next()
                    u2, Bu2 = f32_r.next()
                    stt("dve", u1[64:96, 0:N], qf[64:96, 0:N], vec[64:96, 11:12], c[64:96, 0, 0:N], ALU.mult, ALU.mult,
                        [Bqf, Bvec, Bc], [Bu1])
                    stt("dve", u2[64:96, 0:N], pr[64:96, 0:N], vec[64:96, 12:13], c[64:96, 1, 0:N], ALU.mult, ALU.mult,
                        [Bpr, Bvec, Bc], [Bu2])
                    tt("pool", u1[64:96, 0:N], u1[64:96, 0:N], u2[64:96, 0:N], ALU.add, [Bu2], [Bu1])
                    tt("pool", qo[64:96, 0:N], u1[64:96, 0:N], rq[64:96, 0:N], ALU.mult, [Bu1, Brq], [Bqo])
                    S.dma("sp", q_scr[hh * 96:(hh + 1) * 96, g0:g0 + N], qo[0:96, 0:N], reads=[Bqo])
                    pk, Bpk = pb()
                    mm(pk[:, 0:N], wkv_b[:, hh * 128:(hh + 1) * 128], ckvb[:, 0:N], True, True, [Bwkv, Bckvb], [Bpk])
                    sx, Bsx = sx_r.next()
                    act(sx[0:64, 0:N], pk[0:64, 0:N], AF.Square, [Bpk], [Bsx])
                    kf, Bkf = qf_r.next()
                    acp(kf[0:64, 0:N], pk[0:64, 0:N], [Bpk], [Bkf])
                    pm, Bpm = pb()
                    mm(pm[0:96, 0:N], ones_b[:, 0:96], sx[:, 0:N], True, True, [Bones, Bsx], [Bpm])
                    ta, Bta = f32_r.next()
                    tb_, Btb = f32_r.next()
                    rk, Brk = f32_r.next()
                    tt("dve", ta[0:96, 0:N], pm[0:96, 0:N], S2[:, 0:N], ALU.mult, [Bpm, BS2], [Bta])
                    tt("pool", ta[0:96, 0:N], ta[0:96, 0:N], MR[:, 0:N], ALU.add, [BMR], [Bta])
                    act(tb_[0:96, 0:N], ta[0:96, 0:N], AF.Sqrt, [Bta], [Btb], scale=1.0 / 96.0)
                    recip(rk[0:96, 0:N], tb_[0:96, 0:N], [Btb], [Brk])
                    tt("pool", tb_[0:64, 0:N], rk[0:64, 0:N], skv[0:64, 0:N], ALU.mult, [Brk, Bskv], [Btb])
                    ko, Bko = ko_r.next()
                    stt("dve", ko[0:64, 0:N], kf[0:64, 0:N], vec[0:64, 13:14], tb_[0:64, 0:N], ALU.mult, ALU.mult,
                        [Bkf, Bvec, Btb], [Bko])
                    tt("pool", ko[64:96, 0:N], KRr[64:96, 0:N], rk[64:96, 0:N], ALU.mult, [BKRr, Brk], [Bko])
                    S.dma("sp", k_gath[r * 768 + hh * 96:r * 768 + (hh + 1) * 96, t0:t0 + N], ko[0:96, 0:N], reads=[Bko])
                for j0 in range(0, N, 128):
                    nt = min(128, N - j0)
                    pv, Bpv = pb()
                    mm(pv[:, 0:512], ckvb[:, j0:j0 + 128], wv_b[:, :], True, True, [Bckvb, Bwv], [Bpv])
                    p1, Bp1 = pb()
                    mm(p1[:, 0:2], sqkv[:, j0:j0 + 128], ones_b[:, 0:2], True, True, [Bsqkv, Bones], [Bp1])
                    svt, Bsvt = sv_r.next()
                    act(svt[0:nt, 0:1], p1[0:nt, 0:1], AF.Sqrt, [Bp1], [Bsvt], bias=EPS, scale=1.0 / 128.0)
                    recip(svt[0:nt, 1:2], svt[0:nt, 0:1], [], [Bsvt])
                    vo, Bvo = vo_r.next()
                    ts("dve", vo[0:nt, :], pv[0:nt, 0:512], svt[0:nt, 1:2], None, ALU.mult, None, [Bpv, Bsvt], [Bvo])
                    S.dma("sp", v_gath[r * LT + t0 + j0:r * LT + t0 + j0 + nt, :], vo[0:nt, :], reads=[Bvo])

            cur = load(0)
            for i in range(len(tiles)):
                nxt = load(i + 1) if i + 1 < len(tiles) else None
                compute(i, *cur)
                cur = nxt
            S.flush()

    def phaseC(l):
        with ExitStack() as es:
            nchmax = (TP + 127) // 128
            KT_r = Ring(nc, es, "KT", 2, [96, TP], BF16)
            V_r = Ring(nc, es, "Vt", 2, [128, nchmax, 65], BF16)
            QT_r = Ring(nc, es, "QT", 3, [96, 512], BF16)
            PT_r = Ring(nc, es, "PT", 3, [128, 2, 512], BF16)
            rec_r = Ring(nc, es, "recC", 2, [128, 512], F32)
            for (rt, Brt) in rec_r.items:
                mset("pool", rt[:, :], 0.0, [Brt])
            bcs_r = Ring(nc, es, "bcsC", 2, [65, 512], F32)
            yo_r = Ring(nc, es, "yoC", 2, [65, 512], F32)
            for (V, BV) in V_r.items:
                mset("pool", V[:, :, 0:1], 1.0, [BV])
            stg = PSP[0:2]
            ob = PS[4:6]
            bcb = PS[6:8]
            pend = [None]

            def groups(chs):
                g = []
                i = 0
                while i < len(chs):
                    if i + 1 < len(chs) and chs[i][1] == chs[i + 1][1]:
                        g.append((i, 2))
                        i += 2
                    else:
                        g.append((i, 1))
                        i += 1
                return g
            units = [(s, hh) for s in range(2) for hh in range(8)]

            def chunks(T):
                nfull, rem = T // 128, T % 128
                if rem == 0:
                    return [(i * 128, 128) for i in range(nfull)]
                if rem >= 65 or nfull == 0:
                    return [(i * 128, 128) for i in range(nfull)] + [(nfull * 128, rem)]
                tot = 128 + rem
                a = (tot + 1) // 2
                return [(i * 128, 128) for i in range(nfull - 1)] + [((nfull - 1) * 128, a), ((nfull - 1) * 128 + a, tot - a)]

            def loadKV(s, hh):
                T, L = seq_T[s], seg_len[s]
                KT, BK = KT_r.next()
                V, BV = V_r.next()
                for kc0 in range(0, T, 2048):
                    kn = min(2048, T - kc0)
                    S.dma("sp", KT[0:96, kc0:kc0 + kn],
                          k_gath[hh * 96:(hh + 1) * 96, seg_off[s] + kc0:seg_off[s] + kc0 + kn],
                          reads=[Bkg], writes=[BK])
                chs = chunks(T)
                for r in range(VR):
                    n0, n1 = r * L, (r + 1) * L
                    base = r * LT + seg_off[s]
                    ci = 0
                    while ci < len(chs):
                        cs0, csz = chs[ci]
                        lo, hi = max(cs0, n0), min(cs0 + csz, n1)
                        if lo >= hi:
                            ci += 1
                            continue
                        if lo == cs0 and hi == cs0 + csz and csz == 128:
                            cj = ci
                            while cj < len(chs) and chs[cj][1] == 128 and chs[cj][0] + 128 <= n1:
                                cj += 1
                            nf = min(cj - ci, 8)
                            cj = ci + nf
                            S.dma("sp", V[:, ci:ci + nf, 1:65],
                                  v_gath[base + lo - n0:base + lo - n0 + nf * 128, hh * 64:(hh + 1) * 64]
                                  .rearrange("(c p) d -> p c d", p=128), reads=[Bvg], writes=[BV])
                            ci = cj
                        else:
                            S.dma("sp", V[lo - cs0:hi - cs0, ci, 1:65],
                                  v_gath[base + lo - n0:base + hi - n0, hh * 64:(hh + 1) * 64],
                                  reads=[Bvg], writes=[BV])
                            if hi == cs0 + csz:
                                ci += 1
                            else:
                                break
                return KT, BK, V, BV

            ui = [0]

            def unit(s, hh, KT, BK, V, BV):
                T, L = seq_T[s], seg_len[s]
                chs = chunks(T)
                grp = groups(chs)
                ng = len(grp)
                qts = [(rr_ * LT + seg_off[s] + qq_, min(512, L - qq_)) for rr_ in range(VR) for qq_ in range(0, L, 512)]

                def loadQ(j):
                    qc, nq = qts[j]
                    QT, BQ = QT_r.next()
                    S.dma("sp", QT[0:96, 0:nq], q_scr[hh * 96:(hh + 1) * 96, qc:qc + nq], writes=[BQ])
                    return QT, BQ

                curq = loadQ(0)
                for j, (qc, nq) in enumerate(qts):
                    nxtq = loadQ(j + 1) if j + 1 < len(qts) else None
                    QT, BQ = curq
                    curq = nxtq
                    o, Bo = ob[ui[0] % 2]
                    bc, Bbc = bcb[ui[0] % 2]
                    ui[0] += 1
                    pts = {}
                    for g in range(ng + 2):
                        if g == 2 and pend[0] is not None:
                            pend[0]()
                            pend[0] = None
                        if g < ng:
                            k0, cnt = grp[g]
                            KS = chs[k0][1]
                            st, Bst = stg[g % 2]
                            for c in range(cnt):
                                K0 = chs[k0 + c][0]
                                mm(st[0:KS, c * 512:c * 512 + nq], KT[0:96, K0:K0 + KS], QT[0:96, 0:nq], True, True,
                                   [BK, BQ], [Bst])
                            pt, Bpt = PT_r.next()
                            if cnt == 2:
                                act(pt[0:KS, :, 0:nq], st[:, :].rearrange("p (c n) -> p c n", c=2)[0:KS, :, 0:nq],
                                    AF.Exp, [Bst], [Bpt], scale=SCALE)
                            else:
                                act(pt[0:KS, 0, 0:nq], st[0:KS, 0:nq], AF.Exp, [Bst], [Bpt], scale=SCALE)
                            pts[g] = (pt, Bpt, KS, k0, cnt)
                        if g >= 2:
                            pt, Bpt, KS, k0, cnt = pts.pop(g - 2)
                            for c in range(cnt):
                                kk = k0 + c
                                mm(o[0:65, 0:nq], V[0:KS, kk, 0:65], pt[0:KS, c, 0:nq], kk == 0, kk == len(chs) - 1,
                                   [BV, Bpt], [Bo])

                    def fin(o=o, Bo=Bo, bc=bc, Bbc=Bbc, nq=nq, qc=qc):
                        rec, Brec = rec_r.next()
                        recip(rec[0:1, 0:nq], o[0:1, 0:nq], [Bo], [Brec])
                        mm(bc[0:65, 0:nq], ones_f[:, 0:65], rec[:, 0:nq], True, True, [Bonesf, Brec], [Bbc])
                        bcs, Bbcs = bcs_r.next()
                        acp(bcs[0:65, 0:nq], bc[0:65, 0:nq], [Bbc], [Bbcs])
                        yo, Byo = yo_r.next()
                        tt("dve", yo[0:65, 0:nq], o[0:65, 0:nq], bcs[0:65, 0:nq], ALU.mult, [Bo, Bbcs], [Byo])
                        S.dma("sp", ya_scr[hh * 64:(hh + 1) * 64, qc:qc + nq], yo[1:65, 0:nq], reads=[Byo])
                    if pend[0] is not None:
                        pend[0]()
                    pend[0] = fin

            cur = loadKV(*units[0])
            for i, (s, hh) in enumerate(units):
                nxt = loadKV(*units[i + 1]) if i + 1 < len(units) else None
                unit(s, hh, *cur)
                cur = nxt
            if pend[0] is not None:
                pend[0]()
                pend[0] = None
            S.flush()

    def phaseB(l):
        with ExitStack() as es:
            LM = TP // BR
            vec, Bvec = tile(es, "vecB", [128, NV], F32)
            lw, Blw = tile(es, "lwB", [128, 16, 128], F32)
            bd_b, Bbd = tile(es, "bdB", [128, 16, 128], BF16)
            coef, Bcoef = tile(es, "coefB", [128, 32], F32)
            carry, Bcarry = tile(es, "carryB", [128, 1], F32)
            S.dma("sp", vec[:, :], vecs[l, :, :], writes=[Bvec])
            S.dma("sp", lw[:, :, :], lruw[l].rearrange("f m p q -> p (f m) q"), writes=[Blw])
            cp("dve", bd_b[:, :, :], lw[:, :, :], [Blw], [Bbd])
            act(coef[:, 0:8], vec[:, 67:75], AF.Exp, [Bvec], [Bcoef], scale=-1.0)
            act(coef[:, 8:16], coef[:, 0:8], AF.Ln, [], [Bcoef], bias=1.0)
            ts("dve", coef[:, 16:24], coef[:, 8:16], -8.0, None, ALU.mult, None, [], [Bcoef])
            ts("dve", coef[:, 24:32], coef[:, 8:16], -16.0, None, ALU.mult, None, [], [Bcoef])
            U_r = Ring(nc, es, "UB", 2, [128, LM + 4], F32)
            G_r = Ring(nc, es, "GB", 2, [128, LM], F32)
            HF_r = Ring(nc, es, "HFB", 2, [128, LM], F32)
            H_r = Ring(nc, es, "HB", 2, [128, LM], F32)
            Y_r = Ring(nc, es, "YB", 2, [128, LM], F32)
            xc, Bxc = tile(es, "xcB", [128, LM], F32)
            xcb, Bxcb = tile(es, "xcbB", [128, LM], BF16)
            rr, Brr = tile(es, "rrB", [128, LM], F32)
            ig, Big = tile(es, "igB", [128, LM], F32)
            aa, Baa = tile(es, "aaB", [128, LM], F32)
            a2, Ba2 = tile(es, "a2B", [128, LM], F32)
            bb, Bbb = tile(es, "bbB", [128, LM], F32)
            BHF = {}

            def loads(s_, m, L, r, direction):
                o0 = seg_off[s_]
                U, BU = U_r.next()
                if r == 0:
                    mset("pool", U[:, 0:2], 0.0, [BU])
                if r == BR - 1:
                    mset("pool", U[:, 2 + L:4 + L], 0.0, [BU])
                c0 = o0 + r * L
                S.dma("sp", U[:, 2:2 + L], ug_gath[m * 128:(m + 1) * 128, c0:c0 + L], reads=[Bug], writes=[BU])
                if r > 0:
                    S.dma("sp", U[:, 0:2], ug_gath[m * 128:(m + 1) * 128, c0 - 2:c0], reads=[Bug], writes=[BU])
                if r < BR - 1:
                    S.dma("sp", U[:, 2 + L:4 + L], ug_gath[m * 128:(m + 1) * 128, c0 + L:c0 + L + 2],
                          reads=[Bug], writes=[BU])
                Gt = BG = HF = BHFt = None
                if direction == 1:
                    Gt, BG = G_r.next()
                    S.dma("sp", Gt[:, 0:L], ug_gath[512 + m * 128:512 + (m + 1) * 128, c0:c0 + L],
                          reads=[Bug], writes=[BG])
                    HF, BHFt = HF_r.next()
                    S.dma("sp", HF[:, 0:L], hf_scr[:, r * L:(r + 1) * L], reads=[BHF[(s_, m, r)]], writes=[BHFt])
                return U, BU, Gt, BG, HF, BHFt

            def compute(s_, m, L, r, direction, U, BU, Gt, BG, HF, BHFt):
                o0 = seg_off[s_]
                zi = direction * 4 + m
                ts("dve", xc[:, 0:L], U[:, 0:L], vec[:, 31 + m * 4:32 + m * 4], vec[:, 47 + m:48 + m], ALU.mult, ALU.add,
                   [BU, Bvec], [Bxc])
                for tap in range(1, 4):
                    stt("dve", xc[:, 0:L], U[:, tap:tap + L], vec[:, 31 + m * 4 + tap:32 + m * 4 + tap], xc[:, 0:L],
                        ALU.mult, ALU.add, [BU, Bvec], [Bxc])
                cp("pool", xcb[:, 0:L], xc[:, 0:L], [Bxc], [Bxcb])
                for c0 in range(0, L, 512):
                    n = min(512, L - c0)
                    ps, Bps = pb()
                    mm(ps[:, 0:n], bd_b[:, zi, :], xcb[:, c0:c0 + n], True, True, [Bbd, Bxcb], [Bps])
                    act(rr[:, c0:c0 + n], ps[:, 0:n], AF.Sigmoid, [Bps, Bvec], [Brr], bias=vec[:, 51 + zi:52 + zi])
                    ps, Bps = pb()
                    mm(ps[:, 0:n], bd_b[:, 8 + zi, :], xcb[:, c0:c0 + n], True, True, [Bbd, Bxcb], [Bps])
                    act(ig[:, c0:c0 + n], ps[:, 0:n], AF.Sigmoid, [Bps, Bvec], [Big], bias=vec[:, 59 + zi:60 + zi])
                act(aa[:, 0:L], rr[:, 0:L], AF.Exp, [Brr, Bcoef], [Baa], scale=coef[:, 16 + zi:17 + zi])
                act(a2[:, 0:L], rr[:, 0:L], AF.Exp, [Brr, Bcoef], [Ba2], scale=coef[:, 24 + zi:25 + zi])
                ts("dve", a2[:, 0:L], a2[:, 0:L], -1.0, 1.0, ALU.mult, ALU.add, [], [Ba2])
                act(a2[:, 0:L], a2[:, 0:L], AF.Sqrt, [], [Ba2])
                tt("pool", bb[:, 0:L], a2[:, 0:L], ig[:, 0:L], ALU.mult, [Ba2, Big], [Bbb])
                tt("dve", bb[:, 0:L], bb[:, 0:L], xc[:, 0:L], ALU.mult, [Bxc], [Bbb])
                H, BH = H_r.next()
                if direction == 0:
                    S.op("dve", lambda e, o=H[:, 0:L], d0=aa[:, 0:L], d1=bb[:, 0:L]: e.tensor_tensor_scan(
                        out=o, data0=d0, data1=d1, initial=carry[:, 0:1], op0=ALU.mult, op1=ALU.add),
                        [Baa, Bbb, Bcarry], [BH])
                    cp("dve", carry[:, 0:1], H[:, L - 1:L], [BH], [Bcarry])
                    BHF[(s_, m, r)] = Buf(f"hf{s_}_{m}_{r}_{l}")
                    S.dma("sp", hf_scr[:, r * L:(r + 1) * L], H[:, 0:L], reads=[BH], writes=[BHF[(s_, m, r)]], owner=BH)
                else:
                    S.op("dve", lambda e, o=H[:, 0:L][:, ::-1], d0=aa[:, 0:L][:, ::-1], d1=bb[:, 0:L][:, ::-1]: e.tensor_tensor_scan(
                        out=o, data0=d0, data1=d1, initial=carry[:, 0:1], op0=ALU.mult, op1=ALU.add),
                        [Baa, Bbb, Bcarry], [BH])
                    cp("dve", carry[:, 0:1], H[:, 0:1], [BH], [Bcarry])
                    tt("pool", H[:, 0:L], H[:, 0:L], HF[:, 0:L], ALU.add, [BHFt], [BH])
                    act(Gt[:, 0:L], Gt[:, 0:L], AF.Gelu_apprx_tanh, [], [BG])
                    Y, BY = Y_r.next()
                    tt("dve", Y[:, 0:L], H[:, 0:L], Gt[:, 0:L], ALU.mult, [BH, BG], [BY])
                    S.dma("sp", y_gath[m * 128:(m + 1) * 128, o0 + r * L:o0 + (r + 1) * L], Y[:, 0:L], reads=[BY])

            work = []
            for s_ in range(2):
                L = seq_T[s_] // BR
                for m in range(4):
                    for direction in (0, 1):
                        order = list(range(BR)) if direction == 0 else list(range(BR - 1, -1, -1))
                        for idx, r in enumerate(order):
                            work.append((s_, m, L, r, direction, idx == 0))
            ld = loads(*work[0][:5])
            for i, w in enumerate(work):
                nld = None
                pre = i + 1 < len(work) and not (work[i + 1][4] == 1 and w[4] == 0)
                if pre:
                    nld = loads(*work[i + 1][:5])
                if w[5]:
                    mset("dve", carry[:, 0:1], 0.0, [Bcarry])
                compute(*w[:5], *ld)
                if i + 1 < len(work) and not pre:
                    nld = loads(*work[i + 1][:5])
                ld = nld
            S.flush()

    def phaseD1(l, src):
        with ExitStack() as es:
            vec, Bvec = tile(es, "vecD", [128, NV], F32)
            wo_b, Bwo = tile(es, "wo_b", [128, 8, D], BF16)
            stg = Ring(nc, es, "wstgD", 2, [128, D], F32)
            S.dma("sp", vec[:, :], vecs[l, :, :], writes=[Bvec])
            for kc in range(8):
                st, Bst = stg.next()
                S.dma("sp", st[:, :], w_out[l, kc * 128:(kc + 1) * 128, :], writes=[Bst])
                ts("dve", wo_b[:, kc, :], st[:, :], vec[:, 15 + kc:16 + kc], None, ALU.mult, None, [Bst, Bvec], [Bwo])
            tiles = [(t0, min(512, NTOK - t0)) for t0 in range(0, NTOK, 512)]
            hT_r = Ring(nc, es, "hTD", 2, [128, 8, 512], F32)
            Y_r = Ring(nc, es, "YD", 2, [128, 8, 512], F32)
            sq_r = Ring(nc, es, "sqD", 2, [128, 8, 512], BF16)
            yb_r = Ring(nc, es, "ybD", 2, [128, 8, 512], BF16)
            f32_r = Ring(nc, es, "f32D", 4, [128, 512], F32)
            hsrc = src.rearrange("(kc p) t -> p kc t", p=128)
            hdst = hT_scr.rearrange("(kc p) t -> p kc t", p=128)
            ygv = y_gath.rearrange("(kc p) t -> p kc t", p=128)
            yav = ya_scr.rearrange("(kc p) t -> p kc t", p=128)
            def load(i):
                t0, N = tiles[i]
                h, Bh = hT_r.next()
                S.dma("sp", h[:, :, 0:N], hsrc[:, :, t0:t0 + N], writes=[Bh])
                Y, BY = Y_r.next()
                S.dma("sp", Y[:, 0:4, 0:N], ygv[:, :, t0:t0 + N], reads=[Byg], writes=[BY])
                S.dma("sp", Y[:, 4:8, 0:N], yav[:, :, t0:t0 + N], writes=[BY])
                return h, Bh, Y, BY

            def compute(i, h, Bh, Y, BY):
                t0, N = tiles[i]
                sq, Bsq = sq_r.next()
                yb, Byb = yb_r.next()
                for kc in range(8):
                    act(sq[:, kc, 0:N], Y[:, kc, 0:N], AF.Square, [BY], [Bsq])
                Rs = []
                for grp in range(2):
                    ps, Bps = pb()
                    for kc in range(4):
                        mm(ps[:, 0:N], ones_b[:, :], sq[:, grp * 4 + kc, 0:N], kc == 0, kc == 3, [Bones, Bsq], [Bps])
                    tmp, Btmp = f32_r.next()
                    R, BR = f32_r.next()
                    rstd_from(ps[:, 0:N], 1.0 / 512.0, EPS, tmp[:, 0:N], R[:, 0:N], Bps, Btmp, BR)
                    Rs.append((R, BR))
                for kc in range(8):
                    R, BR = Rs[kc // 4]
                    tt("dve" if kc % 2 == 0 else "pool", yb[:, kc, 0:N], Y[:, kc, 0:N], R[:, 0:N], ALU.mult,
                       [BY, BR], [Byb])
                for oc in range(8):
                    ps, Bps = pb()
                    for kc in range(8):
                        mm(ps[:, 0:N], wo_b[:, kc, oc * 128:(oc + 1) * 128], yb[:, kc, 0:N], kc == 0, kc == 7,
                           [Bwo, Byb], [Bps])
                    tt("dve", h[:, oc, 0:N], h[:, oc, 0:N], ps[:, 0:N], ALU.add, [Bps], [Bh])
                S.dma("sp", hdst[:, :, t0:t0 + N], h[:, :, 0:N], reads=[Bh])

            cur = load(0)
            for i in range(len(tiles)):
                nxt = load(i + 1) if i + 1 < len(tiles) else None
                compute(i, *cur)
                cur = nxt
            S.flush()

    def phaseD2(l, dst):
        with ExitStack() as es:
            NT = 256
            vec, Bvec = tile(es, "vecE", [128, NV], F32)
            wu_b, Bwu = tile(es, "wu_b", [128, 8, 4 * D], BF16)
            wd_b, Bwd = tile(es, "wd_b", [128, 32, D], BF16)
            S.dma("sp", vec[:, :], vecs[l, :, :], writes=[Bvec])
            with ExitStack() as es2:
                stg = Ring(nc, es2, "wstgE", 2, [128, 4 * D], F32)
                for kc in range(8):
                    st, Bst = stg.next()
                    S.dma("sp", st[:, :], w_up[l, kc * 128:(kc + 1) * 128, :], writes=[Bst])
                    ts("dve" if kc % 2 == 0 else "pool", wu_b[:, kc, :], st[:, :], vec[:, 23 + kc:24 + kc], None,
                       ALU.mult, None, [Bst, Bvec], [Bwu])
                for f4 in range(8):
                    st, Bst = stg.next()
                    S.dma("sp", st[:, :].rearrange("p (f n) -> p f n", n=D),
                          w_down[l, f4 * 512:(f4 + 1) * 512, :].rearrange("(f p) n -> p f n", p=128), writes=[Bst])
                    cp("dve" if f4 % 2 == 0 else "pool", wd_b[:, f4 * 4:(f4 + 1) * 4, :],
                       st[:, :].rearrange("p (f n) -> p f n", n=D), [Bst], [Bwd])
                S.flush()
            tiles = [(t0, min(NT, NTOK - t0)) for t0 in range(0, NTOK, NT)]
            hT_r = Ring(nc, es, "hTE", 2, [128, 8, NT], F32)
            sq_r = Ring(nc, es, "sqE", 2, [128, 8, NT], BF16)
            hb_r = Ring(nc, es, "hbE", 2, [128, 8, NT], BF16)
            ac_r = Ring(nc, es, "acE", 2, [128, 32, NT], BF16)
            f32_r = Ring(nc, es, "f32E", 6, [128, NT], F32)
            hsrc = hT_scr.rearrange("(kc p) t -> p kc t", p=128)
            hdst = dst.rearrange("(kc p) t -> p kc t", p=128)

            def load(i):
                t0, N = tiles[i]
                h, Bh = hT_r.next()
                S.dma("sp", h[:, :, 0:N], hsrc[:, :, t0:t0 + N], writes=[Bh])
                return h, Bh

            def compute(i, h, Bh):
                t0, N = tiles[i]
                sq, Bsq = sq_r.next()
                hb, Bhb = hb_r.next()
                for kc in range(8):
                    act(sq[:, kc, 0:N], h[:, kc, 0:N], AF.Square, [Bh], [Bsq])
                ps, Bps = pb()
                for kc in range(8):
                    mm(ps[:, 0:N], ones_b[:, :], sq[:, kc, 0:N], kc == 0, kc == 7, [Bones, Bsq], [Bps])
                tmp, Btmp = f32_r.next()
                R, BR = f32_r.next()
                rstd_from(ps[:, 0:N], 1.0 / D, EPS, tmp[:, 0:N], R[:, 0:N], Bps, Btmp, BR)
                for kc in range(8):
                    tt("dve" if kc % 2 == 0 else "pool", hb[:, kc, 0:N], h[:, kc, 0:N], R[:, 0:N], ALU.mult,
                       [Bh, BR], [Bhb])
                ac, Bac = ac_r.next()
                for fc in range(32):
                    ps, Bps = pb()
                    for kc in range(8):
                        mm(ps[:, 0:N], wu_b[:, kc, fc * 128:(fc + 1) * 128], hb[:, kc, 0:N], kc == 0, kc == 7,
                           [Bwu, Bhb], [Bps])
                    rl, Brl = f32_r.next()
                    act(rl[:, 0:N], ps[:, 0:N], AF.Relu, [Bps], [Brl])
                    tt("dve" if fc % 2 == 0 else "pool", ac[:, fc, 0:N], rl[:, 0:N], rl[:, 0:N], ALU.mult, [Brl], [Bac])
                for oc in range(8):
                    ps, Bps = pb()
                    for fc in range(32):
                        mm(ps[:, 0:N], wd_b[:, fc, oc * 128:(oc + 1) * 128], ac[:, fc, 0:N], fc == 0, fc == 31,
                           [Bwd, Bac], [Bps])
                    tt("dve", h[:, oc, 0:N], h[:, oc, 0:N], ps[:, 0:N], ALU.add, [Bps], [Bh])
                S.dma("sp", hdst[:, :, t0:t0 + N], h[:, :, 0:N], reads=[Bh])

            cur = load(0)
            for i in range(len(tiles)):
                nxt = load(i + 1) if i + 1 < len(tiles) else None
                compute(i, *cur)
                cur = nxt
            S.flush()

    for l in range(NLAYER):
        src = xT if l == 0 else hT_scr
        if "A" in PHASES:
            phaseA(l, src)
        if "C" in PHASES:
            phaseC(l)
        if "B" in PHASES:
            phaseB(l)
        if "D" in PHASES:
            phaseD1(l, src)
        if "E" in PHASES:
            phaseD2(l, yT if l == NLAYER - 1 else hT_scr)
    S.flush()
    G.close()
    return nc, dict(LT=LT, NTOK=NTOK, seg_len=seg_len, seg_off=seg_off, seq_T=seq_T, LP=LP, LS=LS, TP=TP, TS=TS)


def prep(inp, SEQ, DSEQ, meta):
    LT, NTOK, seg_len, seg_off = meta["LT"], meta["NTOK"], meta["seg_len"], meta["seg_off"]
    f = lambda k: np.asarray(inp[k], dtype=np.float32)
    xp, xs, mt = f("x_prompt"), f("x_sample"), f("meta_tokens")
    nP, nS = xp.shape[0], xs.shape[0]
    inv_freq = (1.0 / (10000.0 ** (np.arange(0, 32, 2, dtype=np.float32) / np.float32(32)))).astype(np.float32)
    qg, kg = f("qk_q_g"), f("qk_k_g")

    def swp(g):
        o = np.zeros(96, np.float32)
        o[64:80] = g[80:96]
        o[80:96] = g[64:80]
        return o

    vecs = np.zeros((2, 128, NV), np.float32)
    lruw = np.zeros((2, 4, 4, 128, 128), np.float32)
    for l in range(2):
        vecs[l, :, 0:8] = f("norm_mix_g")[l].reshape(8, 128).T
        vecs[l, :, 8:10] = f("q_norm_g")[l].reshape(2, 128).T
        vecs[l, :, 10] = f("kv_norm_g")[l]
        vecs[l, 0:96, 11] = qg[l]
        vecs[l, 0:96, 12] = swp(qg[l])
        vecs[l, 0:96, 13] = kg[l]
        vecs[l, 0:96, 14] = swp(kg[l])
        vecs[l, :, 15:19] = f("out_norm_lru_g")[l].reshape(4, 128).T
        vecs[l, :, 19:23] = f("out_norm_attn_g")[l].reshape(4, 128).T
        vecs[l, :, 23:31] = f("norm_ff_g")[l].reshape(8, 128).T
        for m in range(4):
            ch = slice(m * 128, (m + 1) * 128)
            vecs[l, :, 31 + m * 4:35 + m * 4] = f("conv_w")[l][:, ch].T
            vecs[l, :, 47 + m] = f("conv_b")[l][ch]
            for z in range(2):
                vecs[l, :, 51 + z * 4 + m] = f("lru_ba")[l, z][ch]
                vecs[l, :, 59 + z * 4 + m] = f("lru_bx")[l, z][ch]
                vecs[l, :, 67 + z * 4 + m] = f("lru_lambda")[l, z][ch]
                for half in range(2):
                    hs = slice(half * 64, (half + 1) * 64)
                    lruw[l, z, m, hs, hs] = f("lru_wa")[l, z, 2 * m + half]
                    lruw[l, 2 + z, m, hs, hs] = f("lru_wx")[l, z, 2 * m + half]
    shared = {"vecs": vecs, "lruw": lruw, "w_in": f("w_in"), "w_uq": f("w_uq"), "w_ukv": f("w_ukv"),
              "w_out": f("w_out"), "w_up": f("w_up"), "w_down": f("w_down")}
    in_maps = []
    cache = {}
    for c in range(NCORE):
        key = (c % nP, c % nS)
        if key not in cache:
            seqs = [np.concatenate([mt, xp[key[0]]], 0), np.concatenate([mt, xs[key[1]]], 0)]
            xT = np.empty((D, NTOK), np.float32)
            pos = np.empty(NTOK, np.float32)
            for r in range(VR):
                for s in range(2):
                    L = seg_len[s]
                    c0 = r * LT + seg_off[s]
                    xT[:, c0:c0 + L] = seqs[s][r * L:(r + 1) * L].T
                    pos[c0:c0 + L] = np.arange(r * L, (r + 1) * L, dtype=np.float32)
            ang = pos[None, :] * inv_freq[:, None]
            cs = np.empty((2, 32, NTOK), np.float32)
            cs[0, 0:16] = np.cos(ang)
            cs[0, 16:32] = np.cos(ang)
            cs[1, 0:16] = np.sin(ang)
            cs[1, 16:32] = np.sin(ang)
            cache[key] = (xT, cs)
        xT, cs = cache[key]
        m = {"xT": xT, "cs": cs}
        m.update(shared)
        in_maps.append(m)
    return in_maps


_CACHE = {}


def run(inp, SEQ, DSEQ, trace=False):
    key = (SEQ, DSEQ)
    if key not in _CACHE:
        _CACHE[key] = build(SEQ, DSEQ)
    nc, meta = _CACHE[key]
    in_maps = prep(inp, SEQ, DSEQ, meta)
    res = run_bass_kernel_spmd(nc, in_maps, core_ids=list(range(NCORE)), **({"trace": True} if trace else {}))
    B, DB = inp["x_prompt"].shape[0], inp["x_sample"].shape[0]
    TP, TS, LT, LP, LS = meta["TP"], meta["TS"], meta["LT"], meta["LP"], meta["LS"]
    yp = np.empty((B, TP, D), np.float32)
    ys = np.empty((DB, TS, D), np.float32)
    for b in range(B):
        yT = np.asarray(res.results[b]["yT"])
        for r in range(VR):
            yp[b, r * LP:(r + 1) * LP] = yT[:, r * LT:r * LT + LP].T
    for b in range(DB):
        yT = np.asarray(res.results[b]["yT"])
        for r in range(VR):
            ys[b, r * LS:(r + 1) * LS] = yT[:, r * LT + LP:r * LT + LP + LS].T
    return (np.ascontiguousarray(yp[:, 16:]), np.ascontiguousarray(ys[:, 16:])), res


def kernel(**inputs):
    SEQ = inputs["x_prompt"].shape[1]
    DSEQ = inputs["x_sample"].shape[1]
    out, _ = run(inputs, SEQ, DSEQ)
    return out
```

```python
import numpy as np
from contextlib import ExitStack
import concourse.bass as bass
import concourse.mybir as mybir
from concourse.bass_utils import run_bass_kernel_spmd

F32 = mybir.dt.float32
BF16 = mybir.dt.bfloat16
AF = mybir.ActivationFunctionType
ALU = mybir.AluOpType

D = 1024
NCORE = 8
VR = 1
BR = 8
NV = 75
EPS = 1e-6
IN_W = 1440
SCALE = 96 ** -0.5
PHASES = "AGCBDE"
NLAYER = 2
DBG = 99
VAR = 0


class Buf:
    __slots__ = ("name", "w", "r", "dsem")

    def __init__(self, name):
        self.name = name
        self.w = None
        self.r = []
        self.dsem = None


class Eng:
    def __init__(self, name, sem):
        self.name = name
        self.sem = sem
        self.cnt = 0
        self.known = {}
        self.insts = []


class Sched:
    def __init__(self, nc):
        self.nc = nc
        self.sems = {}
        self.E = {}
        for name in ("pe", "act", "dve", "pool", "sp"):
            self.sems[name] = nc.alloc_semaphore(name="sem_" + name)
            self.E[name] = Eng(name, name)
        self.dval = {}

    def _waits(self, eng, reads, writes):
        need = {}
        for b in reads:
            if b.w is not None:
                k, v = b.w
                if need.get(k, 0) < v:
                    need[k] = v
        for b in writes:
            if b.w is not None:
                k, v = b.w
                if need.get(k, 0) < v:
                    need[k] = v
            for (k, v) in b.r:
                if need.get(k, 0) < v:
                    need[k] = v
        out = []
        for k, v in need.items():
            if eng.name == "pe" and k == "pe":
                continue
            if eng.known.get(k, 0) >= v:
                continue
            eng.known[k] = v
            out.append((k, v))
        return out

    def _mark(self, tag, reads, writes):
        for b in reads:
            if len(b.r) > 64:
                m = {}
                for (k, v) in b.r:
                    if m.get(k, 0) < v:
                        m[k] = v
                b.r = list(m.items())
            b.r.append(tag)
        for b in writes:
            b.w = tag
            b.r = []

    def op(self, en, fn, reads=(), writes=()):
        eng = self.E[en]
        waits = self._waits(eng, reads, writes)
        eng.cnt += 1
        tag = (eng.sem, eng.cnt)
        eng.insts.append((waits, fn, (eng.sem, 1)))
        self._mark(tag, reads, writes)
        return tag

    def _own(self, owner):
        if owner.dsem is None:
            key = "d_" + owner.name
            if key not in self.sems:
                self.sems[key] = self.nc.alloc_semaphore(name=key)
                self.dval[key] = 0
            owner.dsem = key

    def dma(self, en, out, in_, reads=(), writes=(), owner=None, slow=False):
        eng = self.E[en]
        waits = self._waits(eng, reads, writes)
        if owner is None:
            owner = writes[0] if writes else reads[0]
        self._own(owner)
        self.dval[owner.dsem] += 16
        tag = (owner.dsem, self.dval[owner.dsem])
        kw = {"allow_slow_non_contiguous": True} if slow else {}
        def _f(e, o=out, i=in_):
            try:
                return e.dma_start(out=o, in_=i, **kw)
            except Exception:
                print("DMA FAIL", en, o, i)
                raise
        eng.insts.append((waits, _f, (owner.dsem, 16)))
        self._mark(tag, reads, writes)
        return tag

    def coll(self, ins, outs, reads, writes, owner):
        eng = self.E["pool"]
        waits = self._waits(eng, reads, writes)
        self._own(owner)
        self.dval[owner.dsem] += 1
        tag = (owner.dsem, self.dval[owner.dsem])
        rg = [list(range(NCORE))]
        eng.insts.append((waits, (lambda e: e.collective_compute("AllGather", ALU.bypass, replica_groups=rg,
                                                                 ins=ins, outs=outs)), (owner.dsem, None)))
        self._mark(tag, reads, writes)
        return tag

    def flush(self):
        eng = self.E["sp"]
        dr = []
        for key, val in self.dval.items():
            if val > 0 and eng.known.get(key, 0) < val:
                eng.known[key] = val
                dr.append((key, val))
        if dr:
            eng.insts.append((dr, None, None))
        sems = self.sems
        with self.nc.Block() as block:
            for name, attr in (("pe", "tensor"), ("act", "scalar"), ("dve", "vector"),
                               ("pool", "gpsimd"), ("sp", "sync")):
                e = self.E[name]
                if not e.insts:
                    continue

                def body(be, insts=e.insts):
                    for waits, fn, inc in insts:
                        for (k, v) in waits:
                            be.wait_ge(sems[k], v)
                        if fn is None:
                            continue
                        ins = fn(be)
                        if inc is not None:
                            if inc[1] is None:
                                ins.then_inc(sems[inc[0]])
                            else:
                                ins.then_inc(sems[inc[0]], inc[1])
                getattr(block, attr)(body)
                e.insts = []


_UID = [0]


def _uname(name):
    _UID[0] += 1
    return f"{name}_u{_UID[0]}"


class Ring:
    def __init__(self, nc, es, name, n, shape, dtype):
        self.items = []
        for i in range(n):
            t = es.enter_context(nc.sbuf_tensor(_uname(f"{name}{i}"), shape, dtype))
            self.items.append((t, Buf(f"{name}{i}")))
        self.i = 0

    def next(self):
        it = self.items[self.i % len(self.items)]
        self.i += 1
        return it


def build(SEQ, DSEQ):
    TP, TS = SEQ + 16, DSEQ + 16
    LP, LS = TP // VR, TS // VR
    assert LP * VR == TP and LS * VR == TS and TP % BR == 0 and TS % BR == 0 and VR == 1
    seg_len = [LP, LS]
    seq_T = [TP, TS]
    seg_off = [0, LP]
    LT = LP + LS
    NTOK = VR * LT

    nc = bass.Bass("TRN2", target_bir_lowering=False)

    def din(name, shape, dtype=F32):
        return nc.dram_tensor(name, shape, dtype, kind="ExternalInput").ap()

    xT = din("xT", [D, NTOK])
    cs = din("cs", [2, 32, NTOK])
    vecs = din("vecs", [2, 128, NV])
    lruw = din("lruw", [2, 4, 4, 128, 128])
    w_in = din("w_in", [2, D, IN_W])
    w_uq = din("w_uq", [2, 256, 768])
    w_ukv = din("w_ukv", [2, 128, 1024])
    w_out = din("w_out", [2, D, D])
    w_up = din("w_up", [2, D, 4 * D])
    w_down = din("w_down", [2, 4 * D, D])
    yT = nc.dram_tensor("yT", [D, NTOK], F32, kind="ExternalOutput").ap()

    def dscr(name, shape, dtype=F32):
        return nc.dram_tensor(name, shape, dtype, kind="Internal")

    hT_scr = dscr("hT_scr", [D, NTOK]).ap()
    ug_gath = dscr("ug_gath", [VR * D, LT]).ap()
    k_gath = dscr("k_gath", [VR * 768, LT], BF16).ap()
    v_gath = dscr("v_gath", [VR * LT, 512], BF16).ap()
    q_scr = dscr("q_scr", [768, NTOK], BF16).ap()
    hf_scr = dscr("hf_scr", [128, TP]).ap()
    y_gath = dscr("y_gath", [512, VR * LT]).ap()
    ya_scr = dscr("ya_scr", [512, NTOK]).ap()
    Bug, Bkg, Bvg, Byg = Buf("ug_gath"), Buf("k_gath"), Buf("v_gath"), Buf("y_gath")

    S = Sched(nc)
    G = ExitStack()

    def tile(es, name, shape, dtype):
        t = es.enter_context(nc.sbuf_tensor(_uname(name), shape, dtype))
        return t, Buf(name)

    PS = []
    PSP = []
    for i in range(4):
        t = G.enter_context(nc.psum_tensor(f"psp{i}", [128, 1024], F32))
        PSP.append((t, Buf(f"psp{i}")))
        PS.append((t[:, 0:512], Buf(f"ps{2 * i}")))
        PS.append((t[:, 512:1024], Buf(f"ps{2 * i + 1}")))
    pbi = [0]

    def pb():
        it = PS[pbi[0] % 8]
        pbi[0] += 1
        return it

    ones_b, Bones = tile(G, "ones_b", [128, 128], BF16)
    onesR_b, BonesR = tile(G, "onesR_b", [96, 96], BF16)
    ones_f, Bonesf = tile(G, "ones_f", [128, 128], F32)
    S.op("pool", lambda e: e.memset(ones_b[:, :], 1.0), writes=[Bones])
    S.op("pool", lambda e: e.memset(onesR_b[:, :], 0.0), writes=[BonesR])
    S.op("pool", lambda e: e.memset(onesR_b[64:96, :], 1.0), writes=[BonesR])
    S.op("pool", lambda e: e.memset(ones_f[:, :], 0.0), writes=[Bonesf])
    S.op("pool", lambda e: e.memset(ones_f[0:1, :], 1.0), writes=[Bonesf])

    def mm(out, lhsT, rhs, start, stop, reads, writes):
        S.op("pe", lambda e: e.matmul(out, lhsT=lhsT, rhs=rhs, start=start, stop=stop), reads, writes)

    def act(out, in_, func, reads, writes, bias=None, scale=None):
        kw = {}
        if bias is not None:
            kw["bias"] = bias
        if scale is not None:
            kw["scale"] = scale
        S.op("act", lambda e: e.activation(out=out, in_=in_, func=func, **kw), reads, writes)

    def tt(en, out, in0, in1, op, reads, writes):
        S.op(en, lambda e: e.tensor_tensor(out=out, in0=in0, in1=in1, op=op), reads, writes)

    def ts(en, out, in0, s1, s2, op0, op1, reads, writes):
        if s2 is None:
            S.op(en, lambda e: e.tensor_scalar(out=out, in0=in0, scalar1=s1, scalar2=None, op0=op0), reads, writes)
        else:
            S.op(en, lambda e: e.tensor_scalar(out=out, in0=in0, scalar1=s1, scalar2=s2, op0=op0, op1=op1),
                 reads, writes)

    def stt(en, out, in0, scalar, in1, op0, op1, reads, writes):
        S.op(en, lambda e: e.scalar_tensor_tensor(out=out, in0=in0, scalar=scalar, in1=in1, op0=op0, op1=op1),
             reads, writes)

    def cp(en, out, in_, reads, writes):
        S.op(en, lambda e: e.tensor_copy(out=out, in_=in_), reads, writes)

    def acp(out, in_, reads, writes):
        S.op("act", lambda e: e.copy(out=out, in_=in_), reads, writes)

    def recip(out, in_, reads, writes):
        S.op("dve", lambda e: e.reciprocal(out=out, in_=in_), reads, writes)

    def mset(en, ap, val, writes):
        S.op(en, lambda e: e.memset(ap, val), (), writes)

    def rstd_from(ps_ap, scale, bias_c, tmp_ap, out_ap, Bps, Btmp, Bout):
        act(tmp_ap, ps_ap, AF.Sqrt, [Bps], [Btmp], bias=bias_c, scale=scale)
        recip(out_ap, tmp_ap, [Btmp], [Bout])

    def phaseA(l, src):
        with ExitStack() as es:
            win_b, Bwin = tile(es, "win_b", [128, 8, 1632], BF16)
            wq_b, Bwq = tile(es, "wq_b", [128, 2, 768], BF16)
            wqr_b, Bwqr = tile(es, "wqr_b", [128, 2, 768], BF16)
            wkv_b, Bwkv = tile(es, "wkv_b", [128, 1024], BF16)
            wv_b, Bwv = tile(es, "wv_b", [128, 512], BF16)
            vec, Bvec = tile(es, "vecA", [128, NV], F32)
            stg = Ring(nc, es, "wstgA", 2, [128, IN_W], F32)
            S.dma("sp", vec[:, :], vecs[l, :, :], writes=[Bvec])
            mset("pool", win_b[:, :, 1440:1632], 0.0, [Bwin])
            mset("pool", wqr_b[:, :, :], 0.0, [Bwqr])
            for kc in range(8):
                st, Bst = stg.next()
                S.dma("sp", st[:, :], w_in[l, kc * 128:(kc + 1) * 128, :], writes=[Bst])
                g = vec[:, kc:kc + 1]
                ts("dve", win_b[:, kc, 0:1440], st[:, :], g, None, ALU.mult, None, [Bst, Bvec], [Bwin])
                ts("dve", win_b[:, kc, 1504:1536], st[:, 1408:1440], g, None, ALU.mult, None, [Bst, Bvec], [Bwin])
                ts("dve", win_b[:, kc, 1600:1616], st[:, 1424:1440], g, -1.0, ALU.mult, ALU.mult, [Bst, Bvec], [Bwin])
                ts("dve", win_b[:, kc, 1616:1632], st[:, 1408:1424], g, None, ALU.mult, None, [Bst, Bvec], [Bwin])
            for j in range(2):
                st, Bst = stg.next()
                S.dma("sp", st[:, 0:768], w_uq[l, j * 128:(j + 1) * 128, :], writes=[Bst])
                g = vec[:, 8 + j:9 + j]
                ts("dve", wq_b[:, j, :], st[:, 0:768], g, None, ALU.mult, None, [Bst, Bvec], [Bwq])
                sv = st[:, 0:768].rearrange("p (h c) -> p h c", c=96)
                rv = wqr_b[:, j, :].rearrange("p (h c) -> p h c", c=96)
                ts("dve", rv[:, :, 64:80], sv[:, :, 80:96], g, -1.0, ALU.mult, ALU.mult, [Bst, Bvec], [Bwqr])
                ts("dve", rv[:, :, 80:96], sv[:, :, 64:80], g, None, ALU.mult, None, [Bst, Bvec], [Bwqr])
            st, Bst = stg.next()
            S.dma("sp", st[:, 0:1024], w_ukv[l, :, :], writes=[Bst])
            ts("dve", wkv_b[:, :], st[:, 0:1024], vec[:, 10:11], None, ALU.mult, None, [Bst, Bvec], [Bwkv])
            cp("pool", wv_b[:, :].rearrange("p (h c) -> p h c", c=64),
               wkv_b[:, :].rearrange("p (h c) -> p h c", c=128)[:, :, 64:128], [Bwkv], [Bwv])

            tiles = [(r, t0, min(512, LT - t0)) for r in range(VR) for t0 in range(0, LT, 512)]
            hT_r = Ring(nc, es, "hTA", 2, [128, 8, 512], F32)
            cs_r = Ring(nc, es, "csA", 2, [96, 2, 512], F32)
            sq_r = Ring(nc, es, "sqA", 2, [128, 8, 512], BF16)
            hb_r = Ring(nc, es, "hbA", 2, [128, 8, 512], BF16)
            f32_r = Ring(nc, es, "f32A", 6, [128, 512], F32)
            stgo_r = Ring(nc, es, "stgoA", 3, [128, 512], F32)
            b16_r = Ring(nc, es, "b16A", 6, [128, 512], BF16)
            qo_r = Ring(nc, es, "qoA", 3, [96, 512], BF16)
            ko_r = Ring(nc, es, "koA", 3, [96, 512], BF16)
            vo_r = Ring(nc, es, "voA", 2, [128, 512], BF16)
            cqb_r = Ring(nc, es, "cqbA", 2, [128, 2, 512], BF16)
            sqq_r = Ring(nc, es, "sqqA", 2, [128, 2, 512], BF16)
            ckvb_r = Ring(nc, es, "ckvbA", 2, [128, 512], BF16)
            sqkv_r = Ring(nc, es, "sqkvA", 2, [128, 512], BF16)
            per_r = Ring(nc, es, "perA", 10, [96, 512], F32)
            sv_r = Ring(nc, es, "svA", 4, [128, 2], F32)
            qf_r = Ring(nc, es, "qfA", 3, [96, 512], F32)
            sx_r = Ring(nc, es, "sxA", 2, [128, 512], BF16)
            for (sxt, Bsxt) in sx_r.items:
                mset("pool", sxt[:, :], 0.0, [Bsxt])
            hsrc = src.rearrange("(kc p) t -> p kc t", p=128)
            csrc = cs.rearrange("two r t -> r two t")

            def load(i):
                r, t0, N = tiles[i]
                g0 = r * LT + t0
                h, Bh = hT_r.next()
                S.dma("sp", h[:, :, 0:N], hsrc[:, :, g0:g0 + N], writes=[Bh])
                c, Bc = cs_r.next()
                S.dma("sp", c[64:96, :, 0:N], csrc[:, :, g0:g0 + N], writes=[Bc])
                return h, Bh, c, Bc

            def compute(i, h, Bh, c, Bc):
                r, t0, N = tiles[i]
                g0 = r * LT + t0
                if DBG < 1:
                    return
                sq, Bsq = sq_r.next()
                hb, Bhb = hb_r.next()
                for kc in range(8):
                    act(sq[:, kc, 0:N], h[:, kc, 0:N], AF.Square, [Bh], [Bsq])
                ps, Bps = pb()
                for kc in range(8):
                    mm(ps[:, 0:N], ones_b[:, :], sq[:, kc, 0:N], kc == 0, kc == 7, [Bones, Bsq], [Bps])
                tmp, Btmp = f32_r.next()
                R, BR = f32_r.next()
                rstd_from(ps[:, 0:N], 1.0 / D, EPS, tmp[:, 0:N], R[:, 0:N], Bps, Btmp, BR)
                for kc in range(8):
                    tt("dve", hb[:, kc, 0:N], h[:, kc, 0:N], R[:, 0:N], ALU.mult, [Bh, BR], [Bhb])
                for oc in range(8):
                    ps, Bps = pb()
                    for kc in range(8):
                        mm(ps[:, 0:N], win_b[:, kc, oc * 128:(oc + 1) * 128], hb[:, kc, 0:N], kc == 0, kc == 7,
                           [Bwin, Bhb], [Bps])
                    so, Bso = stgo_r.next()
                    if oc % 2 == 0:
                        S.op("act", lambda e, o=so[:, 0:N], p=ps[:, 0:N]: e.copy(out=o, in_=p), [Bps], [Bso])
                    else:
                        cp("dve", so[:, 0:N], ps[:, 0:N], [Bps], [Bso])
                    S.dma("sp", ug_gath[r * D + oc * 128:r * D + (oc + 1) * 128, t0:t0 + N], so[:, 0:N], reads=[Bso])
                if DBG < 2:
                    return
                cqb, Bcqb = cqb_r.next()
                sqq, Bsqq = sqq_r.next()
                for j in range(2):
                    ps, Bps = pb()
                    for kc in range(8):
                        mm(ps[:, 0:N], win_b[:, kc, 1024 + j * 128:1152 + j * 128], hb[:, kc, 0:N], kc == 0, kc == 7,
                           [Bwin, Bhb], [Bps])
                    act(sqq[:, j, 0:N], ps[:, 0:N], AF.Square, [Bps], [Bsqq])
                    acp(cqb[:, j, 0:N], ps[:, 0:N], [Bps], [Bcqb])
                ckvb, Bckvb = ckvb_r.next()
                sqkv, Bsqkv = sqkv_r.next()
                ps, Bps = pb()
                for kc in range(8):
                    mm(ps[:, 0:N], win_b[:, kc, 1280:1408], hb[:, kc, 0:N], kc == 0, kc == 7, [Bwin, Bhb], [Bps])
                act(sqkv[:, 0:N], ps[:, 0:N], AF.Square, [Bps], [Bsqkv])
                acp(ckvb[:, 0:N], ps[:, 0:N], [Bps], [Bckvb])
                if DBG == 2:
                    return
                kr, Bkr = per_r.next()
                krot, Bkrot = per_r.next()
                sqr, Bsqr = b16_r.next()
                ps, Bps = pb()
                for kc in range(8):
                    mm(ps[0:96, 0:N], win_b[:, kc, 1440:1536], hb[:, kc, 0:N], kc == 0, kc == 7, [Bwin, Bhb], [Bps])
                act(sqr[0:96, 0:N], ps[0:96, 0:N], AF.Square, [Bps], [Bsqr])
                acp(kr[64:96, 0:N], ps[64:96, 0:N], [Bps], [Bkr])
                ps, Bps = pb()
                for kc in range(8):
                    mm(ps[0:96, 0:N], win_b[:, kc, 1536:1632], hb[:, kc, 0:N], kc == 0, kc == 7, [Bwin, Bhb], [Bps])
                cp("dve", krot[64:96, 0:N], ps[64:96, 0:N], [Bps], [Bkrot])
                if DBG < 3:
                    return
                Cq, BCq = per_r.next()
                ps, Bps = pb()
                for j in range(2):
                    mm(ps[0:96, 0:N], ones_b[:, 0:96], sqq[:, j, 0:N], j == 0, j == 1, [Bones, Bsqq], [Bps])
                ts("dve", Cq[:, 0:N], ps[0:96, 0:N], 96.0 * EPS / 256.0, 96.0 * EPS * EPS, ALU.mult, ALU.add,
                   [Bps], [BCq])
                S2, BS2 = per_r.next()
                skv, Bskv = per_r.next()
                MR, BMR = per_r.next()
                KRr, BKRr = per_r.next()
                tA, BtA = per_r.next()
                ps, Bps = pb()
                mm(ps[0:96, 0:N], ones_b[:, 0:96], sqkv[:, 0:N], True, True, [Bones, Bsqkv], [Bps])
                ts("dve", tA[:, 0:N], ps[0:96, 0:N], 1.0 / 128.0, EPS, ALU.mult, ALU.add, [Bps], [BtA])
                recip(S2[:, 0:N], tA[:, 0:N], [BtA], [BS2])
                act(skv[:, 0:N], S2[:, 0:N], AF.Sqrt, [BS2], [Bskv])
                ps, Bps = pb()
                mm(ps[0:96, 0:N], onesR_b[:, :], sqr[0:96, 0:N], True, True, [BonesR, Bsqr], [Bps])
                ts("dve", MR[:, 0:N], ps[0:96, 0:N], 96.0 * EPS, None, ALU.add, None, [Bps], [BMR])
                t1, Bt1 = per_r.next()
                t2, Bt2 = per_r.next()
                stt("dve", t1[64:96, 0:N], kr[64:96, 0:N], vec[64:96, 13:14], c[64:96, 0, 0:N], ALU.mult, ALU.mult,
                    [Bkr, Bvec, Bc], [Bt1])
                stt("dve", t2[64:96, 0:N], krot[64:96, 0:N], vec[64:96, 14:15], c[64:96, 1, 0:N], ALU.mult, ALU.mult,
                    [Bkrot, Bvec, Bc], [Bt2])
                tt("pool", KRr[64:96, 0:N], t1[64:96, 0:N], t2[64:96, 0:N], ALU.add, [Bt1, Bt2], [BKRr])
                if DBG < 4:
                    return
                for hh in range(8 if DBG != 5 else 0):
                    pq, Bpq = pb()
                    for j in range(2):
                        mm(pq[0:96, 0:N], wq_b[:, j, hh * 96:(hh + 1) * 96], cqb[:, j, 0:N], j == 0, j == 1,
                           [Bwq, Bcqb], [Bpq])
                    pr, Bpr = pb()
                    for j in range(2):
                        mm(pr[0:96, 0:N], wqr_b[:, j, hh * 96:(hh + 1) * 96], cqb[:, j, 0:N], j == 0, j == 1,
                           [Bwqr, Bcqb], [Bpr])
                    s2, Bs2 = b16_r.next()
                    act(s2[0:96, 0:N], pq[0:96, 0:N], AF.Square, [Bpq], [Bs2])
                    qf, Bqf = qf_r.next()
                    acp(qf[0:96, 0:N], pq[0:96, 0:N], [Bpq], [Bqf])
                    pm, Bpm = pb()
                    mm(pm[0:96, 0:N], ones_b[0:96, 0:96], s2[0:96, 0:N], True, True, [Bones, Bs2], [Bpm])
                    ta, Bta = f32_r.next()
                    tb_, Btb = f32_r.next()
                    rq, Brq = f32_r.next()
                    tt("dve", ta[0:96, 0:N], pm[0:96, 0:N], Cq[:, 0:N], ALU.add, [Bpm, BCq], [Bta])
                    act(tb_[0:96, 0:N], ta[0:96, 0:N], AF.Sqrt, [Bta], [Btb], scale=1.0 / 96.0)
                    recip(rq[0:96, 0:N], tb_[0:96, 0:N], [Btb], [Brq])
                    qo, Bqo = qo_r.next()
                    stt("dve", qo[0:64, 0:N], qf[0:64, 0:N], vec[0:64, 11:12], rq[0:64, 0:N], ALU.mult, ALU.mult,
                        [Bqf, Bvec, Brq], [Bqo])
                    u1, Bu1 = f32_r.next()
                    u2, Bu2 = f32_r.next()
                    stt("dve", u1[64:96, 0:N], qf[64:96, 0:N], vec[64:96, 11:12], c[64:96, 0, 0:N], ALU.mult, ALU.mult,
                        [Bqf, Bvec, Bc], [Bu1])
                    stt("dve", u2[64:96, 0:N], pr[64:96, 0:N], vec[64:96, 12:13], c[64:96, 1, 0:N], ALU.mult, ALU.mult,
                        [Bpr, Bvec, Bc], [Bu2])
                    tt("pool", u1[64:96, 0:N], u1[64:96, 0:N], u2[64:96, 0:N], ALU.add, [Bu2], [Bu1])
                    tt("pool", qo[64:96, 0:N], u1[64:96, 0:N], rq[64:96, 0:N], ALU.mult, [Bu1, Brq], [Bqo])
                    S.dma("sp", q_scr[hh * 96:(hh + 1) * 96, g0:g0 + N], qo[0:96, 0:N], reads=[Bqo])
                    pk, Bpk = pb()
                    mm(pk[:, 0:N], wkv_b[:, hh * 128:(hh + 1) * 128], ckvb[:, 0:N], True, True, [Bwkv, Bckvb], [Bpk])
                    sx, Bsx = sx_r.next()
                    act(sx[0:64, 0:N], pk[0:64, 0:N], AF.Square, [Bpk], [Bsx])
                    kf, Bkf = qf_r.next()
                    acp(kf[0:64, 0:N], pk[0:64, 0:N], [Bpk], [Bkf])
                    pm, Bpm = pb()
                    mm(pm[0:96, 0:N], ones_b[:, 0:96], sx[:, 0:N], True, True, [Bones, Bsx], [Bpm])
                    ta, Bta = f32_r.next()
                    tb_, Btb = f32_r.next()
                    rk, Brk = f32_r.next()
                    tt("dve", ta[0:96, 0:N], pm[0:96, 0:N], S2[:, 0:N], ALU.mult, [Bpm, BS2], [Bta])
                    tt("pool", ta[0:96, 0:N], ta[0:96, 0:N], MR[:, 0:N], ALU.add, [BMR], [Bta])
                    act(tb_[0:96, 0:N], ta[0:96, 0:N], AF.Sqrt, [Bta], [Btb], scale=1.0 / 96.0)
                    recip(rk[0:96, 0:N], tb_[0:96, 0:N], [Btb], [Brk])
                    tt("pool", tb_[0:64, 0:N], rk[0:64, 0:N], skv[0:64, 0:N], ALU.mult, [Brk, Bskv], [Btb])
                    ko, Bko = ko_r.next()
                    stt("dve", ko[0:64, 0:N], kf[0:64, 0:N], vec[0:64, 13:14], tb_[0:64, 0:N], ALU.mult, ALU.mult,
                        [Bkf, Bvec, Btb], [Bko])
                    tt("pool", ko[64:96, 0:N], KRr[64:96, 0:N], rk[64:96, 0:N], ALU.mult, [BKRr, Brk], [Bko])
                    S.dma("sp", k_gath[r * 768 + hh * 96:r * 768 + (hh + 1) * 96, t0:t0 + N], ko[0:96, 0:N], reads=[Bko])
                for j0 in range(0, N, 128):
                    nt = min(128, N - j0)
                    pv, Bpv = pb()
                    mm(pv[:, 0:512], ckvb[:, j0:j0 + 128], wv_b[:, :], True, True, [Bckvb, Bwv], [Bpv])
                    p1, Bp1 = pb()
                    mm(p1[:, 0:2], sqkv[:, j0:j0 + 128], ones_b[:, 0:2], True, True, [Bsqkv, Bones], [Bp1])
                    svt, Bsvt = sv_r.next()
                    act(svt[0:nt, 0:1], p1[0:nt, 0:1], AF.Sqrt, [Bp1], [Bsvt], bias=EPS, scale=1.0 / 128.0)
                    recip(svt[0:nt, 1:2], svt[0:nt, 0:1], [], [Bsvt])
                    vo, Bvo = vo_r.next()
                    ts("dve", vo[0:nt, :], pv[0:nt, 0:512], svt[0:nt, 1:2], None, ALU.mult, None, [Bpv, Bsvt], [Bvo])
                    S.dma("sp", v_gath[r * LT + t0 + j0:r * LT + t0 + j0 + nt, :], vo[0:nt, :], reads=[Bvo])

            cur = load(0)
            for i in range(len(tiles)):
                nxt = load(i + 1) if i + 1 < len(tiles) else None
                compute(i, *cur)
                cur = nxt
            S.flush()

    def phaseC(l):
        with ExitStack() as es:
            nchmax = (TP + 127) // 128
            KT_r = Ring(nc, es, "KT", 2, [96, TP], BF16)
            V_r = Ring(nc, es, "Vt", 2, [128, nchmax, 65], BF16)
            QT_r = Ring(nc, es, "QT", 3, [96, 512], BF16)
            PT_r = Ring(nc, es, "PT", 3, [128, 2, 512], BF16)
            rec_r = Ring(nc, es, "recC", 2, [128, 512], F32)
            for (rt, Brt) in rec_r.items:
                mset("pool", rt[:, :], 0.0, [Brt])
            bcs_r = Ring(nc, es, "bcsC", 2, [65, 512], F32)
            yo_r = Ring(nc, es, "yoC", 2, [65, 512], F32)
            for (V, BV) in V_r.items:
                mset("pool", V[:, :, 0:1], 1.0, [BV])
            stg = PSP[0:2]
            ob = PS[4:6]
            bcb = PS[6:8]
            pend = [None]

            def groups(chs):
                g = []
                i = 0
                while i < len(chs):
                    if i + 1 < len(chs) and chs[i][1] == chs[i + 1][1]:
                        g.append((i, 2))
                        i += 2
                    else:
                        g.append((i, 1))
                        i += 1
                return g
            units = [(s, hh) for s in range(2) for hh in range(8)]

            def chunks(T):
                nfull, rem = T // 128, T % 128
                if rem == 0:
                    return [(i * 128, 128) for i in range(nfull)]
                if rem >= 65 or nfull == 0:
                    return [(i * 128, 128) for i in range(nfull)] + [(nfull * 128, rem)]
                tot = 128 + rem
                a = (tot + 1) // 2
                return [(i * 128, 128) for i in range(nfull - 1)] + [((nfull - 1) * 128, a), ((nfull - 1) * 128 + a, tot - a)]

            def loadKV(s, hh):
                T, L = seq_T[s], seg_len[s]
                KT, BK = KT_r.next()
                V, BV = V_r.next()
                for kc0 in range(0, T, 2048):
                    kn = min(2048, T - kc0)
                    S.dma("sp", KT[0:96, kc0:kc0 + kn],
                          k_gath[hh * 96:(hh + 1) * 96, seg_off[s] + kc0:seg_off[s] + kc0 + kn],
                          reads=[Bkg], writes=[BK])
                chs = chunks(T)
                for r in range(VR):
                    n0, n1 = r * L, (r + 1) * L
                    base = r * LT + seg_off[s]
                    ci = 0
                    while ci < len(chs):
                        cs0, csz = chs[ci]
                        lo, hi = max(cs0, n0), min(cs0 + csz, n1)
                        if lo >= hi:
                            ci += 1
                            continue
                        if lo == cs0 and hi == cs0 + csz and csz == 128:
                            cj = ci
                            while cj < len(chs) and chs[cj][1] == 128 and chs[cj][0] + 128 <= n1:
                                cj += 1
                            nf = min(cj - ci, 8)
                            cj = ci + nf
                            S.dma("sp", V[:, ci:ci + nf, 1:65],
                                  v_gath[base + lo - n0:base + lo - n0 + nf * 128, hh * 64:(hh + 1) * 64]
                                  .rearrange("(c p) d -> p c d", p=128), reads=[Bvg], writes=[BV])
                            ci = cj
                        else:
                            S.dma("sp", V[lo - cs0:hi - cs0, ci, 1:65],
                                  v_gath[base + lo - n0:base + hi - n0, hh * 64:(hh + 1) * 64],
                                  reads=[Bvg], writes=[BV])
                            if hi == cs0 + csz:
                                ci += 1
                            else:
                                break
                return KT, BK, V, BV

            ui = [0]

            def unit(s, hh, KT, BK, V, BV):
                T, L = seq_T[s], seg_len[s]
                chs = chunks(T)
                grp = groups(chs)
                ng = len(grp)
                qts = [(rr_ * LT + seg_off[s] + qq_, min(512, L - qq_)) for rr_ in range(VR) for qq_ in range(0, L, 512)]

                def loadQ(j):
                    qc, nq = qts[j]
                    QT, BQ = QT_r.next()
                    S.dma("sp", QT[0:96, 0:nq], q_scr[hh * 96:(hh + 1) * 96, qc:qc + nq], writes=[BQ])
                    return QT, BQ

                curq = loadQ(0)
                for j, (qc, nq) in enumerate(qts):
                    nxtq = loadQ(j + 1) if j + 1 < len(qts) else None
                    QT, BQ = curq
                    curq = nxtq
                    o, Bo = ob[ui[0] % 2]
                    bc, Bbc = bcb[ui[0] % 2]
                    ui[0] += 1
                    pts = {}
                    for g in range(ng + 2):
                        if g == 2 and pend[0] is not None:
                            pend[0]()
                            pend[0] = None
                        if g < ng:
                            k0, cnt = grp[g]
                            KS = chs[k0][1]
                            st, Bst = stg[g % 2]
                            for c in range(cnt):
                                K0 = chs[k0 + c][0]
                                mm(st[0:KS, c * 512:c * 512 + nq], KT[0:96, K0:K0 + KS], QT[0:96, 0:nq], True, True,
                                   [BK, BQ], [Bst])
                            pt, Bpt = PT_r.next()
                            if cnt == 2:
                                act(pt[0:KS, :, 0:nq], st[:, :].rearrange("p (c n) -> p c n", c=2)[0:KS, :, 0:nq],
                                    AF.Exp, [Bst], [Bpt], scale=SCALE)
                            else:
                                act(pt[0:KS, 0, 0:nq], st[0:KS, 0:nq], AF.Exp, [Bst], [Bpt], scale=SCALE)
                            pts[g] = (pt, Bpt, KS, k0, cnt)
                        if g >= 2:
                            pt, Bpt, KS, k0, cnt = pts.pop(g - 2)
                            for c in range(cnt):
                                kk = k0 + c
                                mm(o[0:65, 0:nq], V[0:KS, kk, 0:65], pt[0:KS, c, 0:nq], kk == 0, kk == len(chs) - 1,
                                   [BV, Bpt], [Bo])

                    def fin(o=o, Bo=Bo, bc=bc, Bbc=Bbc, nq=nq, qc=qc):
                        rec, Brec = rec_r.next()
                        recip(rec[0:1, 0:nq], o[0:1, 0:nq], [Bo], [Brec])
                        mm(bc[0:65, 0:nq], ones_f[:, 0:65], rec[:, 0:nq], True, True, [Bonesf, Brec], [Bbc])
                        bcs, Bbcs = bcs_r.next()
                        acp(bcs[0:65, 0:nq], bc[0:65, 0:nq], [Bbc], [Bbcs])
                        yo, Byo = yo_r.next()
                        tt("dve", yo[0:65, 0:nq], o[0:65, 0:nq], bcs[0:65, 0:nq], ALU.mult, [Bo, Bbcs], [Byo])
                        S.dma("sp", ya_scr[hh * 64:(hh + 1) * 64, qc:qc + nq], yo[1:65, 0:nq], reads=[Byo])
                    if pend[0] is not None:
                        pend[0]()
                    pend[0] = fin

            cur = loadKV(*units[0])
            for i, (s, hh) in enumerate(units):
                nxt = loadKV(*units[i + 1]) if i + 1 < len(units) else None
                unit(s, hh, *cur)
                cur = nxt
            if pend[0] is not None:
                pend[0]()
                pend[0] = None
            S.flush()

    def phaseB(l):
        with ExitStack() as es:
            LM = TP // BR
            vec, Bvec = tile(es, "vecB", [128, NV], F32)
            lw, Blw = tile(es, "lwB", [128, 16, 128], F32)
            bd_b, Bbd = tile(es, "bdB", [128, 16, 128], BF16)
            coef, Bcoef = tile(es, "coefB", [128, 32], F32)
            carry, Bcarry = tile(es, "carryB", [128, 1], F32)
            S.dma("sp", vec[:, :], vecs[l, :, :], writes=[Bvec])
            S.dma("sp", lw[:, :, :], lruw[l].rearrange("f m p q -> p (f m) q"), writes=[Blw])
            cp("dve", bd_b[:, :, :], lw[:, :, :], [Blw], [Bbd])
            act(coef[:, 0:8], vec[:, 67:75], AF.Exp, [Bvec], [Bcoef], scale=-1.0)
            act(coef[:, 8:16], coef[:, 0:8], AF.Ln, [], [Bcoef], bias=1.0)
            ts("dve", coef[:, 16:24], coef[:, 8:16], -8.0, None, ALU.mult, None, [], [Bcoef])
            ts("dve", coef[:, 24:32], coef[:, 8:16], -16.0, None, ALU.mult, None, [], [Bcoef])
            U_r = Ring(nc, es, "UB", 2, [128, LM + 4], F32)
            G_r = Ring(nc, es, "GB", 2, [128, LM], F32)
            HF_r = Ring(nc, es, "HFB", 2, [128, LM], F32)
            H_r = Ring(nc, es, "HB", 2, [128, LM], F32)
            Y_r = Ring(nc, es, "YB", 2, [128, LM], F32)
            xc_r = Ring(nc, es, "xcB", 2, [128, LM], F32)
            xcb_r = Ring(nc, es, "xcbB", 2, [128, LM], BF16)
            rr_r = Ring(nc, es, "rrB", 2, [128, LM], F32)
            ig_r = Ring(nc, es, "igB", 2, [128, LM], F32)
            aa_r = Ring(nc, es, "aaB", 2, [128, LM], F32)
            a2_r = Ring(nc, es, "a2B", 2, [128, LM], F32)
            bb_r = Ring(nc, es, "bbB", 2, [128, LM], F32)
            BHF = {}

            def loads(s_, m, L, r, direction):
                o0 = seg_off[s_]
                U, BU = U_r.next()
                if r == 0:
                    mset("pool", U[:, 0:2], 0.0, [BU])
                if r == BR - 1:
                    mset("pool", U[:, 2 + L:4 + L], 0.0, [BU])
                c0 = o0 + r * L
                S.dma("sp", U[:, 2:2 + L], ug_gath[m * 128:(m + 1) * 128, c0:c0 + L], reads=[Bug], writes=[BU])
                if r > 0:
                    S.dma("sp", U[:, 0:2], ug_gath[m * 128:(m + 1) * 128, c0 - 2:c0], reads=[Bug], writes=[BU])
                if r < BR - 1:
                    S.dma("sp", U[:, 2 + L:4 + L], ug_gath[m * 128:(m + 1) * 128, c0 + L:c0 + L + 2],
                          reads=[Bug], writes=[BU])
                Gt = BG = HF = BHFt = None
                if direction == 1:
                    Gt, BG = G_r.next()
                    S.dma("sp", Gt[:, 0:L], ug_gath[512 + m * 128:512 + (m + 1) * 128, c0:c0 + L],
                          reads=[Bug], writes=[BG])
                    HF, BHFt = HF_r.next()
                    S.dma("sp", HF[:, 0:L], hf_scr[:, r * L:(r + 1) * L], reads=[BHF[(s_, m, r)]], writes=[BHFt])
                return U, BU, Gt, BG, HF, BHFt

            def compute(s_, m, L, r, direction, U, BU, Gt, BG, HF, BHFt):
                o0 = seg_off[s_]
                zi = direction * 4 + m
                xc, Bxc = xc_r.next()
                xcb, Bxcb = xcb_r.next()
                rr, Brr = rr_r.next()
                ig, Big = ig_r.next()
                aa, Baa = aa_r.next()
                a2, Ba2 = a2_r.next()
                bb, Bbb = bb_r.next()
                ts("dve", xc[:, 0:L], U[:, 0:L], vec[:, 31 + m * 4:32 + m * 4], vec[:, 47 + m:48 + m], ALU.mult, ALU.add,
                   [BU, Bvec], [Bxc])
                for tap in range(1, 4):
                    stt("dve", xc[:, 0:L], U[:, tap:tap + L], vec[:, 31 + m * 4 + tap:32 + m * 4 + tap], xc[:, 0:L],
                        ALU.mult, ALU.add, [BU, Bvec], [Bxc])
                cp("pool", xcb[:, 0:L], xc[:, 0:L], [Bxc], [Bxcb])
                for c0 in range(0, L, 512):
                    n = min(512, L - c0)
                    ps, Bps = pb()
                    mm(ps[:, 0:n], bd_b[:, zi, :], xcb[:, c0:c0 + n], True, True, [Bbd, Bxcb], [Bps])
                    act(rr[:, c0:c0 + n], ps[:, 0:n], AF.Sigmoid, [Bps, Bvec], [Brr], bias=vec[:, 51 + zi:52 + zi])
                    ps, Bps = pb()
                    mm(ps[:, 0:n], bd_b[:, 8 + zi, :], xcb[:, c0:c0 + n], True, True, [Bbd, Bxcb], [Bps])
                    act(ig[:, c0:c0 + n], ps[:, 0:n], AF.Sigmoid, [Bps, Bvec], [Big], bias=vec[:, 59 + zi:60 + zi])
                act(aa[:, 0:L], rr[:, 0:L], AF.Exp, [Brr, Bcoef], [Baa], scale=coef[:, 16 + zi:17 + zi])
                act(a2[:, 0:L], rr[:, 0:L], AF.Exp, [Brr, Bcoef], [Ba2], scale=coef[:, 24 + zi:25 + zi])
                ts("dve", a2[:, 0:L], a2[:, 0:L], -1.0, 1.0, ALU.mult, ALU.add, [], [Ba2])
                act(a2[:, 0:L], a2[:, 0:L], AF.Sqrt, [], [Ba2])
                tt("pool", bb[:, 0:L], a2[:, 0:L], ig[:, 0:L], ALU.mult, [Ba2, Big], [Bbb])
                tt("dve", bb[:, 0:L], bb[:, 0:L], xc[:, 0:L], ALU.mult, [Bxc], [Bbb])
                H, BH = H_r.next()
                if direction == 0:
                    S.op("dve", lambda e, o=H[:, 0:L], d0=aa[:, 0:L], d1=bb[:, 0:L]: e.tensor_tensor_scan(
                        out=o, data0=d0, data1=d1, initial=carry[:, 0:1], op0=ALU.mult, op1=ALU.add),
                        [Baa, Bbb, Bcarry], [BH])
                    cp("dve", carry[:, 0:1], H[:, L - 1:L], [BH], [Bcarry])
                    BHF[(s_, m, r)] = Buf(f"hf{s_}_{m}_{r}_{l}")
                    S.dma("sp", hf_scr[:, r * L:(r + 1) * L], H[:, 0:L], reads=[BH], writes=[BHF[(s_, m, r)]], owner=BH)
                else:
                    S.op("dve", lambda e, o=H[:, 0:L][:, ::-1], d0=aa[:, 0:L][:, ::-1], d1=bb[:, 0:L][:, ::-1]: e.tensor_tensor_scan(
                        out=o, data0=d0, data1=d1, initial=carry[:, 0:1], op0=ALU.mult, op1=ALU.add),
                        [Baa, Bbb, Bcarry], [BH])
                    cp("dve", carry[:, 0:1], H[:, 0:1], [BH], [Bcarry])
                    tt("pool", H[:, 0:L], H[:, 0:L], HF[:, 0:L], ALU.add, [BHFt], [BH])
                    act(Gt[:, 0:L], Gt[:, 0:L], AF.Gelu_apprx_tanh, [], [BG])
                    Y, BY = Y_r.next()
                    tt("dve", Y[:, 0:L], H[:, 0:L], Gt[:, 0:L], ALU.mult, [BH, BG], [BY])
                    S.dma("sp", y_gath[m * 128:(m + 1) * 128, o0 + r * L:o0 + (r + 1) * L], Y[:, 0:L], reads=[BY])

            work = []
            for s_ in range(2):
                L = seq_T[s_] // BR
                for m in range(4):
                    for direction in (0, 1):
                        order = list(range(BR)) if direction == 0 else list(range(BR - 1, -1, -1))
                        for idx, r in enumerate(order):
                            work.append((s_, m, L, r, direction, idx == 0))
            ld = loads(*work[0][:5])
            for i, w in enumerate(work):
                nld = None
                pre = i + 1 < len(work) and not (work[i + 1][4] == 1 and w[4] == 0)
                if pre:
                    nld = loads(*work[i + 1][:5])
                if w[5]:
                    mset("dve", carry[:, 0:1], 0.0, [Bcarry])
                compute(*w[:5], *ld)
                if i + 1 < len(work) and not pre:
                    nld = loads(*work[i + 1][:5])
                ld = nld
            S.flush()

    def phaseD1(l, src):
        with ExitStack() as es:
            vec, Bvec = tile(es, "vecD", [128, NV], F32)
            wo_b, Bwo = tile(es, "wo_b", [128, 8, D], BF16)
            stg = Ring(nc, es, "wstgD", 2, [128, D], F32)
            S.dma("sp", vec[:, :], vecs[l, :, :], writes=[Bvec])
            for kc in range(8):
                st, Bst = stg.next()
                S.dma("sp", st[:, :], w_out[l, kc * 128:(kc + 1) * 128, :], writes=[Bst])
                ts("dve", wo_b[:, kc, :], st[:, :], vec[:, 15 + kc:16 + kc], None, ALU.mult, None, [Bst, Bvec], [Bwo])
            tiles = [(t0, min(512, NTOK - t0)) for t0 in range(0, NTOK, 512)]
            hT_r = Ring(nc, es, "hTD", 2, [128, 8, 512], F32)
            Y_r = Ring(nc, es, "YD", 2, [128, 8, 512], F32)
            sq_r = Ring(nc, es, "sqD", 2, [128, 8, 512], BF16)
            yb_r = Ring(nc, es, "ybD", 2, [128, 8, 512], BF16)
            f32_r = Ring(nc, es, "f32D", 4, [128, 512], F32)
            hsrc = src.rearrange("(kc p) t -> p kc t", p=128)
            hdst = hT_scr.rearrange("(kc p) t -> p kc t", p=128)
            ygv = y_gath.rearrange("(kc p) t -> p kc t", p=128)
            yav = ya_scr.rearrange("(kc p) t -> p kc t", p=128)
            def load(i):
                t0, N = tiles[i]
                h, Bh = hT_r.next()
                S.dma("sp", h[:, :, 0:N], hsrc[:, :, t0:t0 + N], writes=[Bh])
                Y, BY = Y_r.next()
                S.dma("sp", Y[:, 0:4, 0:N], ygv[:, :, t0:t0 + N], reads=[Byg], writes=[BY])
                S.dma("sp", Y[:, 4:8, 0:N], yav[:, :, t0:t0 + N], writes=[BY])
                return h, Bh, Y, BY

            def compute(i, h, Bh, Y, BY):
                t0, N = tiles[i]
                sq, Bsq = sq_r.next()
                yb, Byb = yb_r.next()
                for kc in range(8):
                    act(sq[:, kc, 0:N], Y[:, kc, 0:N], AF.Square, [BY], [Bsq])
                Rs = []
                for grp in range(2):
                    ps, Bps = pb()
                    for kc in range(4):
                        mm(ps[:, 0:N], ones_b[:, :], sq[:, grp * 4 + kc, 0:N], kc == 0, kc == 3, [Bones, Bsq], [Bps])
                    tmp, Btmp = f32_r.next()
                    R, BR = f32_r.next()
                    rstd_from(ps[:, 0:N], 1.0 / 512.0, EPS, tmp[:, 0:N], R[:, 0:N], Bps, Btmp, BR)
                    Rs.append((R, BR))
                for kc in range(8):
                    R, BR = Rs[kc // 4]
                    tt("dve" if kc % 2 == 0 else "pool", yb[:, kc, 0:N], Y[:, kc, 0:N], R[:, 0:N], ALU.mult,
                       [BY, BR], [Byb])
                for oc in range(8):
                    ps, Bps = pb()
                    for kc in range(8):
                        mm(ps[:, 0:N], wo_b[:, kc, oc * 128:(oc + 1) * 128], yb[:, kc, 0:N], kc == 0, kc == 7,
                           [Bwo, Byb], [Bps])
                    tt("dve", h[:, oc, 0:N], h[:, oc, 0:N], ps[:, 0:N], ALU.add, [Bps], [Bh])
                S.dma("sp", hdst[:, :, t0:t0 + N], h[:, :, 0:N], reads=[Bh])

            cur = load(0)
            for i in range(len(tiles)):
                nxt = load(i + 1) if i + 1 < len(tiles) else None
                compute(i, *cur)
                cur = nxt
            S.flush()

    def phaseD2(l, dst):
        with ExitStack() as es:
            NT = 256
            vec, Bvec = tile(es, "vecE", [128, NV], F32)
            wu_b, Bwu = tile(es, "wu_b", [128, 8, 4 * D], BF16)
            wd_b, Bwd = tile(es, "wd_b", [128, 32, D], BF16)
            S.dma("sp", vec[:, :], vecs[l, :, :], writes=[Bvec])
            with ExitStack() as es2:
                stg = Ring(nc, es2, "wstgE", 2, [128, 4 * D], F32)
                for kc in range(8):
                    st, Bst = stg.next()
                    S.dma("sp", st[:, :], w_up[l, kc * 128:(kc + 1) * 128, :], writes=[Bst])
                    ts("dve" if kc % 2 == 0 else "pool", wu_b[:, kc, :], st[:, :], vec[:, 23 + kc:24 + kc], None,
                       ALU.mult, None, [Bst, Bvec], [Bwu])
                for f4 in range(8):
                    st, Bst = stg.next()
                    S.dma("sp", st[:, :].rearrange("p (f n) -> p f n", n=D),
                          w_down[l, f4 * 512:(f4 + 1) * 512, :].rearrange("(f p) n -> p f n", p=128), writes=[Bst])
                    cp("dve" if f4 % 2 == 0 else "pool", wd_b[:, f4 * 4:(f4 + 1) * 4, :],
                       st[:, :].rearrange("p (f n) -> p f n", n=D), [Bst], [Bwd])
                S.flush()
            tiles = [(t0, min(NT, NTOK - t0)) for t0 in range(0, NTOK, NT)]
            hT_r = Ring(nc, es, "hTE", 2, [128, 8, NT], F32)
            sq_r = Ring(nc, es, "sqE", 2, [128, 8, NT], BF16)
            hb_r = Ring(nc, es, "hbE", 2, [128, 8, NT], BF16)
            ac_r = Ring(nc, es, "acE", 2, [128, 32, NT], BF16)
            f32_r = Ring(nc, es, "f32E", 6, [128, NT], F32)
            hsrc = hT_scr.rearrange("(kc p) t -> p kc t", p=128)
            hdst = dst.rearrange("(kc p) t -> p kc t", p=128)

            def load(i):
                t0, N = tiles[i]
                h, Bh = hT_r.next()
                S.dma("sp", h[:, :, 0:N], hsrc[:, :, t0:t0 + N], writes=[Bh])
                return h, Bh

            def compute(i, h, Bh):
                t0, N = tiles[i]
                sq, Bsq = sq_r.next()
                hb, Bhb = hb_r.next()
                for kc in range(8):
                    act(sq[:, kc, 0:N], h[:, kc, 0:N], AF.Square, [Bh], [Bsq])
                ps, Bps = pb()
                for kc in range(8):
                    mm(ps[:, 0:N], ones_b[:, :], sq[:, kc, 0:N], kc == 0, kc == 7, [Bones, Bsq], [Bps])
                tmp, Btmp = f32_r.next()
                R, BR = f32_r.next()
                rstd_from(ps[:, 0:N], 1.0 / D, EPS, tmp[:, 0:N], R[:, 0:N], Bps, Btmp, BR)
                for kc in range(8):
                    tt("dve" if kc % 2 == 0 else "pool", hb[:, kc, 0:N], h[:, kc, 0:N], R[:, 0:N], ALU.mult,
                       [Bh, BR], [Bhb])
                ac, Bac = ac_r.next()
                for fc in range(32):
                    ps, Bps = pb()
                    for kc in range(8):
                        mm(ps[:, 0:N], wu_b[:, kc, fc * 128:(fc + 1) * 128], hb[:, kc, 0:N], kc == 0, kc == 7,
                           [Bwu, Bhb], [Bps])
                    rl, Brl = f32_r.next()
                    act(rl[:, 0:N], ps[:, 0:N], AF.Relu, [Bps], [Brl])
                    tt("dve" if fc % 2 == 0 else "pool", ac[:, fc, 0:N], rl[:, 0:N], rl[:, 0:N], ALU.mult, [Brl], [Bac])
                for oc in range(8):
                    ps, Bps = pb()
                    for fc in range(32):
                        mm(ps[:, 0:N], wd_b[:, fc, oc * 128:(oc + 1) * 128], ac[:, fc, 0:N], fc == 0, fc == 31,
                           [Bwd, Bac], [Bps])
                    tt("dve", h[:, oc, 0:N], h[:, oc, 0:N], ps[:, 0:N], ALU.add, [Bps], [Bh])
                S.dma("sp", hdst[:, :, t0:t0 + N], h[:, :, 0:N], reads=[Bh])

            cur = load(0)
            for i in range(len(tiles)):
                nxt = load(i + 1) if i + 1 < len(tiles) else None
                compute(i, *cur)
                cur = nxt
            S.flush()

    for l in range(NLAYER):
        src = xT if l == 0 else hT_scr
        if "A" in PHASES:
            phaseA(l, src)
        if "C" in PHASES:
            phaseC(l)
        if "B" in PHASES:
            phaseB(l)
        if "D" in PHASES:
            phaseD1(l, src)
        if "E" in PHASES:
            phaseD2(l, yT if l == NLAYER - 1 else hT_scr)
    S.flush()
    G.close()
    return nc, dict(LT=LT, NTOK=NTOK, seg_len=seg_len, seg_off=seg_off, seq_T=seq_T, LP=LP, LS=LS, TP=TP, TS=TS)


def prep(inp, SEQ, DSEQ, meta):
    LT, NTOK, seg_len, seg_off = meta["LT"], meta["NTOK"], meta["seg_len"], meta["seg_off"]
    f = lambda k: np.asarray(inp[k], dtype=np.float32)
    xp, xs, mt = f("x_prompt"), f("x_sample"), f("meta_tokens")
    nP, nS = xp.shape[0], xs.shape[0]
    inv_freq = (1.0 / (10000.0 ** (np.arange(0, 32, 2, dtype=np.float32) / np.float32(32)))).astype(np.float32)
    qg, kg = f("qk_q_g"), f("qk_k_g")

    def swp(g):
        o = np.zeros(96, np.float32)
        o[64:80] = g[80:96]
        o[80:96] = g[64:80]
        return o

    vecs = np.zeros((2, 128, NV), np.float32)
    lruw = np.zeros((2, 4, 4, 128, 128), np.float32)
    for l in range(2):
        vecs[l, :, 0:8] = f("norm_mix_g")[l].reshape(8, 128).T
        vecs[l, :, 8:10] = f("q_norm_g")[l].reshape(2, 128).T
        vecs[l, :, 10] = f("kv_norm_g")[l]
        vecs[l, 0:96, 11] = qg[l]
        vecs[l, 0:96, 12] = swp(qg[l])
        vecs[l, 0:96, 13] = kg[l]
        vecs[l, 0:96, 14] = swp(kg[l])
        vecs[l, :, 15:19] = f("out_norm_lru_g")[l].reshape(4, 128).T
        vecs[l, :, 19:23] = f("out_norm_attn_g")[l].reshape(4, 128).T
        vecs[l, :, 23:31] = f("norm_ff_g")[l].reshape(8, 128).T
        for m in range(4):
            ch = slice(m * 128, (m + 1) * 128)
            vecs[l, :, 31 + m * 4:35 + m * 4] = f("conv_w")[l][:, ch].T
            vecs[l, :, 47 + m] = f("conv_b")[l][ch]
            for z in range(2):
                vecs[l, :, 51 + z * 4 + m] = f("lru_ba")[l, z][ch]
                vecs[l, :, 59 + z * 4 + m] = f("lru_bx")[l, z][ch]
                vecs[l, :, 67 + z * 4 + m] = f("lru_lambda")[l, z][ch]
                for half in range(2):
                    hs = slice(half * 64, (half + 1) * 64)
                    lruw[l, z, m, hs, hs] = f("lru_wa")[l, z, 2 * m + half]
                    lruw[l, 2 + z, m, hs, hs] = f("lru_wx")[l, z, 2 * m + half]
    shared = {"vecs": vecs, "lruw": lruw, "w_in": f("w_in"), "w_uq": f("w_uq"), "w_ukv": f("w_ukv"),
              "w_out": f("w_out"), "w_up": f("w_up"), "w_down": f("w_down")}
    in_maps = []
    cache = {}
    for c in range(NCORE):
        key = (c % nP, c % nS)
        if key not in cache:
            seqs = [np.concatenate([mt, xp[key[0]]], 0), np.concatenate([mt, xs[key[1]]], 0)]
            xT = np.empty((D, NTOK), np.float32)
            pos = np.empty(NTOK, np.float32)
            for r in range(VR):
                for s in range(2):
                    L = seg_len[s]
                    c0 = r * LT + seg_off[s]
                    xT[:, c0:c0 + L] = seqs[s][r * L:(r + 1) * L].T
                    pos[c0:c0 + L] = np.arange(r * L, (r + 1) * L, dtype=np.float32)
            ang = pos[None, :] * inv_freq[:, None]
            cs = np.empty((2, 32, NTOK), np.float32)
            cs[0, 0:16] = np.cos(ang)
            cs[0, 16:32] = np.cos(ang)
            cs[1, 0:16] = np.sin(ang)
            cs[1, 16:32] = np.sin(ang)
            cache[key] = (xT, cs)
        xT, cs = cache[key]
        m = {"xT": xT, "cs": cs}
        m.update(shared)
        in_maps.append(m)
    return in_maps


_CACHE = {}


def run(inp, SEQ, DSEQ, trace=False):
    key = (SEQ, DSEQ)
    if key not in _CACHE:
        _CACHE[key] = build(SEQ, DSEQ)
    nc, meta = _CACHE[key]
    in_maps = prep(inp, SEQ, DSEQ, meta)
    res = run_bass_kernel_spmd(nc, in_maps, core_ids=list(range(NCORE)), **({"trace": True} if trace else {}))
    B, DB = inp["x_prompt"].shape[0], inp["x_sample"].shape[0]
    TP, TS, LT, LP, LS = meta["TP"], meta["TS"], meta["LT"], meta["LP"], meta["LS"]
    yp = np.empty((B, TP, D), np.float32)
    ys = np.empty((DB, TS, D), np.float32)
    for b in range(B):
        yT = np.asarray(res.results[b]["yT"])
        for r in range(VR):
            yp[b, r * LP:(r + 1) * LP] = yT[:, r * LT:r * LT + LP].T
    for b in range(DB):
        yT = np.asarray(res.results[b]["yT"])
        for r in range(VR):
            ys[b, r * LS:(r + 1) * LS] = yT[:, r * LT + LP:r * LT + LP + LS].T
    return (np.ascontiguousarray(yp[:, 16:]), np.ascontiguousarray(ys[:, 16:])), res


def kernel(**inputs):
    SEQ = inputs["x_prompt"].shape[1]
    DSEQ = inputs["x_sample"].shape[1]
    out, _ = run(inputs, SEQ, DSEQ)
    return out
```

```python
import numpy as np
from contextlib import ExitStack
import concourse.bass as bass
import concourse.mybir as mybir
from concourse.bass_utils import run_bass_kernel_spmd

F32 = mybir.dt.float32
BF16 = mybir.dt.bfloat16
AF = mybir.ActivationFunctionType
ALU = mybir.AluOpType

D = 1024
NCORE = 8
VR = 1
BR = 8
NV = 75
EPS = 1e-6
IN_W = 1440
SCALE = 96 ** -0.5
PHASES = "AGCBDE"
NLAYER = 2
DBG = 99
VAR = 0


class Buf:
    __slots__ = ("name", "w", "r", "dsem")

    def __init__(self, name):
        self.name = name
        self.w = None
        self.r = []
        self.dsem = None


class Eng:
    def __init__(self, name, sem):
        self.name = name
        self.sem = sem
        self.cnt = 0
        self.known = {}
        self.insts = []


class Sched:
    def __init__(self, nc):
        self.nc = nc
        self.sems = {}
        self.E = {}
        for name in ("pe", "act", "dve", "pool", "sp"):
            self.sems[name] = nc.alloc_semaphore(name="sem_" + name)
            self.E[name] = Eng(name, name)
        self.dval = {}

    def _waits(self, eng, reads, writes):
        need = {}
        for b in reads:
            if b.w is not None:
                k, v = b.w
                if need.get(k, 0) < v:
                    need[k] = v
        for b in writes:
            if b.w is not None:
                k, v = b.w
                if need.get(k, 0) < v:
                    need[k] = v
            for (k, v) in b.r:
                if need.get(k, 0) < v:
                    need[k] = v
        out = []
        for k, v in need.items():
            if eng.name == "pe" and k == "pe":
                continue
            if eng.known.get(k, 0) >= v:
                continue
            eng.known[k] = v
            out.append((k, v))
        return out

    def _mark(self, tag, reads, writes):
        for b in reads:
            if len(b.r) > 64:
                m = {}
                for (k, v) in b.r:
                    if m.get(k, 0) < v:
                        m[k] = v
                b.r = list(m.items())
            b.r.append(tag)
        for b in writes:
            b.w = tag
            b.r = []

    def op(self, en, fn, reads=(), writes=()):
        eng = self.E[en]
        waits = self._waits(eng, reads, writes)
        eng.cnt += 1
        tag = (eng.sem, eng.cnt)
        eng.insts.append((waits, fn, (eng.sem, 1)))
        self._mark(tag, reads, writes)
        return tag

    def _own(self, owner):
        if owner.dsem is None:
            key = "d_" + owner.name
            if key not in self.sems:
                self.sems[key] = self.nc.alloc_semaphore(name=key)
                self.dval[key] = 0
            owner.dsem = key

    def dma(self, en, out, in_, reads=(), writes=(), owner=None, slow=False):
        eng = self.E[en]
        waits = self._waits(eng, reads, writes)
        if owner is None:
            owner = writes[0] if writes else reads[0]
        self._own(owner)
        self.dval[owner.dsem] += 16
        tag = (owner.dsem, self.dval[owner.dsem])
        kw = {"allow_slow_non_contiguous": True} if slow else {}
        def _f(e, o=out, i=in_):
            try:
                return e.dma_start(out=o, in_=i, **kw)
            except Exception:
                print("DMA FAIL", en, o, i)
                raise
        eng.insts.append((waits, _f, (owner.dsem, 16)))
        self._mark(tag, reads, writes)
        return tag

    def coll(self, ins, outs, reads, writes, owner):
        eng = self.E["pool"]
        waits = self._waits(eng, reads, writes)
        self._own(owner)
        self.dval[owner.dsem] += 1
        tag = (owner.dsem, self.dval[owner.dsem])
        rg = [list(range(NCORE))]
        eng.insts.append((waits, (lambda e: e.collective_compute("AllGather", ALU.bypass, replica_groups=rg,
                                                                 ins=ins, outs=outs)), (owner.dsem, None)))
        self._mark(tag, reads, writes)
        return tag

    def flush(self):
        eng = self.E["sp"]
        dr = []
        for key, val in self.dval.items():
            if val > 0 and eng.known.get(key, 0) < val:
                eng.known[key] = val
                dr.append((key, val))
        if dr:
            eng.insts.append((dr, None, None))
        sems = self.sems
        with self.nc.Block() as block:
            for name, attr in (("pe", "tensor"), ("act", "scalar"), ("dve", "vector"),
                               ("pool", "gpsimd"), ("sp", "sync")):
                e = self.E[name]
                if not e.insts:
                    continue

                def body(be, insts=e.insts):
                    for waits, fn, inc in insts:
                        for (k, v) in waits:
                            be.wait_ge(sems[k], v)
                        if fn is None:
                            continue
                        ins = fn(be)
                        if inc is not None:
                            if inc[1] is None:
                                ins.then_inc(sems[inc[0]])
                            else:
                                ins.then_inc(sems[inc[0]], inc[1])
                getattr(block, attr)(body)
                e.insts = []


_UID = [0]


def _uname(name):
    _UID[0] += 1
    return f"{name}_u{_UID[0]}"


class Ring:
    def __init__(self, nc, es, name, n, shape, dtype):
        self.items = []
        for i in range(n):
            t = es.enter_context(nc.sbuf_tensor(_uname(f"{name}{i}"), shape, dtype))
            self.items.append((t, Buf(f"{name}{i}")))
        self.i = 0

    def next(self):
        it = self.items[self.i % len(self.items)]
        self.i += 1
        return it


def build(SEQ, DSEQ):
    TP, TS = SEQ + 16, DSEQ + 16
    LP, LS = TP // VR, TS // VR
    assert LP * VR == TP and LS * VR == TS and TP % BR == 0 and TS % BR == 0 and VR == 1
    seg_len = [LP, LS]
    seq_T = [TP, TS]
    seg_off = [0, LP]
    LT = LP + LS
    NTOK = VR * LT

    nc = bass.Bass("TRN2", target_bir_lowering=False)

    def din(name, shape, dtype=F32):
        return nc.dram_tensor(name, shape, dtype, kind="ExternalInput").ap()

    xT = din("xT", [D, NTOK])
    cs = din("cs", [2, 32, NTOK])
    vecs = din("vecs", [2, 128, NV])
    lruw = din("lruw", [2, 4, 4, 128, 128])
    w_in = din("w_in", [2, D, IN_W])
    w_uq = din("w_uq", [2, 256, 768])
    w_ukv = din("w_ukv", [2, 128, 1024])
    w_out = din("w_out", [2, D, D])
    w_up = din("w_up", [2, D, 4 * D])
    w_down = din("w_down", [2, 4 * D, D])
    yT = nc.dram_tensor("yT", [D, NTOK], F32, kind="ExternalOutput").ap()

    def dscr(name, shape, dtype=F32):
        return nc.dram_tensor(name, shape, dtype, kind="Internal")

    hT_scr = dscr("hT_scr", [D, NTOK]).ap()
    ug_gath = dscr("ug_gath", [VR * D, LT]).ap()
    k_gath = dscr("k_gath", [VR * 768, LT], BF16).ap()
    v_gath = dscr("v_gath", [VR * LT, 512], BF16).ap()
    q_scr = dscr("q_scr", [768, NTOK], BF16).ap()
    hf_scr = dscr("hf_scr", [128, TP]).ap()
    y_gath = dscr("y_gath", [512, VR * LT]).ap()
    ya_scr = dscr("ya_scr", [512, NTOK]).ap()
    Bug, Bkg, Bvg, Byg = Buf("ug_gath"), Buf("k_gath"), Buf("v_gath"), Buf("y_gath")

    S = Sched(nc)
    G = ExitStack()

    def tile(es, name, shape, dtype):
        t = es.enter_context(nc.sbuf_tensor(_uname(name), shape, dtype))
        return t, Buf(name)

    PS = []
    PSP = []
    for i in range(4):
        t = G.enter_context(nc.psum_tensor(f"psp{i}", [128, 1024], F32))
        PSP.append((t, Buf(f"psp{i}")))
        PS.append((t[:, 0:512], Buf(f"ps{2 * i}")))
        PS.append((t[:, 512:1024], Buf(f"ps{2 * i + 1}")))
    pbi = [0]

    def pb():
        it = PS[pbi[0] % 8]
        pbi[0] += 1
        return it

    ones_b, Bones = tile(G, "ones_b", [128, 128], BF16)
    onesR_b, BonesR = tile(G, "onesR_b", [96, 96], BF16)
    ones_f, Bonesf = tile(G, "ones_f", [128, 128], F32)
    S.op("pool", lambda e: e.memset(ones_b[:, :], 1.0), writes=[Bones])
    S.op("pool", lambda e: e.memset(onesR_b[:, :], 0.0), writes=[BonesR])
    S.op("pool", lambda e: e.memset(onesR_b[64:96, :], 1.0), writes=[BonesR])
    S.op("pool", lambda e: e.memset(ones_f[:, :], 0.0), writes=[Bonesf])
    S.op("pool", lambda e: e.memset(ones_f[0:1, :], 1.0), writes=[Bonesf])

    def mm(out, lhsT, rhs, start, stop, reads, writes):
        S.op("pe", lambda e: e.matmul(out, lhsT=lhsT, rhs=rhs, start=start, stop=stop), reads, writes)

    def act(out, in_, func, reads, writes, bias=None, scale=None):
        kw = {}
        if bias is not None:
            kw["bias"] = bias
        if scale is not None:
            kw["scale"] = scale
        S.op("act", lambda e: e.activation(out=out, in_=in_, func=func, **kw), reads, writes)

    def tt(en, out, in0, in1, op, reads, writes):
        S.op(en, lambda e: e.tensor_tensor(out=out, in0=in0, in1=in1, op=op), reads, writes)

    def ts(en, out, in0, s1, s2, op0, op1, reads, writes):
        if s2 is None:
            S.op(en, lambda e: e.tensor_scalar(out=out, in0=in0, scalar1=s1, scalar2=None, op0=op0), reads, writes)
        else:
            S.op(en, lambda e: e.tensor_scalar(out=out, in0=in0, scalar1=s1, scalar2=s2, op0=op0, op1=op1),
                 reads, writes)

    def stt(en, out, in0, scalar, in1, op0, op1, reads, writes):
        S.op(en, lambda e: e.scalar_tensor_tensor(out=out, in0=in0, scalar=scalar, in1=in1, op0=op0, op1=op1),
             reads, writes)

    def cp(en, out, in_, reads, writes):
        S.op(en, lambda e: e.tensor_copy(out=out, in_=in_), reads, writes)

    def acp(out, in_, reads, writes):
        S.op("act", lambda e: e.copy(out=out, in_=in_), reads, writes)

    def recip(out, in_, reads, writes):
        S.op("dve", lambda e: e.reciprocal(out=out, in_=in_), reads, writes)

    def mset(en, ap, val, writes):
        S.op(en, lambda e: e.memset(ap, val), (), writes)

    def rstd_from(ps_ap, scale, bias_c, tmp_ap, out_ap, Bps, Btmp, Bout):
        act(tmp_ap, ps_ap, AF.Sqrt, [Bps], [Btmp], bias=bias_c, scale=scale)
        recip(out_ap, tmp_ap, [Btmp], [Bout])

    def phaseA(l, src):
        with ExitStack() as es:
            win_b, Bwin = tile(es, "win_b", [128, 8, 1632], BF16)
            wq_b, Bwq = tile(es, "wq_b", [128, 2, 768], BF16)
            wqr_b, Bwqr = tile(es, "wqr_b", [128, 2, 768], BF16)
            wkv_b, Bwkv = tile(es, "wkv_b", [128, 1024], BF16)
            wv_b, Bwv = tile(es, "wv_b", [128, 512], BF16)
            vec, Bvec = tile(es, "vecA", [128, NV], F32)
            stg = Ring(nc, es, "wstgA", 2, [128, IN_W], F32)
            S.dma("sp", vec[:, :], vecs[l, :, :], writes=[Bvec])
            mset("pool", win_b[:, :, 1440:1632], 0.0, [Bwin])
            mset("pool", wqr_b[:, :, :], 0.0, [Bwqr])
            for kc in range(8):
                st, Bst = stg.next()
                S.dma("sp", st[:, :], w_in[l, kc * 128:(kc + 1) * 128, :], writes=[Bst])
                g = vec[:, kc:kc + 1]
                ts("dve", win_b[:, kc, 0:1440], st[:, :], g, None, ALU.mult, None, [Bst, Bvec], [Bwin])
                ts("dve", win_b[:, kc, 1504:1536], st[:, 1408:1440], g, None, ALU.mult, None, [Bst, Bvec], [Bwin])
                ts("dve", win_b[:, kc, 1600:1616], st[:, 1424:1440], g, -1.0, ALU.mult, ALU.mult, [Bst, Bvec], [Bwin])
                ts("dve", win_b[:, kc, 1616:1632], st[:, 1408:1424], g, None, ALU.mult, None, [Bst, Bvec], [Bwin])
            for j in range(2):
                st, Bst = stg.next()
                S.dma("sp", st[:, 0:768], w_uq[l, j * 128:(j + 1) * 128, :], writes=[Bst])
                g = vec[:, 8 + j:9 + j]
                ts("dve", wq_b[:, j, :], st[:, 0:768], g, None, ALU.mult, None, [Bst, Bvec], [Bwq])
                sv = st[:, 0:768].rearrange("p (h c) -> p h c", c=96)
                rv = wqr_b[:, j, :].rearrange("p (h c) -> p h c", c=96)
                ts("dve", rv[:, :, 64:80], sv[:, :, 80:96], g, -1.0, ALU.mult, ALU.mult, [Bst, Bvec], [Bwqr])
                ts("dve", rv[:, :, 80:96], sv[:, :, 64:80], g, None, ALU.mult, None, [Bst, Bvec], [Bwqr])
            st, Bst = stg.next()
            S.dma("sp", st[:, 0:1024], w_ukv[l, :, :], writes=[Bst])
            ts("dve", wkv_b[:, :], st[:, 0:1024], vec[:, 10:11], None, ALU.mult, None, [Bst, Bvec], [Bwkv])
            cp("pool", wv_b[:, :].rearrange("p (h c) -> p h c", c=64),
               wkv_b[:, :].rearrange("p (h c) -> p h c", c=128)[:, :, 64:128], [Bwkv], [Bwv])

            tiles = [(r, t0, min(512, LT - t0)) for r in range(VR) for t0 in range(0, LT, 512)]
            hT_r = Ring(nc, es, "hTA", 2, [128, 8, 512], F32)
            cs_r = Ring(nc, es, "csA", 2, [96, 2, 512], F32)
            sq_r = Ring(nc, es, "sqA", 2, [128, 8, 512], BF16)
            hb_r = Ring(nc, es, "hbA", 2, [128, 8, 512], BF16)
            f32_r = Ring(nc, es, "f32A", 6, [128, 512], F32)
            stgo_r = Ring(nc, es, "stgoA", 3, [128, 512], F32)
            b16_r = Ring(nc, es, "b16A", 6, [128, 512], BF16)
            qo_r = Ring(nc, es, "qoA", 3, [96, 512], BF16)
            ko_r = Ring(nc, es, "koA", 3, [96, 512], BF16)
            vo_r = Ring(nc, es, "voA", 2, [128, 512], BF16)
            cqb_r = Ring(nc, es, "cqbA", 2, [128, 2, 512], BF16)
            sqq_r = Ring(nc, es, "sqqA", 2, [128, 2, 512], BF16)
            ckvb_r = Ring(nc, es, "ckvbA", 2, [128, 512], BF16)
            sqkv_r = Ring(nc, es, "sqkvA", 2, [128, 512], BF16)
            per_r = Ring(nc, es, "perA", 10, [96, 512], F32)
            sv_r = Ring(nc, es, "svA", 4, [128, 2], F32)
            qf_r = Ring(nc, es, "qfA", 3, [96, 512], F32)
            sx_r = Ring(nc, es, "sxA", 2, [128, 512], BF16)
            for (sxt, Bsxt) in sx_r.items:
                mset("pool", sxt[:, :], 0.0, [Bsxt])
            hsrc = src.rearrange("(kc p) t -> p kc t", p=128)
            csrc = cs.rearrange("two r t -> r two t")

            def load(i):
                r, t0, N = tiles[i]
                g0 = r * LT + t0
                h, Bh = hT_r.next()
                S.dma("sp", h[:, :, 0:N], hsrc[:, :, g0:g0 + N], writes=[Bh])
                c, Bc = cs_r.next()
                S.dma("sp", c[64:96, :, 0:N], csrc[:, :, g0:g0 + N], writes=[Bc])
                return h, Bh, c, Bc

            def compute(i, h, Bh, c, Bc):
                r, t0, N = tiles[i]
                g0 = r * LT + t0
                if DBG < 1:
                    return
                sq, Bsq = sq_r.next()
                hb, Bhb = hb_r.next()
                for kc in range(8):
                    act(sq[:, kc, 0:N], h[:, kc, 0:N], AF.Square, [Bh], [Bsq])
                ps, Bps = pb()
                for kc in range(8):
                    mm(ps[:, 0:N], ones_b[:, :], sq[:, kc, 0:N], kc == 0, kc == 7, [Bones, Bsq], [Bps])
                tmp, Btmp = f32_r.next()
                R, BR = f32_r.next()
                rstd_from(ps[:, 0:N], 1.0 / D, EPS, tmp[:, 0:N], R[:, 0:N], Bps, Btmp, BR)
                for kc in range(8):
                    tt("dve", hb[:, kc, 0:N], h[:, kc, 0:N], R[:, 0:N], ALU.mult, [Bh, BR], [Bhb])
                for oc in range(8):
                    ps, Bps = pb()
                    for kc in range(8):
                        mm(ps[:, 0:N], win_b[:, kc, oc * 128:(oc + 1) * 128], hb[:, kc, 0:N], kc == 0, kc == 7,
                           [Bwin, Bhb], [Bps])
                    so, Bso = stgo_r.next()
                    if oc % 2 == 0:
                        S.op("act", lambda e, o=so[:, 0:N], p=ps[:, 0:N]: e.copy(out=o, in_=p), [Bps], [Bso])
                    else:
                        cp("dve", so[:, 0:N], ps[:, 0:N], [Bps], [Bso])
                    S.dma("sp", ug_gath[r * D + oc * 128:r * D + (oc + 1) * 128, t0:t0 + N], so[:, 0:N], reads=[Bso])
                if DBG < 2:
                    return
                cqb, Bcqb = cqb_r.next()
                sqq, Bsqq = sqq_r.next()
                for j in range(2):
                    ps, Bps = pb()
                    for kc in range(8):
                        mm(ps[:, 0:N], win_b[:, kc, 1024 + j * 128:1152 + j * 128], hb[:, kc, 0:N], kc == 0, kc == 7,
                           [Bwin, Bhb], [Bps])
                    act(sqq[:, j, 0:N], ps[:, 0:N], AF.Square, [Bps], [Bsqq])
                    acp(cqb[:, j, 0:N], ps[:, 0:N], [Bps], [Bcqb])
                ckvb, Bckvb = ckvb_r.next()
                sqkv, Bsqkv = sqkv_r.next()
                ps, Bps = pb()
                for kc in range(8):
                    mm(ps[:, 0:N], win_b[:, kc, 1280:1408], hb[:, kc, 0:N], kc == 0, kc == 7, [Bwin, Bhb], [Bps])
                act(sqkv[:, 0:N], ps[:, 0:N], AF.Square, [Bps], [Bsqkv])
                acp(ckvb[:, 0:N], ps[:, 0:N], [Bps], [Bckvb])
                if DBG == 2:
                    return
                kr, Bkr = per_r.next()
                krot, Bkrot = per_r.next()
                sqr, Bsqr = b16_r.next()
                ps, Bps = pb()
                for kc in range(8):
                    mm(ps[0:96, 0:N], win_b[:, kc, 1440:1536], hb[:, kc, 0:N], kc == 0, kc == 7, [Bwin, Bhb], [Bps])
                act(sqr[0:96, 0:N], ps[0:96, 0:N], AF.Square, [Bps], [Bsqr])
                acp(kr[64:96, 0:N], ps[64:96, 0:N], [Bps], [Bkr])
                ps, Bps = pb()
                for kc in range(8):
                    mm(ps[0:96, 0:N], win_b[:, kc, 1536:1632], hb[:, kc, 0:N], kc == 0, kc == 7, [Bwin, Bhb], [Bps])
                cp("dve", krot[64:96, 0:N], ps[64:96, 0:N], [Bps], [Bkrot])
                if DBG < 3:
                    return
                Cq, BCq = per_r.next()
                ps, Bps = pb()
                for j in range(2):
                    mm(ps[0:96, 0:N], ones_b[:, 0:96], sqq[:, j, 0:N], j == 0, j == 1, [Bones, Bsqq], [Bps])
                ts("dve", Cq[:, 0:N], ps[0:96, 0:N], 96.0 * EPS / 256.0, 96.0 * EPS * EPS, ALU.mult, ALU.add,
                   [Bps], [BCq])
                S2, BS2 = per_r.next()
                skv, Bskv = per_r.next()
                MR, BMR = per_r.next()
                KRr, BKRr = per_r.next()
                tA, BtA = per_r.next()
                ps, Bps = pb()
                mm(ps[0:96, 0:N], ones_b[:, 0:96], sqkv[:, 0:N], True, True, [Bones, Bsqkv], [Bps])
                ts("dve", tA[:, 0:N], ps[0:96, 0:N], 1.0 / 128.0, EPS, ALU.mult, ALU.add, [Bps], [BtA])
                recip(S2[:, 0:N], tA[:, 0:N], [BtA], [BS2])
                act(skv[:, 0:N], S2[:, 0:N], AF.Sqrt, [BS2], [Bskv])
                ps, Bps = pb()
                mm(ps[0:96, 0:N], onesR_b[:, :], sqr[0:96, 0:N], True, True, [BonesR, Bsqr], [Bps])
                ts("dve", MR[:, 0:N], ps[0:96, 0:N], 96.0 * EPS, None, ALU.add, None, [Bps], [BMR])
                t1, Bt1 = per_r.next()
                t2, Bt2 = per_r.next()
                stt("dve", t1[64:96, 0:N], kr[64:96, 0:N], vec[64:96, 13:14], c[64:96, 0, 0:N], ALU.mult, ALU.mult,
                    [Bkr, Bvec, Bc], [Bt1])
                stt("dve", t2[64:96, 0:N], krot[64:96, 0:N], vec[64:96, 14:15], c[64:96, 1, 0:N], ALU.mult, ALU.mult,
                    [Bkrot, Bvec, Bc], [Bt2])
                tt("pool", KRr[64:96, 0:N], t1[64:96, 0:N], t2[64:96, 0:N], ALU.add, [Bt1, Bt2], [BKRr])
                if DBG < 4:
                    return
                for hh in range(8 if DBG != 5 else 0):
                    pq, Bpq = pb()
                    for j in range(2):
                        mm(pq[0:96, 0:N], wq_b[:, j, hh * 96:(hh + 1) * 96], cqb[:, j, 0:N], j == 0, j == 1,
                           [Bwq, Bcqb], [Bpq])
                    pr, Bpr = pb()
                    for j in range(2):
                        mm(pr[0:96, 0:N], wqr_b[:, j, hh * 96:(hh + 1) * 96], cqb[:, j, 0:N], j == 0, j == 1,
                           [Bwqr, Bcqb], [Bpr])
                    s2, Bs2 = b16_r.next()
                    act(s2[0:96, 0:N], pq[0:96, 0:N], AF.Square, [Bpq], [Bs2])
                    qf, Bqf = qf_r.next()
                    acp(qf[0:96, 0:N], pq[0:96, 0:N], [Bpq], [Bqf])
                    pm, Bpm = pb()
                    mm(pm[0:96, 0:N], ones_b[0:96, 0:96], s2[0:96, 0:N], True, True, [Bones, Bs2], [Bpm])
                    ta, Bta = f32_r.next()
                    tb_, Btb = f32_r.next()
                    rq, Brq = f32_r.next()
                    tt("dve", ta[0:96, 0:N], pm[0:96, 0:N], Cq[:, 0:N], ALU.add, [Bpm, BCq], [Bta])
                    act(tb_[0:96, 0:N], ta[0:96, 0:N], AF.Sqrt, [Bta], [Btb], scale=1.0 / 96.0)
                    recip(rq[0:96, 0:N], tb_[0:96, 0:N], [Btb], [Brq])
                    qo, Bqo = qo_r.next()
                    stt("dve", qo[0:64, 0:N], qf[0:64, 0:N], vec[0:64, 11:12], rq[0:64, 0:N], ALU.mult, ALU.mult,
                        [Bqf, Bvec, Brq], [Bqo])
                    u1, Bu1 = f32_r.next()
                    u2, Bu2 = f32_r.next()
                    stt("dve", u1[64:96, 0:N], qf[64:96, 0:N], vec[64:96, 11:12], c[64:96, 0, 0:N], ALU.mult, ALU.mult,
                        [Bqf, Bvec, Bc], [Bu1])
                    stt("dve", u2[64:96, 0:N], pr[64:96, 0:N], vec[64:96, 12:13], c[64:96, 1, 0:N], ALU.mult, ALU.mult,
                        [Bpr, Bvec, Bc], [Bu2])
                    tt("pool", u1[64:96, 0:N], u1[64:96, 0:N], u2[64:96, 0:N], ALU.add, [Bu2], [Bu1])
                    tt("pool", qo[64:96, 0:N], u1[64:96, 0:N], rq[64:96, 0:N], ALU.mult, [Bu1, Brq], [Bqo])
                    S.dma("sp", q_scr[hh * 96:(hh + 1) * 96, g0:g0 + N], qo[0:96, 0:N], reads=[Bqo])
                    pk, Bpk = pb()
                    mm(pk[:, 0:N], wkv_b[:, hh * 128:(hh + 1) * 128], ckvb[:, 0:N], True, True, [Bwkv, Bckvb], [Bpk])
                    sx, Bsx = sx_r.next()
                    act(sx[0:64, 0:N], pk[0:64, 0:N], AF.Square, [Bpk], [Bsx])
                    kf, Bkf = qf_r.next()
                    acp(kf[0:64, 0:N], pk[0:64, 0:N], [Bpk], [Bkf])
                    pm, Bpm = pb()
                    mm(pm[0:96, 0:N], ones_b[:, 0:96], sx[:, 0:N], True, True, [Bones, Bsx], [Bpm])
                    ta, Bta = f32_r.next()
                    tb_, Btb = f32_r.next()
                    rk, Brk = f32_r.next()
                    tt("dve", ta[0:96, 0:N], pm[0:96, 0:N], S2[:, 0:N], ALU.mult, [Bpm, BS2], [Bta])
                    tt("pool", ta[0:96, 0:N], ta[0:96, 0:N], MR[:, 0:N], ALU.add, [BMR], [Bta])
                    act(tb_[0:96, 0:N], ta[0:96, 0:N], AF.Sqrt, [Bta], [Btb], scale=1.0 / 96.0)
                    recip(rk[0:96, 0:N], tb_[0:96, 0:N], [Btb], [Brk])
                    tt("pool", tb_[0:64, 0:N], rk[0:64, 0:N], skv[0:64, 0:N], ALU.mult, [Brk, Bskv], [Btb])
                    ko, Bko = ko_r.next()
                    stt("dve", ko[0:64, 0:N], kf[0:64, 0:N], vec[0:64, 13:14], tb_[0:64, 0:N], ALU.mult, ALU.mult,
                        [Bkf, Bvec, Btb], [Bko])
                    tt("pool", ko[64:96, 0:N], KRr[64:96, 0:N], rk[64:96, 0:N], ALU.mult, [BKRr, Brk], [Bko])
                    S.dma("sp", k_gath[r * 768 + hh * 96:r * 768 + (hh + 1) * 96, t0:t0 + N], ko[0:96, 0:N], reads=[Bko])
                for j0 in range(0, N, 128):
                    nt = min(128, N - j0)
                    pv, Bpv = pb()
                    mm(pv[:, 0:512], ckvb[:, j0:j0 + 128], wv_b[:, :], True, True, [Bckvb, Bwv], [Bpv])
                    p1, Bp1 = pb()
                    mm(p1[:, 0:2], sqkv[:, j0:j0 + 128], ones_b[:, 0:2], True, True, [Bsqkv, Bones], [Bp1])
                    svt, Bsvt = sv_r.next()
                    act(svt[0:nt, 0:1], p1[0:nt, 0:1], AF.Sqrt, [Bp1], [Bsvt], bias=EPS, scale=1.0 / 128.0)
                    recip(svt[0:nt, 1:2], svt[0:nt, 0:1], [], [Bsvt])
                    vo, Bvo = vo_r.next()
                    ts("dve", vo[0:nt, :], pv[0:nt, 0:512], svt[0:nt, 1:2], None, ALU.mult, None, [Bpv, Bsvt], [Bvo])
                    S.dma("sp", v_gath[r * LT + t0 + j0:r * LT + t0 + j0 + nt, :], vo[0:nt, :], reads=[Bvo])

            cur = load(0)
            for i in range(len(tiles)):
                nxt = load(i + 1) if i + 1 < len(tiles) else None
                compute(i, *cur)
                cur = nxt
            S.flush()

    def phaseC(l):
        with ExitStack() as es:
            nchmax = (TP + 127) // 128
            KT_r = Ring(nc, es, "KT", 2, [96, TP], BF16)
            V_r = Ring(nc, es, "Vt", 2, [128, nchmax, 65], BF16)
            QT_r = Ring(nc, es, "QT", 3, [96, 512], BF16)
            PT_r = Ring(nc, es, "PT", 3, [128, 2, 512], BF16)
            rec_r = Ring(nc, es, "recC", 2, [128, 512], F32)
            for (rt, Brt) in rec_r.items:
                mset("pool", rt[:, :], 0.0, [Brt])
            bcs_r = Ring(nc, es, "bcsC", 2, [65, 512], F32)
            yo_r = Ring(nc, es, "yoC", 2, [65, 512], F32)
            for (V, BV) in V_r.items:
                mset("pool", V[:, :, 0:1], 1.0, [BV])
            stg = PSP[0:2]
            ob = PS[4:6]
            bcb = PS[6:8]
            pend = [None]

            def groups(chs):
                g = []
                i = 0
                while i < len(chs):
                    if i + 1 < len(chs) and chs[i][1] == chs[i + 1][1]:
                        g.append((i, 2))
                        i += 2
                    else:
                        g.append((i, 1))
                        i += 1
                return g
            units = [(s, hh) for s in range(2) for hh in range(8)]

            def chunks(T):
                nfull, rem = T // 128, T % 128
                if rem == 0:
                    return [(i * 128, 128) for i in range(nfull)]
                if rem >= 65 or nfull == 0:
                    return [(i * 128, 128) for i in range(nfull)] + [(nfull * 128, rem)]
                tot = 128 + rem
                a = (tot + 1) // 2
                return [(i * 128, 128) for i in range(nfull - 1)] + [((nfull - 1) * 128, a), ((nfull - 1) * 128 + a, tot - a)]

            def loadKV(s, hh):
                T, L = seq_T[s], seg_len[s]
                KT, BK = KT_r.next()
                V, BV = V_r.next()
                for kc0 in range(0, T, 2048):
                    kn = min(2048, T - kc0)
                    S.dma("sp", KT[0:96, kc0:kc0 + kn],
                          k_gath[hh * 96:(hh + 1) * 96, seg_off[s] + kc0:seg_off[s] + kc0 + kn],
                          reads=[Bkg], writes=[BK])
                chs = chunks(T)
                for r in range(VR):
                    n0, n1 = r * L, (r + 1) * L
                    base = r * LT + seg_off[s]
                    ci = 0
                    while ci < len(chs):
                        cs0, csz = chs[ci]
                        lo, hi = max(cs0, n0), min(cs0 + csz, n1)
                        if lo >= hi:
                            ci += 1
                            continue
                        if lo == cs0 and hi == cs0 + csz and csz == 128:
                            cj = ci
                            while cj < len(chs) and chs[cj][1] == 128 and chs[cj][0] + 128 <= n1:
                                cj += 1
                            nf = min(cj - ci, 8)
                            cj = ci + nf
                            S.dma("sp", V[:, ci:ci + nf, 1:65],
                                  v_gath[base + lo - n0:base + lo - n0 + nf * 128, hh * 64:(hh + 1) * 64]
                                  .rearrange("(c p) d -> p c d", p=128), reads=[Bvg], writes=[BV])
                            ci = cj
                        else:
                            S.dma("sp", V[lo - cs0:hi - cs0, ci, 1:65],
                                  v_gath[base + lo - n0:base + hi - n0, hh * 64:(hh + 1) * 64],
                                  reads=[Bvg], writes=[BV])
                            if hi == cs0 + csz:
                                ci += 1
                            else:
                                break
                return KT, BK, V, BV

            ui = [0]

            def unit(s, hh, KT, BK, V, BV):
                T, L = seq_T[s], seg_len[s]
                chs = chunks(T)
                grp = groups(chs)
                ng = len(grp)
                qts = [(rr_ * LT + seg_off[s] + qq_, min(512, L - qq_)) for rr_ in range(VR) for qq_ in range(0, L, 512)]

                def loadQ(j):
                    qc, nq = qts[j]
                    QT, BQ = QT_r.next()
                    S.dma("sp", QT[0:96, 0:nq], q_scr[hh * 96:(hh + 1) * 96, qc:qc + nq], writes=[BQ])
                    return QT, BQ

                curq = loadQ(0)
                for j, (qc, nq) in enumerate(qts):
                    nxtq = loadQ(j + 1) if j + 1 < len(qts) else None
                    QT, BQ = curq
                    curq = nxtq
                    o, Bo = ob[ui[0] % 2]
                    bc, Bbc = bcb[ui[0] % 2]
                    ui[0] += 1
                    pts = {}
                    for g in range(ng + 2):
                        if g == 2 and pend[0] is not None:
                            pend[0]()
                            pend[0] = None
                        if g < ng:
                            k0, cnt = grp[g]
                            KS = chs[k0][1]
                            st, Bst = stg[g % 2]
                            for c in range(cnt):
                                K0 = chs[k0 + c][0]
                                mm(st[0:KS, c * 512:c * 512 + nq], KT[0:96, K0:K0 + KS], QT[0:96, 0:nq], True, True,
                                   [BK, BQ], [Bst])
                            pt, Bpt = PT_r.next()
                            if cnt == 2:
                                act(pt[0:KS, :, 0:nq], st[:, :].rearrange("p (c n) -> p c n", c=2)[0:KS, :, 0:nq],
                                    AF.Exp, [Bst], [Bpt], scale=SCALE)
                            else:
                                act(pt[0:KS, 0, 0:nq], st[0:KS, 0:nq], AF.Exp, [Bst], [Bpt], scale=SCALE)
                            pts[g] = (pt, Bpt, KS, k0, cnt)
                        if g >= 2:
                            pt, Bpt, KS, k0, cnt = pts.pop(g - 2)
                            for c in range(cnt):
                                kk = k0 + c
                                mm(o[0:65, 0:nq], V[0:KS, kk, 0:65], pt[0:KS, c, 0:nq], kk == 0, kk == len(chs) - 1,
                                   [BV, Bpt], [Bo])

                    def fin(o=o, Bo=Bo, bc=bc, Bbc=Bbc, nq=nq, qc=qc):
                        rec, Brec = rec_r.next()
                        recip(rec[0:1, 0:nq], o[0:1, 0:nq], [Bo], [Brec])
                        mm(bc[0:65, 0:nq], ones_f[:, 0:65], rec[:, 0:nq], True, True, [Bonesf, Brec], [Bbc])
                        bcs, Bbcs = bcs_r.next()
                        cp("dve", bcs[0:65, 0:nq], bc[0:65, 0:nq], [Bbc], [Bbcs])
                        yo, Byo = yo_r.next()
                        tt("dve", yo[0:65, 0:nq], o[0:65, 0:nq], bcs[0:65, 0:nq], ALU.mult, [Bo, Bbcs], [Byo])
                        S.dma("sp", ya_scr[hh * 64:(hh + 1) * 64, qc:qc + nq], yo[1:65, 0:nq], reads=[Byo])
                    if pend[0] is not None:
                        pend[0]()
                    pend[0] = fin

            cur = loadKV(*units[0])
            for i, (s, hh) in enumerate(units):
                nxt = loadKV(*units[i + 1]) if i + 1 < len(units) else None
                unit(s, hh, *cur)
                cur = nxt
            if pend[0] is not None:
                pend[0]()
                pend[0] = None
            S.flush()

    def phaseB(l):
        with ExitStack() as es:
            LM = TP // BR
            vec, Bvec = tile(es, "vecB", [128, NV], F32)
            lw, Blw = tile(es, "lwB", [128, 16, 128], F32)
            bd_b, Bbd = tile(es, "bdB", [128, 16, 128], BF16)
            coef, Bcoef = tile(es, "coefB", [128, 32], F32)
            carry, Bcarry = tile(es, "carryB", [128, 1], F32)
            S.dma("sp", vec[:, :], vecs[l, :, :], writes=[Bvec])
            S.dma("sp", lw[:, :, :], lruw[l].rearrange("f m p q -> p (f m) q"), writes=[Blw])
            cp("dve", bd_b[:, :, :], lw[:, :, :], [Blw], [Bbd])
            act(coef[:, 0:8], vec[:, 67:75], AF.Exp, [Bvec], [Bcoef], scale=-1.0)
            act(coef[:, 8:16], coef[:, 0:8], AF.Ln, [], [Bcoef], bias=1.0)
            ts("dve", coef[:, 16:24], coef[:, 8:16], -8.0, None, ALU.mult, None, [], [Bcoef])
            ts("dve", coef[:, 24:32], coef[:, 8:16], -16.0, None, ALU.mult, None, [], [Bcoef])
            U_r = Ring(nc, es, "UB", 2, [128, LM + 4], F32)
            G_r = Ring(nc, es, "GB", 2, [128, LM], F32)
            HF_r = Ring(nc, es, "HFB", 2, [128, LM], F32)
            H_r = Ring(nc, es, "HB", 2, [128, LM], F32)
            Y_r = Ring(nc, es, "YB", 2, [128, LM], F32)
            xc_r = Ring(nc, es, "xcB", 2, [128, LM], F32)
            xcb_r = Ring(nc, es, "xcbB", 2, [128, LM], BF16)
            rr_r = Ring(nc, es, "rrB", 2, [128, LM], F32)
            ig_r = Ring(nc, es, "igB", 2, [128, LM], F32)
            aa_r = Ring(nc, es, "aaB", 2, [128, LM], F32)
            a2_r = Ring(nc, es, "a2B", 2, [128, LM], F32)
            bb_r = Ring(nc, es, "bbB", 2, [128, LM], F32)
            BHF = {}

            def loads(s_, m, L, r, direction):
                o0 = seg_off[s_]
                U, BU = U_r.next()
                if r == 0:
                    mset("pool", U[:, 0:2], 0.0, [BU])
                if r == BR - 1:
                    mset("pool", U[:, 2 + L:4 + L], 0.0, [BU])
                c0 = o0 + r * L
                S.dma("sp", U[:, 2:2 + L], ug_gath[m * 128:(m + 1) * 128, c0:c0 + L], reads=[Bug], writes=[BU])
                if r > 0:
                    S.dma("sp", U[:, 0:2], ug_gath[m * 128:(m + 1) * 128, c0 - 2:c0], reads=[Bug], writes=[BU])
                if r < BR - 1:
                    S.dma("sp", U[:, 2 + L:4 + L], ug_gath[m * 128:(m + 1) * 128, c0 + L:c0 + L + 2],
                          reads=[Bug], writes=[BU])
                Gt = BG = HF = BHFt = None
                if direction == 1:
                    Gt, BG = G_r.next()
                    S.dma("sp", Gt[:, 0:L], ug_gath[512 + m * 128:512 + (m + 1) * 128, c0:c0 + L],
                          reads=[Bug], writes=[BG])
                    HF, BHFt = HF_r.next()
                    S.dma("sp", HF[:, 0:L], hf_scr[:, r * L:(r + 1) * L], reads=[BHF[(s_, m, r)]], writes=[BHFt])
                return U, BU, Gt, BG, HF, BHFt

            def compute(s_, m, L, r, direction, U, BU, Gt, BG, HF, BHFt):
                o0 = seg_off[s_]
                zi = direction * 4 + m
                xc, Bxc = xc_r.next()
                xcb, Bxcb = xcb_r.next()
                rr, Brr = rr_r.next()
                ig, Big = ig_r.next()
                aa, Baa = aa_r.next()
                a2, Ba2 = a2_r.next()
                bb, Bbb = bb_r.next()
                ts("dve", xc[:, 0:L], U[:, 0:L], vec[:, 31 + m * 4:32 + m * 4], vec[:, 47 + m:48 + m], ALU.mult, ALU.add,
                   [BU, Bvec], [Bxc])
                for tap in range(1, 4):
                    stt("dve", xc[:, 0:L], U[:, tap:tap + L], vec[:, 31 + m * 4 + tap:32 + m * 4 + tap], xc[:, 0:L],
                        ALU.mult, ALU.add, [BU, Bvec], [Bxc])
                cp("pool", xcb[:, 0:L], xc[:, 0:L], [Bxc], [Bxcb])
                for c0 in range(0, L, 512):
                    n = min(512, L - c0)
                    ps, Bps = pb()
                    mm(ps[:, 0:n], bd_b[:, zi, :], xcb[:, c0:c0 + n], True, True, [Bbd, Bxcb], [Bps])
                    act(rr[:, c0:c0 + n], ps[:, 0:n], AF.Sigmoid, [Bps, Bvec], [Brr], bias=vec[:, 51 + zi:52 + zi])
                    ps, Bps = pb()
                    mm(ps[:, 0:n], bd_b[:, 8 + zi, :], xcb[:, c0:c0 + n], True, True, [Bbd, Bxcb], [Bps])
                    act(ig[:, c0:c0 + n], ps[:, 0:n], AF.Sigmoid, [Bps, Bvec], [Big], bias=vec[:, 59 + zi:60 + zi])
                act(aa[:, 0:L], rr[:, 0:L], AF.Exp, [Brr, Bcoef], [Baa], scale=coef[:, 16 + zi:17 + zi])
                act(a2[:, 0:L], rr[:, 0:L], AF.Exp, [Brr, Bcoef], [Ba2], scale=coef[:, 24 + zi:25 + zi])
                ts("dve", a2[:, 0:L], a2[:, 0:L], -1.0, 1.0, ALU.mult, ALU.add, [], [Ba2])
                act(a2[:, 0:L], a2[:, 0:L], AF.Sqrt, [], [Ba2])
                tt("pool", bb[:, 0:L], a2[:, 0:L], ig[:, 0:L], ALU.mult, [Ba2, Big], [Bbb])
                tt("dve", bb[:, 0:L], bb[:, 0:L], xc[:, 0:L], ALU.mult, [Bxc], [Bbb])
                H, BH = H_r.next()
                if direction == 0:
                    S.op("dve", lambda e, o=H[:, 0:L], d0=aa[:, 0:L], d1=bb[:, 0:L]: e.tensor_tensor_scan(
                        out=o, data0=d0, data1=d1, initial=carry[:, 0:1], op0=ALU.mult, op1=ALU.add),
                        [Baa, Bbb, Bcarry], [BH])
                    cp("dve", carry[:, 0:1], H[:, L - 1:L], [BH], [Bcarry])
                    BHF[(s_, m, r)] = Buf(f"hf{s_}_{m}_{r}_{l}")
                    S.dma("sp", hf_scr[:, r * L:(r + 1) * L], H[:, 0:L], reads=[BH], writes=[BHF[(s_, m, r)]], owner=BH)
                else:
                    S.op("dve", lambda e, o=H[:, 0:L][:, ::-1], d0=aa[:, 0:L][:, ::-1], d1=bb[:, 0:L][:, ::-1]: e.tensor_tensor_scan(
                        out=o, data0=d0, data1=d1, initial=carry[:, 0:1], op0=ALU.mult, op1=ALU.add),
                        [Baa, Bbb, Bcarry], [BH])
                    cp("dve", carry[:, 0:1], H[:, 0:1], [BH], [Bcarry])
                    tt("pool", H[:, 0:L], H[:, 0:L], HF[:, 0:L], ALU.add, [BHFt], [BH])
                    act(Gt[:, 0:L], Gt[:, 0:L], AF.Gelu_apprx_tanh, [], [BG])
                    Y, BY = Y_r.next()
                    tt("dve", Y[:, 0:L], H[:, 0:L], Gt[:, 0:L], ALU.mult, [BH, BG], [BY])
                    S.dma("sp", y_gath[m * 128:(m + 1) * 128, o0 + r * L:o0 + (r + 1) * L], Y[:, 0:L], reads=[BY])

            work = []
            for s_ in range(2):
                L = seq_T[s_] // BR
                for m in range(4):
                    for direction in (0, 1):
                        order = list(range(BR)) if direction == 0 else list(range(BR - 1, -1, -1))
                        for idx, r in enumerate(order):
                            work.append((s_, m, L, r, direction, idx == 0))
            ld = loads(*work[0][:5])
            for i, w in enumerate(work):
                nld = None
                pre = i + 1 < len(work) and not (work[i + 1][4] == 1 and w[4] == 0)
                if pre:
                    nld = loads(*work[i + 1][:5])
                if w[5]:
                    mset("dve", carry[:, 0:1], 0.0, [Bcarry])
                compute(*w[:5], *ld)
                if i + 1 < len(work) and not pre:
                    nld = loads(*work[i + 1][:5])
                ld = nld
            S.flush()

    def phaseD1(l, src):
        with ExitStack() as es:
            vec, Bvec = tile(es, "vecD", [128, NV], F32)
            wo_b, Bwo = tile(es, "wo_b", [128, 8, D], BF16)
            stg = Ring(nc, es, "wstgD", 2, [128, D], F32)
            S.dma("sp", vec[:, :], vecs[l, :, :], writes=[Bvec])
            for kc in range(8):
                st, Bst = stg.next()
                S.dma("sp", st[:, :], w_out[l, kc * 128:(kc + 1) * 128, :], writes=[Bst])
                ts("dve", wo_b[:, kc, :], st[:, :], vec[:, 15 + kc:16 + kc], None, ALU.mult, None, [Bst, Bvec], [Bwo])
            tiles = [(t0, min(512, NTOK - t0)) for t0 in range(0, NTOK, 512)]
            hT_r = Ring(nc, es, "hTD", 2, [128, 8, 512], F32)
            Y_r = Ring(nc, es, "YD", 2, [128, 8, 512], F32)
            sq_r = Ring(nc, es, "sqD", 2, [128, 8, 512], BF16)
            yb_r = Ring(nc, es, "ybD", 2, [128, 8, 512], BF16)
            f32_r = Ring(nc, es, "f32D", 4, [128, 512], F32)
            hsrc = src.rearrange("(kc p) t -> p kc t", p=128)
            hdst = hT_scr.rearrange("(kc p) t -> p kc t", p=128)
            ygv = y_gath.rearrange("(kc p) t -> p kc t", p=128)
            yav = ya_scr.rearrange("(kc p) t -> p kc t", p=128)
            def load(i):
                t0, N = tiles[i]
                h, Bh = hT_r.next()
                S.dma("sp", h[:, :, 0:N], hsrc[:, :, t0:t0 + N], writes=[Bh])
                Y, BY = Y_r.next()
                S.dma("sp", Y[:, 0:4, 0:N], ygv[:, :, t0:t0 + N], reads=[Byg], writes=[BY])
                S.dma("sp", Y[:, 4:8, 0:N], yav[:, :, t0:t0 + N], writes=[BY])
                return h, Bh, Y, BY

            def compute(i, h, Bh, Y, BY):
                t0, N = tiles[i]
                sq, Bsq = sq_r.next()
                yb, Byb = yb_r.next()
                for kc in range(8):
                    act(sq[:, kc, 0:N], Y[:, kc, 0:N], AF.Square, [BY], [Bsq])
                Rs = []
                for grp in range(2):
                    ps, Bps = pb()
                    for kc in range(4):
                        mm(ps[:, 0:N], ones_b[:, :], sq[:, grp * 4 + kc, 0:N], kc == 0, kc == 3, [Bones, Bsq], [Bps])
                    tmp, Btmp = f32_r.next()
                    R, BR = f32_r.next()
                    rstd_from(ps[:, 0:N], 1.0 / 512.0, EPS, tmp[:, 0:N], R[:, 0:N], Bps, Btmp, BR)
                    Rs.append((R, BR))
                for kc in range(8):
                    R, BR = Rs[kc // 4]
                    tt("dve" if kc % 2 == 0 else "pool", yb[:, kc, 0:N], Y[:, kc, 0:N], R[:, 0:N], ALU.mult,
                       [BY, BR], [Byb])
                for oc in range(8):
                    ps, Bps = pb()
                    for kc in range(8):
                        mm(ps[:, 0:N], wo_b[:, kc, oc * 128:(oc + 1) * 128], yb[:, kc, 0:N], kc == 0, kc == 7,
                           [Bwo, Byb], [Bps])
                    tt("dve", h[:, oc, 0:N], h[:, oc, 0:N], ps[:, 0:N], ALU.add, [Bps], [Bh])
                S.dma("sp", hdst[:, :, t0:t0 + N], h[:, :, 0:N], reads=[Bh])

            cur = load(0)
            for i in range(len(tiles)):
                nxt = load(i + 1) if i + 1 < len(tiles) else None
                compute(i, *cur)
                cur = nxt
            S.flush()

    def phaseD2(l, dst):
        with ExitStack() as es:
            NT = 256
            vec, Bvec = tile(es, "vecE", [128, NV], F32)
            wu_b, Bwu = tile(es, "wu_b", [128, 8, 4 * D], BF16)
            wd_b, Bwd = tile(es, "wd_b", [128, 32, D], BF16)
            S.dma("sp", vec[:, :], vecs[l, :, :], writes=[Bvec])
            with ExitStack() as es2:
                stg = Ring(nc, es2, "wstgE", 2, [128, 4 * D], F32)
                for kc in range(8):
                    st, Bst = stg.next()
                    S.dma("sp", st[:, :], w_up[l, kc * 128:(kc + 1) * 128, :], writes=[Bst])
                    ts("dve" if kc % 2 == 0 else "pool", wu_b[:, kc, :], st[:, :], vec[:, 23 + kc:24 + kc], None,
                       ALU.mult, None, [Bst, Bvec], [Bwu])
                for f4 in range(8):
                    st, Bst = stg.next()
                    S.dma("sp", st[:, :].rearrange("p (f n) -> p f n", n=D),
                          w_down[l, f4 * 512:(f4 + 1) * 512, :].rearrange("(f p) n -> p f n", p=128), writes=[Bst])
                    cp("dve" if f4 % 2 == 0 else "pool", wd_b[:, f4 * 4:(f4 + 1) * 4, :],
                       st[:, :].rearrange("p (f n) -> p f n", n=D), [Bst], [Bwd])
                S.flush()
            tiles = [(t0, min(NT, NTOK - t0)) for t0 in range(0, NTOK, NT)]
            hT_r = Ring(nc, es, "hTE", 2, [128, 8, NT], F32)
            sq_r = Ring(nc, es, "sqE", 2, [128, 8, NT], BF16)
            hb_r = Ring(nc, es, "hbE", 2, [128, 8, NT], BF16)
            ac_r = Ring(nc, es, "acE", 2, [128, 32, NT], BF16)
            f32_r = Ring(nc, es, "f32E", 6, [128, NT], F32)
            hsrc = hT_scr.rearrange("(kc p) t -> p kc t", p=128)
            hdst = dst.rearrange("(kc p) t -> p kc t", p=128)

            def load(i):
                t0, N = tiles[i]
                h, Bh = hT_r.next()
                S.dma("sp", h[:, :, 0:N], hsrc[:, :, t0:t0 + N], writes=[Bh])
                return h, Bh

            def compute(i, h, Bh):
                t0, N = tiles[i]
                sq, Bsq = sq_r.next()
                hb, Bhb = hb_r.next()
                for kc in range(8):
                    act(sq[:, kc, 0:N], h[:, kc, 0:N], AF.Square, [Bh], [Bsq])
                ps, Bps = pb()
                for kc in range(8):
                    mm(ps[:, 0:N], ones_b[:, :], sq[:, kc, 0:N], kc == 0, kc == 7, [Bones, Bsq], [Bps])
                tmp, Btmp = f32_r.next()
                R, BR = f32_r.next()
                rstd_from(ps[:, 0:N], 1.0 / D, EPS, tmp[:, 0:N], R[:, 0:N], Bps, Btmp, BR)
                for kc in range(8):
                    tt("dve" if kc % 2 == 0 else "pool", hb[:, kc, 0:N], h[:, kc, 0:N], R[:, 0:N], ALU.mult,
                       [Bh, BR], [Bhb])
                ac, Bac = ac_r.next()
                for fc in range(32):
                    ps, Bps = pb()
                    for kc in range(8):
                        mm(ps[:, 0:N], wu_b[:, kc, fc * 128:(fc + 1) * 128], hb[:, kc, 0:N], kc == 0, kc == 7,
                           [Bwu, Bhb], [Bps])
                    rl, Brl = f32_r.next()
                    act(rl[:, 0:N], ps[:, 0:N], AF.Relu, [Bps], [Brl])
                    tt("dve" if fc % 2 == 0 else "pool", ac[:, fc, 0:N], rl[:, 0:N], rl[:, 0:N], ALU.mult, [Brl], [Bac])
                for oc in range(8):
                    ps, Bps = pb()
                    for fc in range(32):
                        mm(ps[:, 0:N], wd_b[:, fc, oc * 128:(oc + 1) * 128], ac[:, fc, 0:N], fc == 0, fc == 31,
                           [Bwd, Bac], [Bps])
                    tt("dve", h[:, oc, 0:N], h[:, oc, 0:N], ps[:, 0:N], ALU.add, [Bps], [Bh])
                S.dma("sp", hdst[:, :, t0:t0 + N], h[:, :, 0:N], reads=[Bh])

            cur = load(0)
            for i in range(len(tiles)):
                nxt = load(i + 1) if i + 1 < len(tiles) else None
                compute(i, *cur)
                cur = nxt
            S.flush()

    for l in range(NLAYER):
        src = xT if l == 0 else hT_scr
        if "A" in PHASES:
            phaseA(l, src)
        if "C" in PHASES:
            phaseC(l)
        if "B" in PHASES:
            phaseB(l)
        if "D" in PHASES:
            phaseD1(l, src)
        if "E" in PHASES:
            phaseD2(l, yT if l == NLAYER - 1 else hT_scr)
    S.flush()
    G.close()
    return nc, dict(LT=LT, NTOK=NTOK, seg_len=seg_len, seg_off=seg_off, seq_T=seq_T, LP=LP, LS=LS, TP=TP, TS=TS)


def prep(inp, SEQ, DSEQ, meta):
    LT, NTOK, seg_len, seg_off = meta["LT"], meta["NTOK"], meta["seg_len"], meta["seg_off"]
    f = lambda k: np.asarray(inp[k], dtype=np.float32)
    xp, xs, mt = f("x_prompt"), f("x_sample"), f("meta_tokens")
    nP, nS = xp.shape[0], xs.shape[0]
    inv_freq = (1.0 / (10000.0 ** (np.arange(0, 32, 2, dtype=np.float32) / np.float32(32)))).astype(np.float32)
    qg, kg = f("qk_q_g"), f("qk_k_g")

    def swp(g):
        o = np.zeros(96, np.float32)
        o[64:80] = g[80:96]
        o[80:96] = g[64:80]
        return o

    vecs = np.zeros((2, 128, NV), np.float32)
    lruw = np.zeros((2, 4, 4, 128, 128), np.float32)
    for l in range(2):
        vecs[l, :, 0:8] = f("norm_mix_g")[l].reshape(8, 128).T
        vecs[l, :, 8:10] = f("q_norm_g")[l].reshape(2, 128).T
        vecs[l, :, 10] = f("kv_norm_g")[l]
        vecs[l, 0:96, 11] = qg[l]
        vecs[l, 0:96, 12] = swp(qg[l])
        vecs[l, 0:96, 13] = kg[l]
        vecs[l, 0:96, 14] = swp(kg[l])
        vecs[l, :, 15:19] = f("out_norm_lru_g")[l].reshape(4, 128).T
        vecs[l, :, 19:23] = f("out_norm_attn_g")[l].reshape(4, 128).T
        vecs[l, :, 23:31] = f("norm_ff_g")[l].reshape(8, 128).T
        for m in range(4):
            ch = slice(m * 128, (m + 1) * 128)
            vecs[l, :, 31 + m * 4:35 + m * 4] = f("conv_w")[l][:, ch].T
            vecs[l, :, 47 + m] = f("conv_b")[l][ch]
            for z in range(2):
                vecs[l, :, 51 + z * 4 + m] = f("lru_ba")[l, z][ch]
                vecs[l, :, 59 + z * 4 + m] = f("lru_bx")[l, z][ch]
                vecs[l, :, 67 + z * 4 + m] = f("lru_lambda")[l, z][ch]
                for half in range(2):
                    hs = slice(half * 64, (half + 1) * 64)
                    lruw[l, z, m, hs, hs] = f("lru_wa")[l, z, 2 * m + half]
                    lruw[l, 2 + z, m, hs, hs] = f("lru_wx")[l, z, 2 * m + half]
    shared = {"vecs": vecs, "lruw": lruw, "w_in": f("w_in"), "w_uq": f("w_uq"), "w_ukv": f("w_ukv"),
              "w_out": f("w_out"), "w_up": f("w_up"), "w_down": f("w_down")}
    in_maps = []
    cache = {}
    for c in range(NCORE):
        key = (c % nP, c % nS)
        if key not in cache:
            seqs = [np.concatenate([mt, xp[key[0]]], 0), np.concatenate([mt, xs[key[1]]], 0)]
            xT = np.empty((D, NTOK), np.float32)
            pos = np.empty(NTOK, np.float32)
            for r in range(VR):
                for s in range(2):
                    L = seg_len[s]
                    c0 = r * LT + seg_off[s]
                    xT[:, c0:c0 + L] = seqs[s][r * L:(r + 1) * L].T
                    pos[c0:c0 + L] = np.arange(r * L, (r + 1) * L, dtype=np.float32)
            ang = pos[None, :] * inv_freq[:, None]
            cs = np.empty((2, 32, NTOK), np.float32)
            cs[0, 0:16] = np.cos(ang)
            cs[0, 16:32] = np.cos(ang)
            cs[1, 0:16] = np.sin(ang)
            cs[1, 16:32] = np.sin(ang)
            cache[key] = (xT, cs)
        xT, cs = cache[key]
        m = {"xT": xT, "cs": cs}
        m.update(shared)
        in_maps.append(m)
    return in_maps


_CACHE = {}


def run(inp, SEQ, DSEQ, trace=False):
    key = (SEQ, DSEQ)
    if key not in _CACHE:
        _CACHE[key] = build(SEQ, DSEQ)
    nc, meta = _CACHE[key]
    in_maps = prep(inp, SEQ, DSEQ, meta)
    res = run_bass_kernel_spmd(nc, in_maps, core_ids=list(range(NCORE)), **({"trace": True} if trace else {}))
    B, DB = inp["x_prompt"].shape[0], inp["x_sample"].shape[0]
    TP, TS, LT, LP, LS = meta["TP"], meta["TS"], meta["LT"], meta["LP"], meta["LS"]
    yp = np.empty((B, TP, D), np.float32)
    ys = np.empty((DB, TS, D), np.float32)
    for b in range(B):
        yT = np.asarray(res.results[b]["yT"])
        for r in range(VR):
            yp[b, r * LP:(r + 1) * LP] = yT[:, r * LT:r * LT + LP].T
    for b in range(DB):
        yT = np.asarray(res.results[b]["yT"])
        for r in range(VR):
            ys[b, r * LS:(r + 1) * LS] = yT[:, r * LT + LP:r * LT + LP + LS].T
    return (np.ascontiguousarray(yp[:, 16:]), np.ascontiguousarray(ys[:, 16:])), res


def kernel(**inputs):
    SEQ = inputs["x_prompt"].shape[1]
    DSEQ = inputs["x_sample"].shape[1]
    out, _ = run(inputs, SEQ, DSEQ)
    return out
```
